# Optimizing a Trainium2 kernel written in Bass

```python
import jax, jax.numpy as jnp
from jax import lax
import numpy as np

D_MODEL = 1024
BATCH = 8
SEQ = 2048
DEPTH = 4

ATTN_WIDTH = D_MODEL // 2
RET_WIDTH = D_MODEL - ATTN_WIDTH
HEAD_DIM = 64
N_ATTN_HEADS = ATTN_WIDTH // HEAD_DIM
N_KV_HEADS = 2
GQA_GROUP = N_ATTN_HEADS // N_KV_HEADS
KV_WIDTH = N_KV_HEADS * HEAD_DIM
WINDOW = 128
ATTN_BLOCK = 128
N_RET_HEADS = 4
RET_HEAD_DIM = RET_WIDTH // N_RET_HEADS
RET_CHUNK = 128
ROPE_BASE = 10000.0
D_FF = 2816
NORM_EPS = 1e-6
GN_EPS = 1e-5
NEG_INF = -1e30
IN_WIDTHS = (ATTN_WIDTH, KV_WIDTH, KV_WIDTH, RET_WIDTH, RET_WIDTH, RET_WIDTH, RET_WIDTH)
IN_SPLITS = tuple(int(v) for v in np.cumsum(IN_WIDTHS)[:-1])
D_IN = int(sum(IN_WIDTHS))

kernel_name = "hymba_style_swa_sink_retention_macaron"


def rms_norm(x, w):
    xf = x.astype(jnp.float32)
    y = xf * lax.rsqrt(jnp.mean(xf * xf, axis=-1, keepdims=True) + NORM_EPS)
    return (y * w.astype(jnp.float32)).astype(x.dtype)


def swiglu(h, w_gate, w_up, w_down):
    return (jax.nn.silu(h @ w_gate) * (h @ w_up)) @ w_down


def sliding_window_sink_attention(q, k, v, sinks):
    B, S, _ = q.shape
    nb = S // ATTN_BLOCK
    q = q.reshape(B, nb, ATTN_BLOCK, N_KV_HEADS, GQA_GROUP, HEAD_DIM)
    k = k.reshape(B, nb, ATTN_BLOCK, N_KV_HEADS, HEAD_DIM)
    v = v.reshape(B, nb, ATTN_BLOCK, N_KV_HEADS, HEAD_DIM)
    pad = ((0, 0), (1, 0), (0, 0), (0, 0), (0, 0))
    kk = jnp.concatenate([jnp.pad(k[:, :-1], pad), k], axis=2)
    vv = jnp.concatenate([jnp.pad(v[:, :-1], pad), v], axis=2)
    s = jnp.einsum('bnqhgd,bnkhd->bnhgqk', q, kk).astype(jnp.float32) * (HEAD_DIM ** -0.5)
    blk = jnp.arange(nb)[:, None, None]
    qi = jnp.arange(ATTN_BLOCK)[None, :, None]
    kj = jnp.arange(2 * ATTN_BLOCK)[None, None, :]
    diff = ATTN_BLOCK + qi - kj
    kpos = (blk - 1) * ATTN_BLOCK + kj
    mask = (diff >= 0) & (diff < WINDOW) & (kpos >= 0)
    s = jnp.where(mask[None, :, None, None], s, NEG_INF)
    sink = sinks.astype(jnp.float32).reshape(1, 1, N_KV_HEADS, GQA_GROUP, 1, 1)
    m = jnp.maximum(jnp.max(s, axis=-1, keepdims=True), sink)
    p = jnp.exp(s - m)
    denom = jnp.sum(p, axis=-1, keepdims=True) + jnp.exp(sink - m)
    probs = (p / denom).astype(v.dtype)
    out = jnp.einsum('bnhgqk,bnkhd->bnqhgd', probs, vv)
    return out.reshape(B, S, ATTN_WIDTH)


def rotary(x, cos, sin):
    half = x.shape[-1] // 2
    x1, x2 = x[..., :half], x[..., half:]
    c = cos[None, :, None, :].astype(x.dtype)
    s = sin[None, :, None, :].astype(x.dtype)
    return jnp.concatenate([x1 * c - x2 * s, x1 * s + x2 * c], axis=-1)


def multiscale_retention(q, k, v, g, gn_w):
    B, S, _ = q.shape
    H, D, C = N_RET_HEADS, RET_HEAD_DIM, RET_CHUNK
    nc = S // C
    pos = jnp.arange(S, dtype=jnp.float32)
    inv_freq = ROPE_BASE ** (-jnp.arange(0, D, 2, dtype=jnp.float32) / D)
    ang = pos[:, None] * inv_freq[None, :]
    cos, sin = jnp.cos(ang), jnp.sin(ang)
    q = rotary(q.reshape(B, S, H, D), cos, sin)
    k = rotary(k.reshape(B, S, H, D), cos, sin) * (D ** -0.5)
    v = v.reshape(B, S, H, D)
    q = q.reshape(B, nc, C, H, D)
    k = k.reshape(B, nc, C, H, D)
    v = v.reshape(B, nc, C, H, D)
    log_gamma = jnp.log(1.0 - 2.0 ** (-5.0 - jnp.arange(H, dtype=jnp.float32)))
    idx = jnp.arange(C, dtype=jnp.float32)
    dif = idx[:, None] - idx[None, :]
    dmat = jnp.where(dif[None] >= 0, jnp.exp(jnp.maximum(dif, 0.0)[None] * log_gamma[:, None, None]), 0.0)
    zeta = jnp.exp((C - 1.0 - idx)[None, :] * log_gamma[:, None])
    xi = jnp.exp((idx + 1.0)[None, :] * log_gamma[:, None])
    chunk_decay = jnp.exp(C * log_gamma).astype(q.dtype)
    scores = jnp.einsum('bnihd,bnjhd->bnhij', q, k) * dmat.astype(q.dtype)[None, None]
    y_intra = jnp.einsum('bnhij,bnjhv->bnihv', scores, v)
    kv = jnp.einsum('bnjhd,bnjhv,hj->bnhdv', k, v, zeta.astype(q.dtype))

    def step(state, kv_n):
        return state * chunk_decay[None, :, None, None] + kv_n, state

    init = jnp.zeros((B, H, D, D), dtype=kv.dtype)
    _, prev = lax.scan(step, init, jnp.moveaxis(kv, 1, 0))
    prev = jnp.moveaxis(prev, 0, 1)
    y_cross = jnp.einsum('bnihd,bnhdv->bnihv', q, prev) * xi.T.astype(q.dtype)[None, None, :, :, None]
    y = (y_intra + y_cross).reshape(B, S, H, D)
    yf = y.astype(jnp.float32)
    mu = jnp.mean(yf, axis=-1, keepdims=True)
    var = jnp.mean(jnp.square(yf - mu), axis=-1, keepdims=True)
    yn = ((yf - mu) * lax.rsqrt(var + GN_EPS)).reshape(B, S, RET_WIDTH)
    yn = (yn * gn_w.astype(jnp.float32)).astype(q.dtype)
    return jax.nn.silu(g) * yn


def setup_inputs(seed: int = 0) -> dict:
    key = jax.random.key(seed)
    ks = jax.random.split(key, 16)

    def w(k, shape, fan_in):
        return jax.random.normal(k, shape, jnp.float32) * fan_in ** -0.5

    def gain(k, shape):
        return 1.0 + 0.02 * jax.random.normal(k, shape, jnp.float32)

    return {
        "x": jax.random.normal(ks[0], (BATCH, SEQ, D_MODEL), jnp.float32),
        "ffn1_norm": gain(ks[1], (DEPTH, D_MODEL)),
        "ffn1_w_gate": w(ks[2], (DEPTH, D_MODEL, D_FF), D_MODEL),
        "ffn1_w_up": w(ks[3], (DEPTH, D_MODEL, D_FF), D_MODEL),
        "ffn1_w_down": w(ks[4], (DEPTH, D_FF, D_MODEL), D_FF),
        "mix_norm": gain(ks[5], (DEPTH, D_MODEL)),
        "w_in": w(ks[6], (DEPTH, D_MODEL, D_IN), D_MODEL),
        "attn_sinks": 0.5 * jax.random.normal(ks[7], (DEPTH, N_ATTN_HEADS), jnp.float32),
        "ret_gn_w": gain(ks[8], (DEPTH, RET_WIDTH)),
        "w_out": w(ks[9], (DEPTH, D_MODEL, D_MODEL), D_MODEL),
        "ffn2_norm": gain(ks[10], (DEPTH, D_MODEL)),
        "ffn2_w_gate": w(ks[11], (DEPTH, D_MODEL, D_FF), D_MODEL),
        "ffn2_w_up": w(ks[12], (DEPTH, D_MODEL, D_FF), D_MODEL),
        "ffn2_w_down": w(ks[13], (DEPTH, D_FF, D_MODEL), D_FF),
        "final_norm": gain(ks[14], (D_MODEL,)),
    }


def reference(x, ffn1_norm, ffn1_w_gate, ffn1_w_up, ffn1_w_down, mix_norm, w_in,
              attn_sinks, ret_gn_w, w_out, ffn2_norm, ffn2_w_gate, ffn2_w_up,
              ffn2_w_down, final_norm):
    h = x
    for l in range(DEPTH):
        h = h + 0.5 * swiglu(rms_norm(h, ffn1_norm[l]), ffn1_w_gate[l], ffn1_w_up[l], ffn1_w_down[l])
        u = rms_norm(h, mix_norm[l])
        aq, ak, av, rq, rk, rv, rg = jnp.split(u @ w_in[l], IN_SPLITS, axis=-1)
        a = sliding_window_sink_attention(aq, ak, av, attn_sinks[l])
        r = multiscale_retention(rq, rk, rv, rg, ret_gn_w[l])
        h = h + jnp.concatenate([a, r], axis=-1) @ w_out[l]
        h = h + 0.5 * swiglu(rms_norm(h, ffn2_norm[l]), ffn2_w_gate[l], ffn2_w_up[l], ffn2_w_down[l])
    return rms_norm(h, final_norm)
```

```python
import numpy as np
from contextlib import ExitStack
import concourse.bass as bass
import concourse.mybir as mybir
from concourse.bass_utils import run_bass_kernel_spmd

F32 = mybir.dt.float32
BF16 = mybir.dt.bfloat16
AF = mybir.ActivationFunctionType
ALU = mybir.AluOpType
AX = mybir.AxisListType

D = 1024
KC = 8
FF = 2816
FC = 22
TS = 512
NB = 4
NLAYER = 4
SEQ = 2048
NCORE = 8
NORM_EPS = 1e-6
GN_EPS = 1e-5
PAGE = 512
RING_CAP = 20480
NWSEM = 12
MASK_NEG = -2400.0


class Rec:
    def __init__(self):
        self.ops = []
        self.last_w = {}
        self.readers = {}

    def add(self, eng, emit, reads=(), writes=(), dma_sem=None):
        idx = len(self.ops)
        deps = set()
        for r in reads:
            w = self.last_w.get(r)
            if w is not None:
                deps.add(w)
        for r in writes:
            w = self.last_w.get(r)
            if w is not None:
                deps.add(w)
            rs = self.readers.get(r)
            if rs:
                deps.update(rs)
        for r in reads:
            self.readers.setdefault(r, []).append(idx)
        for r in writes:
            self.last_w[r] = idx
            self.readers[r] = []
        self.ops.append(dict(eng=eng, emit=emit, deps=deps, dma_sem=dma_sem, sig=False, val=None, semkey=None))
        return idx

    def finalize(self):
        ops = self.ops
        for op in ops:
            keep = set()
            for d in op['deps']:
                dop = ops[d]
                if (dop['dma_sem'] is None and op['dma_sem'] is None
                        and dop['eng'] == 'pe' and op['eng'] == 'pe'):
                    continue
                keep.add(d)
                dop['sig'] = True
            op['deps'] = keep
        cnt = {}
        for op in ops:
            if op['dma_sem'] is not None:
                key = ('dma', op['dma_sem'])
                cnt[key] = cnt.get(key, 0) + 16
                op['val'] = cnt[key]
                op['semkey'] = key
                op['sig'] = True
            elif op['sig']:
                key = ('eng', op['eng'])
                cnt[key] = cnt.get(key, 0) + 1
                op['val'] = cnt[key]
                op['semkey'] = key
        return sorted({op['semkey'] for op in ops if op['sig']}, key=str)

    def emit_engine(self, engname, eng, sems):
        waited = {}
        for op in self.ops:
            if op['eng'] != engname:
                continue
            need = {}
            for d in op['deps']:
                dop = self.ops[d]
                k = dop['semkey']
                if dop['val'] > need.get(k, 0):
                    need[k] = dop['val']
            for k, v in need.items():
                if v > waited.get(k, 0):
                    eng.wait_ge(sems[k], v)
                    waited[k] = v
            if op['emit'] is not None:
                ins = op['emit'](eng)
                if op['sig']:
                    ins.then_inc(sems[op['semkey']], 16 if op['dma_sem'] is not None else 1)


class Ring:
    def __init__(self, rec, ring_t, cap, sched, use_scratch):
        self.rec = rec
        self.t = ring_t
        self.cap = cap
        self.sched = sched
        self.next_load = 0
        self.next_use = 0
        self.live = []
        self.head = 0
        self.off = {}
        self.ndma = 0
        self.nscr = 0
        self.use_scratch = use_scratch

    def _try_alloc(self, size):
        if not self.live:
            self.head = size
            return 0
        tail = self.live[0][1]
        if self.head > tail or (self.head == tail and False):
            if self.head + size <= self.cap:
                off = self.head
            elif size < tail:
                off = 0
            else:
                return None
        else:
            if self.head + size < tail:
                off = self.head
            else:
                return None
        self.head = off + size
        return off

    def pump(self):
        while self.next_load < len(self.sched):
            u = self.sched[self.next_load]
            size = u['size']
            asz = ((size + PAGE - 1) // PAGE) * PAGE
            off = self._try_alloc(asz)
            if off is None:
                return
            self.live.append((u['key'], off, asz))
            self.off[u['key']] = off
            pages = [('rg', p) for p in range(off // PAGE, (off + asz) // PAGE)]
            si = self.ndma % NWSEM
            self.ndma += 1
            dst = self.t[:, off:off + size]
            if u['first'] or not self.use_scratch:
                src = u['src32']
                self.rec.add('pool', lambda e, dst=dst, src=src: e.dma_start(out=dst, in_=src),
                             writes=pages + [('wsem', si)], dma_sem=f'w{si}')
                if self.use_scratch:
                    sj = self.nscr % 4
                    self.nscr += 1
                    scr = u['scr']
                    self.rec.add('sp', lambda e, dst=dst, scr=scr: e.dma_start(out=scr, in_=dst),
                                 reads=pages, writes=[('scr', u['skey']), ('ssem', sj)], dma_sem=f'sc{sj}')
            else:
                scr = u['scr']
                self.rec.add('pool', lambda e, dst=dst, scr=scr: e.dma_start(out=dst, in_=scr),
                             reads=[('scr', u['skey'])], writes=pages + [('wsem', si)], dma_sem=f'w{si}')
            self.next_load += 1

    def get(self, key):
        u = self.sched[self.next_use]
        assert u['key'] == key, (u['key'], key)
        assert self.next_load > self.next_use, "ring too small: unit not loaded " + str(key)
        self.next_use += 1
        return self.off[key]

    def pages(self, key, a, b):
        off = self.off[key]
        return [('rg', p) for p in range((off + a) // PAGE, (off + b - 1) // PAGE + 1)]

    def release(self, key):
        assert self.live[0][0] == key, (self.live[0][0], key)
        self.live.pop(0)
        self.pump()


def _c(a):
    return np.ascontiguousarray(a, dtype=np.float32)


def prep_weights(inp, nl):
    L = nl
    out = {}

    def gu(wg, wu):
        g = wg[:L].reshape(L, 8, 128, 11, 2, 128)
        u_ = wu[:L].reshape(L, 8, 128, 11, 2, 128)
        st = np.stack([g, u_], axis=0)
        st = st.transpose(1, 4, 3, 5, 0, 2, 6)
        return _c(st).reshape(L, 11, 128, 4096)

    def dn(wd):
        d = wd[:L].reshape(L, 22, 128, 8, 128)
        d = d.transpose(0, 3, 2, 1, 4)
        return _c(d).reshape(L, 8, 128, 2816)

    out['w_gu1'] = gu(inp['ffn1_w_gate'], inp['ffn1_w_up'])
    out['w_d1'] = dn(inp['ffn1_w_down'])
    out['w_gu2'] = gu(inp['ffn2_w_gate'], inp['ffn2_w_up'])
    out['w_d2'] = dn(inp['ffn2_w_down'])
    win = inp['w_in'][:L].reshape(L, 8, 128, 2816)
    aq = win[..., 0:512].reshape(L, 8, 128, 8, 64)
    aqt = np.concatenate([aq[:, :, :, 0:4, :], aq[:, :, :, 4:8, :]], axis=-1)
    out['w_aq'] = _c(aqt.transpose(0, 2, 1, 3, 4)).reshape(L, 1, 128, 4096)
    ak = win[..., 512:640]
    akp = np.zeros((L, 8, 128, 2, 128), np.float32)
    akp[:, :, :, 0, 0:64] = ak[..., 0:64]
    akp[:, :, :, 1, 64:128] = ak[..., 64:128]
    out['w_ak'] = _c(akp.transpose(0, 2, 1, 3, 4)).reshape(L, 1, 128, 2048)
    out['w_av'] = _c(win[..., 640:768].transpose(0, 2, 1, 3)).reshape(L, 1, 128, 1024)
    r = win[..., 768:2816].reshape(L, 8, 128, 4, 512)
    out['w_r'] = _c(r.transpose(0, 3, 2, 1, 4)).reshape(L, 4, 128, 4096)
    wo = inp['w_out'][:L]
    wa = wo[:, 0:512].reshape(L, 8, 64, 1024)
    wat = np.concatenate([wa[:, 0:4], wa[:, 4:8]], axis=2)
    wr = wo[:, 512:1024].reshape(L, 4, 128, 1024)
    woc = np.concatenate([wat, wr], axis=1)
    woc = woc.reshape(L, 2, 4, 128, 1024).transpose(0, 1, 3, 2, 4)
    out['w_o'] = _c(woc).reshape(L, 2, 128, 4096)
    nrm = np.stack([inp['ffn1_norm'][:L], inp['mix_norm'][:L], inp['ffn2_norm'][:L]], axis=1)
    nrm = nrm.reshape(L, 3, 8, 128).transpose(3, 0, 1, 2)
    fin = inp['final_norm'].reshape(8, 128).T
    out['c_norm'] = _c(np.concatenate([nrm.reshape(128, L * 24), fin], axis=1))
    out['c_gnw'] = _c(inp['ret_gn_w'][:L].reshape(L * 4, 128).T)
    out['c_sink'] = _c(np.broadcast_to(inp['attn_sinks'][:L].reshape(1, L * 8), (128, L * 8)))
    return out


def const_tables(nt):
    S = nt * TS
    pos = np.arange(S, dtype=np.float32)
    inv_freq = (10000.0 ** (-np.arange(0, 128, 2, dtype=np.float32) / 128.0)).astype(np.float32)
    ang = pos[:, None] * inv_freq[None, :]
    cos = np.cos(ang).astype(np.float32).reshape(S // 128, 128, 64).transpose(1, 0, 2)
    sin = np.sin(ang).astype(np.float32).reshape(S // 128, 128, 64).transpose(1, 0, 2)
    t = {}
    t['c_cos'] = _c(cos).reshape(128, (S // 128) * 64)
    t['c_sin'] = _c(sin).reshape(128, (S // 128) * 64)
    H = 4
    lg = np.log(1.0 - 2.0 ** (-5.0 - np.arange(H, dtype=np.float32))).astype(np.float32)
    idx = np.arange(128, dtype=np.float32)
    dif = idx[:, None] - idx[None, :]
    dm = np.where(dif[None] >= 0, np.exp(np.maximum(dif, 0.0)[None] * lg[:, None, None]), 0.0)
    dmt = dm.transpose(2, 0, 1) * np.float32(128.0 ** -0.5)
    t['c_dmat'] = _c(dmt).reshape(128, 512)
    zeta = np.exp((127.0 - idx)[None, :] * lg[:, None]) * np.float32(128.0 ** -0.5)
    xi = np.exp((idx + 1.0)[None, :] * lg[:, None])
    t['c_xi'] = _c(np.broadcast_to(xi.reshape(1, 512), (128, 512)))
    dec = np.exp(128.0 * lg)
    t['c_zs'] = _c(zeta.T)
    t['dec'] = [float(v) for v in dec.astype(np.float32)]
    kk = np.arange(128)[:, None]
    qq = np.arange(128)[None, :]
    m0 = np.where(kk <= qq, 0.0, MASK_NEG)
    m1 = np.where(qq < kk, 0.0, MASK_NEG)
    t['c_mask'] = _c(np.stack([m0, m1], axis=1)).reshape(128, 256)
    t['c_ident'] = _c(np.eye(128))
    return t


WSPEC = [('w_gu1', 11, 4096), ('w_d1', 8, 2816), ('w_aq', 1, 4096), ('w_ak', 1, 2048), ('w_av', 1, 1024),
         ('w_r', 4, 4096), ('w_o', 2, 4096), ('w_gu2', 11, 4096), ('w_d2', 8, 2816)]


def build(nl, nt, first_layer_only_ffn=False, flags=()):
    nc = bass.Bass("TRN2", target_bir_lowering=False)
    S = nt * TS
    NBLK = S // 128
    tabs = const_tables(nt)
    dec = tabs['dec']
    dram = {}
    for name, nu, sz in WSPEC:
        dram[name] = nc.dram_tensor(name, [nl, nu, 128, sz], F32, kind="ExternalInput").ap()
    use_scratch = nt > 1
    scr = {}
    if use_scratch:
        for name, nu, sz in WSPEC:
            scr[name] = nc.dram_tensor("s_" + name, [nl, nu, 128, sz], BF16, kind="Internal").ap()
    cshape = {'c_norm': nl * 24 + 8, 'c_gnw': nl * 4, 'c_sink': nl * 8, 'c_cos': NBLK * 64, 'c_sin': NBLK * 64,
              'c_dmat': 512, 'c_xi': 512, 'c_zs': 4, 'c_mask': 256, 'c_ident': 128}
    for name, w in cshape.items():
        dram[name] = nc.dram_tensor(name, [128, w], F32, kind="ExternalInput").ap()
    xT = nc.dram_tensor("xT", [KC, 128, S], F32, kind="ExternalInput").ap()
    yT = nc.dram_tensor("yT", [KC, 128, S], F32, kind="ExternalOutput").ap()
    dbg = nc.dram_tensor("dbg", [128, 40 * TS], BF16, kind="ExternalOutput").ap() if 'dbg' in flags else None

    sched = []
    for t in range(nt):
        for l in range(nl):
            order = [('w_gu1', 11), ('w_d1', 8), ('w_aq', 1), ('w_ak', 1), ('w_av', 1), ('w_r', 4), ('w_o', 2),
                     ('w_gu2', 11), ('w_d2', 8)]
            for name, nu in order:
                sz = dict((n, s) for n, _, s in WSPEC)[name]
                for u in range(nu):
                    sched.append(dict(key=(t, l, name, u), skey=(l, name, u), size=sz, first=(t == 0),
                                      src32=dram[name][l, u], scr=(scr[name][l, u] if use_scratch else None)))

    with ExitStack() as es:
        def sb(name, shape, dt):
            return es.enter_context(nc.sbuf_tensor(name, shape, dt))

        rec = Rec()
        ring_t = sb("ring", [128, RING_CAP], BF16)
        ring = Ring(rec, ring_t, RING_CAP, sched, use_scratch)
        hT = sb("hT", [128, KC, TS], F32)
        nT = sb("nT", [128, KC, TS], BF16)
        big = sb("big", [128, 48, TS], BF16)
        sgt = [sb(f"sg{i}", [128, TS], F32) for i in range(2)]
        rstd = sb("rstd", [128, TS], F32)
        cn = sb("cn", [128, cshape['c_norm']], F32)
        gnw = sb("gnw", [128, nl * 4], F32)
        sink = sb("sink", [128, nl * 8], F32)
        sinke = sb("sinke", [128, nl * 8], F32)
        cos_t = sb("cos", [128, NBLK * 64], F32)
        sin_t = sb("sin", [128, NBLK * 64], F32)
        dmat = sb("dmat", [128, 512], F32)
        xi_t = sb("xi", [128, 512], F32)
        zs_t = sb("zs", [128, 4], F32)
        maskf = sb("maskf", [128, 256], F32)
        maskb = sb("maskb", [128, 256], BF16)
        identf = sb("identf", [128, 128], F32)
        ident = sb("ident", [128, 128], BF16)
        onesd = sb("onesd", [128, 128], BF16)
        ones = sb("ones", [128, 128], BF16)
        akP = [sb(f"akP{l}", [128, 2, 128], BF16) for l in range(nl)]
        avP = [sb(f"avP{l}", [128, 128], BF16) for l in range(nl)]
        Rst = [sb(f"R{l}", [128, 512], F32) for l in range(nl)]
        RbC = [sb(f"RbC{l}", [128, 512], BF16) for l in range(nl)]
        Rbt = [sb(f"Rbt{i}", [128, 512], BF16) for i in range(3)]
        akC = sb("akC", [128, 2, TS], BF16)
        skb = sb("skb", [128, 2, TS], BF16)
        avC = sb("avC", [128, NB, 128], BF16)
        rt = [sb(f"rt{i}", [128, 256], F32) for i in range(8)]
        Et = [sb(f"E{i}", [128, TS], BF16) for i in range(4)]
        Sd = [sb(f"Sd{i}", [128, TS], BF16) for i in range(4)]
        gs = sb("gs", [128, NB, TS], F32)
        tmpf = [sb(f"tmpf{i}", [128, TS], F32) for i in range(3)]
        ycb = [sb(f"ycb{i}", [128, TS], F32) for i in range(2)]
        stat = [sb(f"stat{i}", [128, 16], F32) for i in range(2)]
        rtok = [sb(f"rtok{i}", [128, TS], BF16) for i in range(2)]
        NPS = 5
        psf = [es.enter_context(nc.psum_tensor(f"ps{i}", [128, TS], F32)) for i in range(NPS + 1)]
        psbs = [es.enter_context(nc.psum_tensor(f"psb{i}", [128, 2 * TS], BF16)) for i in range(2)]

        state = dict(bank=0, cnt=0)
        def slot(s0, n):
            return big[:, s0:s0 + n, :]
        aT = lambda fc: big[:, fc, :]
        S_AQ, S_QTOK, S_KTOK, S_QT, S_QX, S_KT, S_VT, S_VZ, S_CAT, S_SQ = 0, 4, 8, 12, 16, 20, 24, 28, 32, 40
        SUBS = {}
        for s_ in range(4, 12):
            SUBS[s_] = ['lo', 'hi']
        for s_ in range(12, 24):
            SUBS[s_] = [0, 1, 2, 3]
        for s_ in range(32, 36):
            SUBS[s_] = [0, 1]
        for s_ in range(36, 40):
            SUBS[s_] = [0, 1, 2, 3]

        def bw(slot_, sub=None):
            if sub is None:
                return [('big', slot_)] + [('big', slot_, x) for x in SUBS.get(slot_, [])]
            return [('big', slot_), ('big', slot_, sub)]

        def br(slot_, sub=None):
            if sub is None:
                if slot_ in SUBS:
                    return [('big', slot_, x) for x in SUBS[slot_]]
                return [('big', slot_)]
            return [('big', slot_, sub)]


        def nb():
            b = state['bank']
            state['bank'] = (b + 1) % NPS
            return b

        disabled = {f[4:] for f in flags if f.startswith('off_')}
        state['stage'] = 'init'

        def A(eng, fn, reads=(), writes=()):
            if state['stage'] in disabled:
                return
            reads = list(reads)
            writes = list(writes)
            for r in reads:
                if isinstance(r, tuple) and r[0] in ('ps', 'psb') and r not in writes:
                    writes.append(r)
            rec.add(eng, fn, reads=reads, writes=writes)

        cl = [('c_norm', cn), ('c_gnw', gnw), ('c_sink', sink), ('c_cos', cos_t), ('c_sin', sin_t), ('c_dmat', dmat),
              ('c_xi', xi_t), ('c_zs', zs_t), ('c_mask', maskf), ('c_ident', identf)]
        for i, (name, tt) in enumerate(cl):
            rec.add('sp', lambda e, tt=tt, name=name: e.dma_start(out=tt[:], in_=dram[name]),
                    writes=[('c', name)], dma_sem=f'c{i}')
        A('dve', lambda e: e.tensor_copy(out=maskb[:], in_=maskf[:]), [('c', 'c_mask')], ['maskb'])
        A('dve', lambda e: e.tensor_copy(out=ident[:], in_=identf[:]), [('c', 'c_ident')], ['ident'])
        A('dve', lambda e: e.memset(onesd[:], 1.0 / D), [], ['onesd'])
        A('dve', lambda e: e.memset(ones[:], 1.0), [], ['ones'])
        A('act', lambda e: e.activation(out=sinke[:], in_=sink[:], func=AF.Exp), [('c', 'c_sink')], ['sinke'])
        for l in range(nl):
            A('dve', lambda e, l=l: e.memset(Rst[l][:], 0.0), [], [('R', l)])
            A('dve', lambda e, l=l: e.memset(RbC[l][:], 0.0), [], [('RbC', l)])
        ring.pump()

        npst = dict(pending=[], nmm=0)
        NB_STAT = NPS

        def np_start():
            npst['pending'] = []
            npst['nmm'] = 0

        def np_feed(k):
            A('act', lambda e, k=k: e.activation(out=big[:, S_SQ + k, :], in_=hT[:, k, :], func=AF.Square),
              [('hT', k)], bw(S_SQ + k))
            npst['pending'].append(k)

        def np_mm():
            for k in npst['pending']:
                i = npst['nmm']
                A('pe', lambda e, k=k, i=i: e.matmul(psf[NB_STAT][:], onesd[:], big[:, S_SQ + k, :], start=(i == 0), stop=(i == KC - 1)),
                  [('big', S_SQ + k), 'onesd'], [('ps', NB_STAT)])
                npst['nmm'] += 1
            npst['pending'] = []

        def np_finish(l, which, dst_is_h=False):
            cbase = (l * 24 + which * 8) if l is not None else nl * 24
            np_mm()
            assert npst['nmm'] == KC
            b = NB_STAT
            A('act', lambda e, b=b: e.activation(out=rstd[:], in_=psf[b][:], func=AF.Ln, bias=NORM_EPS, scale=1.0),
              [('ps', b)], ['rstd'])
            A('act', lambda e: e.activation(out=rstd[:], in_=rstd[:], func=AF.Exp, scale=-0.5), ['rstd'], ['rstd'])
            for k in range(KC):
                if dst_is_h:
                    A('dve', lambda e, k=k: e.scalar_tensor_tensor(out=hT[:, k, :], in0=hT[:, k, :], scalar=cn[:, cbase + k:cbase + k + 1],
                                                                    in1=rstd[:], op0=ALU.mult, op1=ALU.mult),
                      [('hT', k), 'rstd', ('c', 'c_norm')], [('hT', k)])
                else:
                    A('dve', lambda e, k=k: e.scalar_tensor_tensor(out=nT[:, k, :], in0=hT[:, k, :], scalar=cn[:, cbase + k:cbase + k + 1],
                                                                    in1=rstd[:], op0=ALU.mult, op1=ALU.mult),
                      [('hT', k), 'rstd', ('c', 'c_norm')], [('nT', k)])

        def ffn(t, l, which):
            gname = 'w_gu1' if which == 0 else 'w_gu2'
            dname = 'w_d1' if which == 0 else 'w_d2'
            np_finish(l, 0 if which == 0 else 2)
            for u in range(11):
                key = (t, l, gname, u)
                off = ring.get(key)
                for fcl in range(2):
                    fc = 2 * u + fcl
                    bg, bu = nb(), nb()
                    for gi, b in ((0, bg), (1, bu)):
                        for k in range(KC):
                            a = ((fcl * 2 + gi) * 8 + k) * 128
                            A('pe', lambda e, b=b, a=a, k=k, off=off: e.matmul(psf[b][:], ring_t[:, off + a:off + a + 128], nT[:, k, :],
                                                                                 start=(k == 0), stop=(k == KC - 1)),
                              ring.pages(key, a, a + 128) + [('nT', k)], [('ps', b)])
                    sg = sgt[fc % 2]
                    A('act', lambda e, sg=sg, bg=bg: e.activation(out=sg[:], in_=psf[bg][:], func=AF.Silu),
                      [('ps', bg)], [('sg', fc % 2)])
                    A('dve', lambda e, sg=sg, bu=bu, fc=fc: e.tensor_tensor(out=big[:, fc, :], in0=psf[bu][:], in1=sg[:], op=ALU.mult),
                      [('ps', bu), ('sg', fc % 2)], bw(fc))
                ring.release(key)
            np_start()
            for dc in range(KC):
                key = (t, l, dname, dc)
                off = ring.get(key)
                b = nb()
                for fc in range(FC):
                    a = fc * 128
                    A('pe', lambda e, b=b, a=a, fc=fc, off=off: e.matmul(psf[b][:], ring_t[:, off + a:off + a + 128], big[:, fc, :],
                                                                           start=(fc == 0), stop=(fc == FC - 1)),
                      ring.pages(key, a, a + 128) + [('big', fc)], [('ps', b)])
                np_mm()
                A('dve', lambda e, b=b, dc=dc: e.scalar_tensor_tensor(out=hT[:, dc, :], in0=psf[b][:], scalar=0.5, in1=hT[:, dc, :],
                                                                       op0=ALU.mult, op1=ALU.add),
                  [('ps', b), ('hT', dc)], [('hT', dc)])
                np_feed(dc)
                ring.release(key)

        def mixer(t, l):
            state['stage'] = 'mnorm'
            np_finish(l, 1)
            G0 = t * NB
            noret = 'noret' in flags
            state['stage'] = 'aq'
            key = (t, l, 'w_aq', 0)
            off = ring.get(key)
            for j in range(4):
                b = nb()
                for k in range(KC):
                    a = (k * 4 + j) * 128
                    A('pe', lambda e, b=b, a=a, k=k, off=off: e.matmul(psf[b][:], ring_t[:, off + a:off + a + 128], nT[:, k, :],
                                                                         start=(k == 0), stop=(k == KC - 1)),
                      ring.pages(key, a, a + 128) + [('nT', k)], [('ps', b)])
                A('act', lambda e, b=b, j=j: e.activation(out=big[:, S_AQ + j, :], in_=psf[b][:], func=AF.Copy),
                  [('ps', b)], bw(S_AQ + j))
            ring.release(key)
            state['stage'] = 'ak'
            key = (t, l, 'w_ak', 0)
            off = ring.get(key)
            for g in range(2):
                b = nb()
                for k in range(KC):
                    a = (k * 2 + g) * 128
                    A('pe', lambda e, b=b, a=a, k=k, off=off: e.matmul(psf[b][:], ring_t[:, off + a:off + a + 128], nT[:, k, :],
                                                                         start=(k == 0), stop=(k == KC - 1)),
                      ring.pages(key, a, a + 128) + [('nT', k)], [('ps', b)])
                A('act', lambda e, b=b, g=g: e.activation(out=akC[:, g, :], in_=psf[b][:], func=AF.Copy),
                  [('ps', b)], [('akC', g)])
            ring.release(key)
            state['stage'] = 'av'
            key = (t, l, 'w_av', 0)
            off = ring.get(key)
            b = nb()
            for blk in range(NB):
                for k in range(KC):
                    a = k * 128
                    A('pe', lambda e, b=b, a=a, k=k, blk=blk, off=off: e.matmul(psf[b][:, blk * 128:(blk + 1) * 128], nT[:, k, blk * 128:(blk + 1) * 128],
                                                                                  ring_t[:, off + a:off + a + 128], start=(k == 0), stop=(k == KC - 1)),
                      ring.pages(key, a, a + 128) + [('nT', k)], [('ps', b)])
            A('act', lambda e, b=b: e.activation(out=avC[:].rearrange("p a b -> p (a b)"), in_=psf[b][:], func=AF.Copy),
              [('ps', b)], ['avC'])
            ring.release(key)

            state['stage'] = 'att'
            items = [(g, qb) for qb in range(NB) for g in range(2)]
            for g_ in range(2):
                A('dve', lambda e, g_=g_: e.tensor_scalar(out=skb[:, g_, :].rearrange("p (a b) -> p a b", a=4),
                                                          in0=sinke[:, l * 8 + 4 * g_:l * 8 + 4 * g_ + 4].unsqueeze(2).to_broadcast([128, 4, 128]),
                                                          scalar1=1.0 / 128, scalar2=None, op0=ALU.mult),
                  ['sinke'], [('skb', g_)])

            def att_front(g, qb, ei):
                state['stage'] = 'att'
                G = G0 + qb
                kbs = []
                if G > 0:
                    kbs.append(1)
                kbs.append(0)
                outs = []
                for kind in kbs:
                    b = nb()
                    if kind == 0:
                        lk = akC[:, g, qb * 128:(qb + 1) * 128]
                        rk = [('akC', g)]
                    elif qb == 0:
                        lk = akP[l][:, g, :]
                        rk = [('akP', l)]
                    else:
                        lk = akC[:, g, (qb - 1) * 128:qb * 128]
                        rk = [('akC', g)]
                    rq = big[:, S_AQ:S_AQ + 4, qb * 128:(qb + 1) * 128]
                    A('pe', lambda e, b=b, lk=lk, rq=rq: e.matmul(psf[b][:], lk, rq, start=True, stop=False),
                      rk + [('big', S_AQ + j) for j in range(4)], [('ps', b)])
                    mb = maskb[:, kind * 128:(kind + 1) * 128].unsqueeze(1).to_broadcast([128, 4, 128])
                    A('pe', lambda e, b=b, mb=mb: e.matmul(psf[b][:], ident[:], mb, start=False, stop=True),
                      ['maskb', 'ident'], [('ps', b)])
                    ee = ei[0] % 4
                    ei[0] += 1
                    A('act', lambda e, b=b, ee=ee: e.activation(out=Et[ee][:], in_=psf[b][:], func=AF.Exp, scale=0.125),
                      [('ps', b)], [('E', ee)])
                    outs.append((kind, ee))
                return outs

            def att_back(g, qb, outs):
                state['stage'] = 'att'
                bpv, bden = nb(), nb()
                n = len(outs)
                for i, (kind, ee) in enumerate(outs):
                    if kind == 0:
                        lv = avC[:, qb, :]
                        rv = ['avC']
                    elif qb == 0:
                        lv = avP[l][:]
                        rv = [('avP', l)]
                    else:
                        lv = avC[:, qb - 1, :]
                        rv = ['avC']
                    A('pe', lambda e, lv=lv, ee=ee, i=i, n=n, bpv=bpv: e.matmul(psf[bpv][:], lv, Et[ee][:], start=(i == 0), stop=(i == n - 1)),
                      rv + [('E', ee)], [('ps', bpv)])
                for i, (kind, ee) in enumerate(outs):
                    A('pe', lambda e, ee=ee, i=i, n=n, bden=bden: e.matmul(psf[bden][:], ones[:], Et[ee][:], start=(i == 0), stop=False),
                      ['ones', ('E', ee)], [('ps', bden)])
                A('pe', lambda e, bden=bden: e.matmul(psf[bden][:], ones[:], skb[:, g, :], start=False, stop=True),
                  ['ones', ('skb', g)], [('ps', bden)])
                r0, r1 = g * 64, (g + 1) * 64
                tf = tmpf[(qb * 2 + g) % 2]
                tfk = ('tmpf', (qb * 2 + g) % 2)
                A('act', lambda e, tf=tf, bden=bden: e.activation(out=tf[r0:r1, :], in_=psf[bden][r0:r1, :], func=AF.Ln), [('ps', bden)], [tfk])
                A('act', lambda e, tf=tf: e.activation(out=tf[r0:r1, :], in_=tf[r0:r1, :], func=AF.Exp, scale=-1.0), [tfk], [tfk])
                A('dve', lambda e, tf=tf, bpv=bpv: e.tensor_tensor(out=big[r0:r1, S_CAT:S_CAT + 4, qb * 128:(qb + 1) * 128],
                                                                    in0=psf[bpv][r0:r1, :].rearrange("p (a b) -> p a b", a=4),
                                                                    in1=tf[r0:r1, :].rearrange("p (a b) -> p a b", a=4), op=ALU.mult),
                  [('ps', bpv), tfk], sum([bw(S_CAT + j, g) for j in range(4)], []))

            def tm_block(u, off, blk):
                state['stage'] = f'rp{u}'
                key = (t, l, 'w_r', u)
                b = nb()
                for k in range(KC):
                    a = k * 512
                    A('pe', lambda e, b=b, a=a, k=k, blk=blk, off=off: e.matmul(psf[b][:], nT[:, k, blk * 128:(blk + 1) * 128],
                                                                                  ring_t[:, off + a:off + a + 512], start=(k == 0), stop=(k == KC - 1)),
                      ring.pages(key, a, a + 512) + [('nT', k)], [('ps', b)])
                return b

            def rot_front(dst_slot, blk, b, ri):
                state['stage'] = 'rot'
                G = G0 + blk
                X = psf[b][:].rearrange("p (h two e) -> p h two e", h=4, two=2)
                x1, x2 = X[:, :, 0, :], X[:, :, 1, :]
                cb = cos_t[:, G * 64:(G + 1) * 64].unsqueeze(1).to_broadcast([128, 4, 64])
                sn = sin_t[:, G * 64:(G + 1) * 64].unsqueeze(1).to_broadcast([128, 4, 64])
                r3 = [rt[ri * 4 + i][:].rearrange("p (h e) -> p h e", h=4) for i in range(4)]
                rk_ = [('rt', ri * 4 + i) for i in range(4)]
                O = big[:, dst_slot + blk, :].rearrange("p (h two e) -> p h two e", h=4, two=2)
                cst = [('c', 'c_cos'), ('c', 'c_sin')]
                A('dve', lambda e: e.tensor_tensor(out=r3[0], in0=x1, in1=cb, op=ALU.mult), [('ps', b)] + cst, [rk_[0]])
                A('dve', lambda e: e.tensor_tensor(out=r3[1], in0=x2, in1=sn, op=ALU.mult), [('ps', b)] + cst, [rk_[1]])
                A('dve', lambda e: e.tensor_tensor(out=r3[2], in0=x1, in1=sn, op=ALU.mult), [('ps', b)] + cst, [rk_[2]])
                A('dve', lambda e: e.tensor_tensor(out=r3[3], in0=x2, in1=cb, op=ALU.mult), [('ps', b)] + cst, [rk_[3]])
                A('dve', lambda e: e.tensor_tensor(out=O[:, :, 0, :], in0=r3[0], in1=r3[1], op=ALU.subtract),
                  [rk_[0], rk_[1]], bw(dst_slot + blk, 'lo'))
                A('dve', lambda e: e.tensor_tensor(out=O[:, :, 1, :], in0=r3[2], in1=r3[3], op=ALU.add),
                  [rk_[2], rk_[3]], bw(dst_slot + blk, 'hi'))

            def rot_back(dst_slot, t_slot, x_slot, blk):
                state['stage'] = 'tr'
                hb = state['cnt'] % 2
                state['cnt'] += 1
                for h in range(4):
                    A('pe', lambda e, h=h, hb=hb: e.transpose(psbs[hb][:, h * 128:(h + 1) * 128],
                                                              big[:, dst_slot + blk, h * 128:(h + 1) * 128], ident[:]),
                      [('big', dst_slot + blk, 'lo'), ('big', dst_slot + blk, 'hi'), 'ident'], [('psb', hb)])
                pv = psbs[hb][:, 0:512].rearrange("p (h i) -> p h i", h=4)
                A('act', lambda e, pv=pv: e.activation(out=big[:, t_slot:t_slot + 4, blk * 128:(blk + 1) * 128], in_=pv, func=AF.Copy),
                  [('psb', hb)], sum([bw(t_slot + h, blk) for h in range(4)], []))
                if x_slot is not None:
                    A('dve', lambda e, pv=pv: e.tensor_tensor(out=big[:, x_slot:x_slot + 4, blk * 128:(blk + 1) * 128], in0=pv,
                                                               in1=xi_t[:].rearrange("p (h i) -> p h i", h=4), op=ALU.mult),
                      [('psb', hb), ('c', 'c_xi')], sum([bw(x_slot + h, blk) for h in range(4)], []))

            def v_post(blk, b):
                state['stage'] = 'vpost'
                A('act', lambda e: e.activation(out=big[:, S_VT + blk, :], in_=psf[b][:], func=AF.Copy),
                  [('ps', b)], bw(S_VT + blk))
                A('dve', lambda e: e.tensor_tensor(out=big[:, S_VZ + blk, :].rearrange("p (h v) -> p h v", h=4),
                                                    in0=psf[b][:].rearrange("p (h v) -> p h v", h=4),
                                                    in1=zs_t[:].unsqueeze(2).to_broadcast([128, 4, 128]), op=ALU.mult),
                  [('ps', b), ('c', 'c_zs')], bw(S_VZ + blk))

            def g_post(blk, b):
                state['stage'] = 'gpost'
                A('act', lambda e: e.activation(out=gs[:, blk, :], in_=psf[b][:], func=AF.Silu), [('ps', b)], [('gs', blk)])

            state['stage'] = 'chunks'
            def rb_before(blk):
                return (RbC[l], ('RbC', l)) if blk == 0 else (Rbt[blk - 1], ('Rbt', blk - 1))

            def ch_scores(blk):
                state['stage'] = 'chunks'
                cs = slice(blk * 128, (blk + 1) * 128)
                bs = nb()
                for h in range(4):
                    A('pe', lambda e, h=h, bs=bs: e.matmul(psf[bs][:, h * 128:(h + 1) * 128], big[:, S_KT + h, cs], big[:, S_QT + h, cs],
                                                           start=True, stop=True),
                      [('big', S_KT + h, blk), ('big', S_QT + h, blk)], [('ps', bs)])
                sd = Sd[blk]
                A('dve', lambda e, bs=bs, sd=sd: e.tensor_tensor(out=sd[:], in0=psf[bs][:], in1=dmat[:], op=ALU.mult),
                  [('ps', bs), ('c', 'c_dmat')], [('Sd', blk)])

            def ch_state(blk):
                state['stage'] = 'chunks'
                bk = nb()
                for h in range(4):
                    hs = slice(h * 128, (h + 1) * 128)
                    A('pe', lambda e, h=h, hs=hs, bk=bk: e.matmul(psf[bk][:, hs], big[:, S_KTOK + blk, hs], big[:, S_VZ + blk, hs], start=True, stop=True),
                      [('big', S_KTOK + blk, 'lo'), ('big', S_KTOK + blk, 'hi'), ('big', S_VZ + blk)], [('ps', bk)])
                for h in range(4):
                    hs = slice(h * 128, (h + 1) * 128)
                    A('dve', lambda e, h=h, hs=hs, bk=bk: e.scalar_tensor_tensor(out=Rst[l][:, hs], in0=Rst[l][:, hs], scalar=dec[h], in1=psf[bk][:, hs],
                                                                                   op0=ALU.mult, op1=ALU.add),
                      [('R', l), ('ps', bk)], [('R', l)])
                if blk < NB - 1:
                    A('act', lambda e: e.activation(out=Rbt[blk][:], in_=Rst[l][:], func=AF.Copy), [('R', l)], [('Rbt', blk)])

            def ch_p1(blk):
                state['stage'] = 'chunks'
                n = G0 + blk
                cs = slice(blk * 128, (blk + 1) * 128)
                sd = Sd[blk]
                by = nb()
                rbp, rbk = rb_before(blk)
                for h in range(4):
                    hs = slice(h * 128, (h + 1) * 128)
                    A('pe', lambda e, h=h, hs=hs, by=by, sd=sd: e.matmul(psf[by][:, hs], sd[:, hs], big[:, S_VT + blk, hs], start=True, stop=False),
                      [('Sd', blk), ('big', S_VT + blk)], [('ps', by)])
                    A('pe', lambda e, h=h, hs=hs, by=by, rbp=rbp: e.matmul(psf[by][:, hs], big[:, S_QX + h, cs], rbp[:, hs], start=False, stop=True),
                      [('big', S_QX + h, blk), rbk], [('ps', by)])
                st = stat[n % 2]
                sk = ('stat', n % 2)
                y3 = psf[by][:].rearrange("p (h v) -> p h v", h=4)
                yc = ycb[n % 2]
                yk = ('ycb', n % 2)
                yc3 = yc[:].rearrange("p (h v) -> p h v", h=4)
                ysq = tmpf[2]
                A('dve', lambda e, st=st, y3=y3: e.tensor_reduce(out=st[:, 0:4], in_=y3, axis=AX.X, op=ALU.add), [('ps', by)], [sk])
                A('dve', lambda e, st=st: e.tensor_scalar(out=st[:, 0:4], in0=st[:, 0:4], scalar1=1.0 / 128, scalar2=None, op0=ALU.mult), [sk], [sk])
                A('dve', lambda e, st=st, y3=y3, yc3=yc3: e.tensor_tensor(out=yc3, in0=y3, in1=st[:, 0:4].unsqueeze(2).to_broadcast([128, 4, 128]), op=ALU.subtract),
                  [('ps', by), sk], [yk])
                A('act', lambda e, yc=yc: e.activation(out=ysq[:], in_=yc[:], func=AF.Square), [yk], [('tmpf', 2)])
                A('dve', lambda e, st=st: e.tensor_reduce(out=st[:, 4:8], in_=ysq[:].rearrange("p (h v) -> p h v", h=4), axis=AX.X, op=ALU.add),
                  [('tmpf', 2)], [sk])

            def ch_p2(blk):
                state['stage'] = 'chunks'
                n = G0 + blk
                st = stat[n % 2]
                sk = ('stat', n % 2)
                A('act', lambda e, st=st: e.activation(out=st[:, 4:8], in_=st[:, 4:8], func=AF.Ln, bias=GN_EPS, scale=1.0 / 128), [sk], [sk])
                A('act', lambda e, st=st: e.activation(out=st[:, 4:8], in_=st[:, 4:8], func=AF.Exp, scale=-0.5), [sk], [sk])

            def ch_p3(blk):
                state['stage'] = 'chunks'
                n = G0 + blk
                st = stat[n % 2]
                sk = ('stat', n % 2)
                yc = ycb[n % 2]
                yk = ('ycb', n % 2)
                yc3 = yc[:].rearrange("p (h v) -> p h v", h=4)
                A('dve', lambda e, st=st, yc3=yc3: e.tensor_tensor(out=yc3, in0=yc3, in1=st[:, 4:8].unsqueeze(2).to_broadcast([128, 4, 128]), op=ALU.mult),
                  [yk, sk], [yk])
                ro = rtok[n % 2]
                A('dve', lambda e, ro=ro, yc=yc: e.tensor_tensor(out=ro[:], in0=yc[:], in1=gs[:, blk, :], op=ALU.mult),
                  [yk, ('gs', blk)], [('rtok', n % 2)])

            def ch_tr(blk):
                state['stage'] = 'chunks'
                n = G0 + blk
                cs = slice(blk * 128, (blk + 1) * 128)
                ro = rtok[n % 2]
                hb = state['cnt'] % 2
                state['cnt'] += 1
                for h in range(4):
                    A('pe', lambda e, h=h, hb=hb, ro=ro: e.transpose(psbs[hb][:, h * 128:(h + 1) * 128], ro[:, h * 128:(h + 1) * 128], ident[:]),
                      [('rtok', n % 2), 'ident'], [('psb', hb)])
                for h in range(4):
                    A('act', lambda e, h=h, hb=hb: e.activation(out=big[:, S_CAT + 4 + h, cs], in_=psbs[hb][:, h * 128:(h + 1) * 128], func=AF.Copy,
                                                                scale=gnw[:, l * 4 + h:l * 4 + h + 1]),
                      [('psb', hb), ('c', 'c_gnw')], bw(S_CAT + 4 + h, blk))

            ei = [0]
            pend = None
            pend_tr = []
            roff = {}
            if noret:
                for u in range(4):
                    ring.get((t, l, 'w_r', u)); ring.release((t, l, 'w_r', u))
                for h in range(4):
                    A('dve', lambda e, h=h: e.memset(big[:, S_CAT + 4 + h, :], 0.0), [], bw(S_CAT + 4 + h))
            if 'noatt' in flags:
                for j in range(4):
                    A('dve', lambda e, j=j: e.memset(big[:, S_CAT + j, :], 0.0), [], bw(S_CAT + j))
            csched = {
                4: [(ch_scores, 0), (ch_scores, 1)],
                5: [(ch_scores, 2), (ch_scores, 3), (ch_state, 0), (ch_p1, 0)],
                6: [(ch_state, 1), (ch_p2, 0), (ch_p1, 1)],
                7: [(ch_state, 2), (ch_p3, 0), (ch_p2, 1), (ch_p1, 2)],
                8: [(ch_state, 3), (ch_tr, 0), (ch_p3, 1), (ch_p2, 2), (ch_p1, 3)],
                9: [(ch_tr, 1), (ch_p3, 2), (ch_p2, 3)],
                10: [(ch_tr, 2), (ch_p3, 3)],
                11: [(ch_tr, 3)],
            }
            for s_i, (g, qb) in enumerate(items):
                if not noret and s_i == 0:
                    roff[0] = ring.get((t, l, 'w_r', 0))
                    roff[1] = ring.get((t, l, 'w_r', 1))
                if not noret and s_i == 4:
                    roff[2] = ring.get((t, l, 'w_r', 2))
                    roff[3] = ring.get((t, l, 'w_r', 3))
                outs = None
                if 'noatt' not in flags:
                    outs = att_front(g, qb, ei)
                new_tr = []
                if not noret:
                    if s_i < 4:
                        blk = s_i
                        bq = tm_block(0, roff[0], blk)
                        rot_front(S_QTOK, blk, bq, 0)
                        bk_ = tm_block(1, roff[1], blk)
                        rot_front(S_KTOK, blk, bk_, 1)
                        new_tr = [(S_QTOK, S_QT, S_QX, blk), (S_KTOK, S_KT, None, blk)]
                    else:
                        blk = s_i - 4
                        bv = tm_block(2, roff[2], blk)
                        v_post(blk, bv)
                        bg_ = tm_block(3, roff[3], blk)
                        g_post(blk, bg_)
                if 'noatt' not in flags:
                    if pend is not None:
                        att_back(*pend)
                    pend = (g, qb, outs)
                for tr in pend_tr:
                    rot_back(*tr)
                pend_tr = new_tr
                if not noret:
                    for fn, blk_ in csched.get(s_i, []):
                        fn(blk_)
                if not noret and s_i == 3:
                    ring.release((t, l, 'w_r', 0))
                    ring.release((t, l, 'w_r', 1))
                if not noret and s_i == 7:
                    ring.release((t, l, 'w_r', 2))
                    ring.release((t, l, 'w_r', 3))
            if pend is not None:
                att_back(*pend)
            for tr in pend_tr:
                rot_back(*tr)
            state['stage'] = 'att'
            if t + 1 < nt:
                A('act', lambda e: e.activation(out=akP[l][:], in_=akC[:, :, (NB - 1) * 128:NB * 128], func=AF.Copy),
                  [('akC', 0), ('akC', 1)], [('akP', l)])
                A('act', lambda e: e.activation(out=avP[l][:], in_=avC[:, NB - 1, :], func=AF.Copy),
                  ['avC'], [('avP', l)])

            offs = {}
            for u in range(2):
                offs[u] = ring.get((t, l, 'w_o', u))
            cat_reads = {}
            for cc in range(8):
                if cc < 4:
                    cat_reads[cc] = [('big', S_CAT + cc, 0), ('big', S_CAT + cc, 1)]
                else:
                    cat_reads[cc] = [('big', S_CAT + cc, blk) for blk in range(NB)]

            def wo_half(dc, half):
                state['stage'] = 'wo'
                b = nb()
                for cc in range(half * 4, half * 4 + 4):
                    u, ccl = cc // 4, cc % 4
                    key = (t, l, 'w_o', u)
                    a = ccl * 1024 + dc * 128
                    A('pe', lambda e, b=b, a=a, cc=cc, u=u: e.matmul(psf[b][:], ring_t[:, offs[u] + a:offs[u] + a + 128], big[:, S_CAT + cc, :],
                                                                       start=(cc % 4 == 0), stop=(cc % 4 == 3)),
                      ring.pages(key, a, a + 128) + cat_reads[cc], [('ps', b)])
                return b

            def wo_add(dc, b):
                state['stage'] = 'wo'
                A('dve', lambda e, b=b, dc=dc: e.tensor_tensor(out=hT[:, dc, :], in0=psf[b][:], in1=hT[:, dc, :], op=ALU.add),
                  [('ps', b), ('hT', dc)], [('hT', dc)])

            for s_i in range(8, 12):
                dcs = [2 * (s_i - 8), 2 * (s_i - 8) + 1]
                bs_ = [wo_half(dc, 0) for dc in dcs]
                if not noret:
                    for fn, blk_ in csched.get(s_i, []):
                        fn(blk_)
                for dc, b_ in zip(dcs, bs_):
                    wo_add(dc, b_)
            if not noret and t + 1 < nt:
                state['stage'] = 'chunks'
                A('act', lambda e: e.activation(out=RbC[l][:], in_=Rst[l][:], func=AF.Copy), [('R', l)], [('RbC', l)])
            np_start()
            for dc in range(KC):
                b_ = wo_half(dc, 1)
                np_mm()
                wo_add(dc, b_)
                np_feed(dc)
            ring.release((t, l, 'w_o', 0))
            ring.release((t, l, 'w_o', 1))
            state['stage'] = 'post'

        for t in range(nt):
            ts = slice(t * TS, (t + 1) * TS)
            rec.add('sp', lambda e, ts=ts: e.dma_start(out=hT[:], in_=xT[:, :, ts].rearrange("k p s -> p k s")),
                    writes=[('hT', k) for k in range(KC)], dma_sem='x')
            np_start()
            for k_ in range(KC):
                np_feed(k_)
            for l in range(nl):
                if 'noffn' not in flags:
                    ffn(t, l, 0)
                else:
                    for u in range(11):
                        ring.get((t, l, 'w_gu1', u)); ring.release((t, l, 'w_gu1', u))
                    for u in range(8):
                        ring.get((t, l, 'w_d1', u)); ring.release((t, l, 'w_d1', u))
                if 'nomix' not in flags:
                    mixer(t, l)
                    if dbg is not None and t == 0 and l == 0:
                        allr = []
                        for s_ in range(40):
                            allr += br(s_)
                        rec.add('sp', lambda e: e.dma_start(out=dbg.rearrange("p (a b) -> p a b", a=40), in_=big[:, 0:40, :]), reads=allr, writes=['dbg'], dma_sem='dbg')
                else:
                    for name, nu in (('w_aq', 1), ('w_ak', 1), ('w_av', 1), ('w_r', 4), ('w_o', 2)):
                        for u in range(nu):
                            ring.get((t, l, name, u)); ring.release((t, l, name, u))
                if 'noffn' not in flags:
                    ffn(t, l, 1)
                else:
                    for u in range(11):
                        ring.get((t, l, 'w_gu2', u)); ring.release((t, l, 'w_gu2', u))
                    for u in range(8):
                        ring.get((t, l, 'w_d2', u)); ring.release((t, l, 'w_d2', u))
            np_finish(None, 0, dst_is_h=True)
            rec.add('sp', lambda e, ts=ts: e.dma_start(out=yT[:, :, ts].rearrange("k p s -> p k s"), in_=hT[:]),
                    reads=[('hT', k) for k in range(KC)], writes=[('yT', t), 'ysem'], dma_sem='y')
        rec.add('sp', None, reads=[('yT', t) for t in range(nt)] + (['dbg'] if dbg is not None else []))
        if use_scratch:
            rec.add('sp', None, reads=[('ssem', j) for j in range(4)])

        semkeys = rec.finalize()
        sems = {k: es.enter_context(nc.semaphore(f"s{n}")) for n, k in enumerate(semkeys)}
        with nc.Block() as block:
            @block.sync
            def _(e):
                rec.emit_engine('sp', e, sems)

            @block.gpsimd
            def _(e):
                rec.emit_engine('pool', e, sems)

            @block.tensor
            def _(e):
                rec.emit_engine('pe', e, sems)

            @block.vector
            def _(e):
                rec.emit_engine('dve', e, sems)

            @block.scalar
            def _(e):
                rec.emit_engine('act', e, sems)
    return nc


def make_in_maps(inputs, nl, nt, ncore):
    S = nt * TS
    w = prep_weights(inputs, nl)
    tabs = const_tables(nt)
    shared = dict(w)
    for k, v in tabs.items():
        if k != 'dec':
            shared[k] = v
    x = np.asarray(inputs['x'], dtype=np.float32)
    maps = []
    for c in range(ncore):
        m = dict(shared)
        m['xT'] = _c(x[c, :S, :].T).reshape(KC, 128, S)
        maps.append(m)
    return maps


def run(inputs, nl=NLAYER, nt=SEQ // TS, ncore=NCORE, flags=(), trace=False):
    inputs = {k: np.asarray(v) for k, v in inputs.items()}
    nc = build(nl, nt, flags=flags)
    maps = make_in_maps(inputs, nl, nt, ncore)
    res = run_bass_kernel_spmd(nc, maps, core_ids=list(range(ncore)), trace=trace)
    S = nt * TS
    out = np.stack([r['yT'].reshape(D, S).T for r in res.results], axis=0)
    return np.ascontiguousarray(out, dtype=np.float32), res


def kernel(**inputs):
    out, _ = run(inputs)
    return out
```

```python
import numpy as np
from contextlib import ExitStack
import concourse.bass as bass
import concourse.mybir as mybir
from concourse.bass_utils import run_bass_kernel_spmd

F32 = mybir.dt.float32
BF16 = mybir.dt.bfloat16
AF = mybir.ActivationFunctionType
ALU = mybir.AluOpType
AX = mybir.AxisListType

D = 1024
KC = 8
FF = 2816
FC = 22
TS = 512
NB = 4
NLAYER = 4
SEQ = 2048
NCORE = 8
NORM_EPS = 1e-6
GN_EPS = 1e-5
PAGE = 512
RING_CAP = 20480
NWSEM = 12
MASK_NEG = -2400.0


class Rec:
    def __init__(self):
        self.ops = []
        self.last_w = {}
        self.readers = {}

    def add(self, eng, emit, reads=(), writes=(), dma_sem=None):
        idx = len(self.ops)
        deps = set()
        for r in reads:
            w = self.last_w.get(r)
            if w is not None:
                deps.add(w)
        for r in writes:
            w = self.last_w.get(r)
            if w is not None:
                deps.add(w)
            rs = self.readers.get(r)
            if rs:
                deps.update(rs)
        for r in reads:
            self.readers.setdefault(r, []).append(idx)
        for r in writes:
            self.last_w[r] = idx
            self.readers[r] = []
        self.ops.append(dict(eng=eng, emit=emit, deps=deps, dma_sem=dma_sem, sig=False, val=None, semkey=None))
        return idx

    def finalize(self):
        ops = self.ops
        for op in ops:
            keep = set()
            for d in op['deps']:
                dop = ops[d]
                if (dop['dma_sem'] is None and op['dma_sem'] is None
                        and dop['eng'] == 'pe' and op['eng'] == 'pe'):
                    continue
                keep.add(d)
                dop['sig'] = True
            op['deps'] = keep
        cnt = {}
        for op in ops:
            if op['dma_sem'] is not None:
                key = ('dma', op['dma_sem'])
                cnt[key] = cnt.get(key, 0) + 16
                op['val'] = cnt[key]
                op['semkey'] = key
                op['sig'] = True
            elif op['sig']:
                key = ('eng', op['eng'])
                cnt[key] = cnt.get(key, 0) + 1
                op['val'] = cnt[key]
                op['semkey'] = key
        return sorted({op['semkey'] for op in ops if op['sig']}, key=str)

    def emit_engine(self, engname, eng, sems):
        waited = {}
        for op in self.ops:
            if op['eng'] != engname:
                continue
            need = {}
            for d in op['deps']:
                dop = self.ops[d]
                k = dop['semkey']
                if dop['val'] > need.get(k, 0):
                    need[k] = dop['val']
            for k, v in need.items():
                if v > waited.get(k, 0):
                    eng.wait_ge(sems[k], v)
                    waited[k] = v
            if op['emit'] is not None:
                ins = op['emit'](eng)
                if op['sig']:
                    ins.then_inc(sems[op['semkey']], 16 if op['dma_sem'] is not None else 1)


class Ring:
    def __init__(self, rec, ring_t, cap, sched, use_scratch):
        self.rec = rec
        self.t = ring_t
        self.cap = cap
        self.sched = sched
        self.next_load = 0
        self.next_use = 0
        self.live = []
        self.head = 0
        self.off = {}
        self.ndma = 0
        self.nscr = 0
        self.use_scratch = use_scratch

    def _try_alloc(self, size):
        if not self.live:
            self.head = size
            return 0
        tail = self.live[0][1]
        if self.head > tail or (self.head == tail and False):
            if self.head + size <= self.cap:
                off = self.head
            elif size < tail:
                off = 0
            else:
                return None
        else:
            if self.head + size < tail:
                off = self.head
            else:
                return None
        self.head = off + size
        return off

    def pump(self):
        while self.next_load < len(self.sched):
            u = self.sched[self.next_load]
            size = u['size']
            asz = ((size + PAGE - 1) // PAGE) * PAGE
            off = self._try_alloc(asz)
            if off is None:
                return
            self.live.append((u['key'], off, asz))
            self.off[u['key']] = off
            pages = [('rg', p) for p in range(off // PAGE, (off + asz) // PAGE)]
            si = self.ndma % NWSEM
            self.ndma += 1
            dst = self.t[:, off:off + size]
            if u['first'] or not self.use_scratch:
                src = u['src32']
                self.rec.add('pool', lambda e, dst=dst, src=src: e.dma_start(out=dst, in_=src),
                             writes=pages + [('wsem', si)], dma_sem=f'w{si}')
                if self.use_scratch:
                    sj = self.nscr % 4
                    self.nscr += 1
                    scr = u['scr']
                    self.rec.add('sp', lambda e, dst=dst, scr=scr: e.dma_start(out=scr, in_=dst),
                                 reads=pages, writes=[('scr', u['skey']), ('ssem', sj)], dma_sem=f'sc{sj}')
            else:
                scr = u['scr']
                self.rec.add('pool', lambda e, dst=dst, scr=scr: e.dma_start(out=dst, in_=scr),
                             reads=[('scr', u['skey'])], writes=pages + [('wsem', si)], dma_sem=f'w{si}')
            self.next_load += 1

    def get(self, key):
        u = self.sched[self.next_use]
        assert u['key'] == key, (u['key'], key)
        assert self.next_load > self.next_use, "ring too small: unit not loaded " + str(key)
        self.next_use += 1
        return self.off[key]

    def pages(self, key, a, b):
        off = self.off[key]
        return [('rg', p) for p in range((off + a) // PAGE, (off + b - 1) // PAGE + 1)]

    def release(self, key):
        assert self.live[0][0] == key, (self.live[0][0], key)
        self.live.pop(0)
        self.pump()


def _c(a):
    return np.ascontiguousarray(a, dtype=np.float32)


def prep_weights(inp, nl):
    L = nl
    out = {}

    def gu(wg, wu):
        g = wg[:L].reshape(L, 8, 128, 11, 2, 128)
        u_ = wu[:L].reshape(L, 8, 128, 11, 2, 128)
        st = np.stack([g, u_], axis=0)
        st = st.transpose(1, 4, 3, 5, 0, 2, 6)
        return _c(st).reshape(L, 11, 128, 4096)

    def dn(wd):
        d = wd[:L].reshape(L, 22, 128, 8, 128)
        d = d.transpose(0, 3, 2, 1, 4)
        return _c(d).reshape(L, 8, 128, 2816)

    out['w_gu1'] = gu(inp['ffn1_w_gate'], inp['ffn1_w_up'])
    out['w_d1'] = dn(inp['ffn1_w_down'])
    out['w_gu2'] = gu(inp['ffn2_w_gate'], inp['ffn2_w_up'])
    out['w_d2'] = dn(inp['ffn2_w_down'])
    win = inp['w_in'][:L].reshape(L, 8, 128, 2816)
    aq = win[..., 0:512].reshape(L, 8, 128, 8, 64)
    aqt = np.concatenate([aq[:, :, :, 0:4, :], aq[:, :, :, 4:8, :]], axis=-1)
    out['w_aq'] = _c(aqt.transpose(0, 2, 1, 3, 4)).reshape(L, 1, 128, 4096)
    ak = win[..., 512:640]
    akp = np.zeros((L, 8, 128, 2, 128), np.float32)
    akp[:, :, :, 0, 0:64] = ak[..., 0:64]
    akp[:, :, :, 1, 64:128] = ak[..., 64:128]
    out['w_ak'] = _c(akp.transpose(0, 2, 1, 3, 4)).reshape(L, 1, 128, 2048)
    out['w_av'] = _c(win[..., 640:768].transpose(0, 2, 1, 3)).reshape(L, 1, 128, 1024)
    r = win[..., 768:2816].reshape(L, 8, 128, 4, 512)
    out['w_r'] = _c(r.transpose(0, 3, 2, 1, 4)).reshape(L, 4, 128, 4096)
    wo = inp['w_out'][:L]
    wa = wo[:, 0:512].reshape(L, 8, 64, 1024)
    wat = np.concatenate([wa[:, 0:4], wa[:, 4:8]], axis=2)
    wr = wo[:, 512:1024].reshape(L, 4, 128, 1024)
    woc = np.concatenate([wat, wr], axis=1)
    woc = woc.reshape(L, 2, 4, 128, 1024).transpose(0, 1, 3, 2, 4)
    out['w_o'] = _c(woc).reshape(L, 2, 128, 4096)
    nrm = np.stack([inp['ffn1_norm'][:L], inp['mix_norm'][:L], inp['ffn2_norm'][:L]], axis=1)
    nrm = nrm.reshape(L, 3, 8, 128).transpose(3, 0, 1, 2)
    fin = inp['final_norm'].reshape(8, 128).T
    out['c_norm'] = _c(np.concatenate([nrm.reshape(128, L * 24), fin], axis=1))
    out['c_gnw'] = _c(inp['ret_gn_w'][:L].reshape(L * 4, 128).T)
    out['c_sink'] = _c(np.broadcast_to(inp['attn_sinks'][:L].reshape(1, L * 8), (128, L * 8)))
    return out


def const_tables(nt):
    S = nt * TS
    pos = np.arange(S, dtype=np.float32)
    inv_freq = (10000.0 ** (-np.arange(0, 128, 2, dtype=np.float32) / 128.0)).astype(np.float32)
    ang = pos[:, None] * inv_freq[None, :]
    cos = np.cos(ang).astype(np.float32).reshape(S // 128, 128, 64).transpose(1, 0, 2)
    sin = np.sin(ang).astype(np.float32).reshape(S // 128, 128, 64).transpose(1, 0, 2)
    t = {}
    t['c_cos'] = _c(cos).reshape(128, (S // 128) * 64)
    t['c_sin'] = _c(sin).reshape(128, (S // 128) * 64)
    H = 4
    lg = np.log(1.0 - 2.0 ** (-5.0 - np.arange(H, dtype=np.float32))).astype(np.float32)
    idx = np.arange(128, dtype=np.float32)
    dif = idx[:, None] - idx[None, :]
    dm = np.where(dif[None] >= 0, np.exp(np.maximum(dif, 0.0)[None] * lg[:, None, None]), 0.0)
    dmt = dm.transpose(2, 0, 1) * np.float32(128.0 ** -0.5)
    t['c_dmat'] = _c(dmt).reshape(128, 512)
    zeta = np.exp((127.0 - idx)[None, :] * lg[:, None]) * np.float32(128.0 ** -0.5)
    xi = np.exp((idx + 1.0)[None, :] * lg[:, None])
    t['c_xi'] = _c(np.broadcast_to(xi.reshape(1, 512), (128, 512)))
    dec = np.exp(128.0 * lg)
    t['c_zs'] = _c(zeta.T)
    t['dec'] = [float(v) for v in dec.astype(np.float32)]
    kk = np.arange(128)[:, None]
    qq = np.arange(128)[None, :]
    m0 = np.where(kk <= qq, 0.0, MASK_NEG)
    m1 = np.where(qq < kk, 0.0, MASK_NEG)
    t['c_mask'] = _c(np.stack([m0, m1], axis=1)).reshape(128, 256)
    t['c_ident'] = _c(np.eye(128))
    return t


WSPEC = [('w_gu1', 11, 4096), ('w_d1', 8, 2816), ('w_aq', 1, 4096), ('w_ak', 1, 2048), ('w_av', 1, 1024),
         ('w_r', 4, 4096), ('w_o', 2, 4096), ('w_gu2', 11, 4096), ('w_d2', 8, 2816)]


def build(nl, nt, first_layer_only_ffn=False, flags=()):
    nc = bass.Bass("TRN2", target_bir_lowering=False)
    S = nt * TS
    NBLK = S // 128
    tabs = const_tables(nt)
    dec = tabs['dec']
    dram = {}
    for name, nu, sz in WSPEC:
        dram[name] = nc.dram_tensor(name, [nl, nu, 128, sz], F32, kind="ExternalInput").ap()
    use_scratch = nt > 1
    scr = {}
    if use_scratch:
        for name, nu, sz in WSPEC:
            scr[name] = nc.dram_tensor("s_" + name, [nl, nu, 128, sz], BF16, kind="Internal").ap()
    cshape = {'c_norm': nl * 24 + 8, 'c_gnw': nl * 4, 'c_sink': nl * 8, 'c_cos': NBLK * 64, 'c_sin': NBLK * 64,
              'c_dmat': 512, 'c_xi': 512, 'c_zs': 4, 'c_mask': 256, 'c_ident': 128}
    for name, w in cshape.items():
        dram[name] = nc.dram_tensor(name, [128, w], F32, kind="ExternalInput").ap()
    xT = nc.dram_tensor("xT", [KC, 128, S], F32, kind="ExternalInput").ap()
    yT = nc.dram_tensor("yT", [KC, 128, S], F32, kind="ExternalOutput").ap()
    dbg = nc.dram_tensor("dbg", [128, 40 * TS], BF16, kind="ExternalOutput").ap() if 'dbg' in flags else None

    sched = []
    for t in range(nt):
        for l in range(nl):
            order = [('w_gu1', 11), ('w_d1', 8), ('w_aq', 1), ('w_ak', 1), ('w_av', 1), ('w_r', 4), ('w_o', 2),
                     ('w_gu2', 11), ('w_d2', 8)]
            for name, nu in order:
                sz = dict((n, s) for n, _, s in WSPEC)[name]
                for u in range(nu):
                    sched.append(dict(key=(t, l, name, u), skey=(l, name, u), size=sz, first=(t == 0),
                                      src32=dram[name][l, u], scr=(scr[name][l, u] if use_scratch else None)))

    with ExitStack() as es:
        def sb(name, shape, dt):
            return es.enter_context(nc.sbuf_tensor(name, shape, dt))

        rec = Rec()
        ring_t = sb("ring", [128, RING_CAP], BF16)
        ring = Ring(rec, ring_t, RING_CAP, sched, use_scratch)
        hT = sb("hT", [128, KC, TS], F32)
        nT = sb("nT", [128, KC, TS], BF16)
        big = sb("big", [128, 48, TS], BF16)
        sgt = [sb(f"sg{i}", [128, TS], F32) for i in range(2)]
        rstd = sb("rstd", [128, TS], F32)
        cn = sb("cn", [128, cshape['c_norm']], F32)
        gnw = sb("gnw", [128, nl * 4], F32)
        sink = sb("sink", [128, nl * 8], F32)
        sinke = sb("sinke", [128, nl * 8], F32)
        cos_t = sb("cos", [128, NBLK * 64], F32)
        sin_t = sb("sin", [128, NBLK * 64], F32)
        dmat = sb("dmat", [128, 512], F32)
        xi_t = sb("xi", [128, 512], F32)
        zs_t = sb("zs", [128, 4], F32)
        maskf = sb("maskf", [128, 256], F32)
        maskb = sb("maskb", [128, 256], BF16)
        identf = sb("identf", [128, 128], F32)
        ident = sb("ident", [128, 128], BF16)
        onesd = sb("onesd", [128, 128], BF16)
        ones = sb("ones", [128, 128], BF16)
        akP = [sb(f"akP{l}", [128, 2, 128], BF16) for l in range(nl)]
        avP = [sb(f"avP{l}", [128, 128], BF16) for l in range(nl)]
        Rst = [sb(f"R{l}", [128, 512], F32) for l in range(nl)]
        RbC = [sb(f"RbC{l}", [128, 512], BF16) for l in range(nl)]
        Rbt = [sb(f"Rbt{i}", [128, 512], BF16) for i in range(3)]
        akC = sb("akC", [128, 2, TS], BF16)
        skb = sb("skb", [128, 2, TS], BF16)
        avC = sb("avC", [128, NB, 128], BF16)
        rt = [sb(f"rt{i}", [128, 256], F32) for i in range(8)]
        Et = [sb(f"E{i}", [128, TS], BF16) for i in range(4)]
        Sd = [sb(f"Sd{i}", [128, TS], BF16) for i in range(4)]
        gs = sb("gs", [128, NB, TS], F32)
        tmpf = [sb(f"tmpf{i}", [128, TS], F32) for i in range(3)]
        ycb = [sb(f"ycb{i}", [128, TS], F32) for i in range(2)]
        stat = [sb(f"stat{i}", [128, 16], F32) for i in range(2)]
        rtok = [sb(f"rtok{i}", [128, TS], BF16) for i in range(2)]
        NPS = 5
        psf = [es.enter_context(nc.psum_tensor(f"ps{i}", [128, TS], F32)) for i in range(NPS + 1)]
        psbs = [es.enter_context(nc.psum_tensor(f"psb{i}", [128, 2 * TS], BF16)) for i in range(2)]

        state = dict(bank=0, cnt=0)
        def slot(s0, n):
            return big[:, s0:s0 + n, :]
        aT = lambda fc: big[:, fc, :]
        S_AQ, S_QTOK, S_KTOK, S_QT, S_QX, S_KT, S_VT, S_VZ, S_CAT, S_SQ = 0, 4, 8, 12, 16, 20, 24, 28, 32, 40
        SUBS = {}
        for s_ in range(4, 12):
            SUBS[s_] = ['lo', 'hi']
        for s_ in range(12, 24):
            SUBS[s_] = [0, 1, 2, 3]
        for s_ in range(32, 36):
            SUBS[s_] = [0, 1]
        for s_ in range(36, 40):
            SUBS[s_] = [0, 1, 2, 3]

        def bw(slot_, sub=None):
            if sub is None:
                return [('big', slot_)] + [('big', slot_, x) for x in SUBS.get(slot_, [])]
            return [('big', slot_), ('big', slot_, sub)]

        def br(slot_, sub=None):
            if sub is None:
                if slot_ in SUBS:
                    return [('big', slot_, x) for x in SUBS[slot_]]
                return [('big', slot_)]
            return [('big', slot_, sub)]


        def nb():
            b = state['bank']
            state['bank'] = (b + 1) % NPS
            return b

        disabled = {f[4:] for f in flags if f.startswith('off_')}
        state['stage'] = 'init'

        def A(eng, fn, reads=(), writes=()):
            if state['stage'] in disabled:
                return
            reads = list(reads)
            writes = list(writes)
            for r in reads:
                if isinstance(r, tuple) and r[0] in ('ps', 'psb') and r not in writes:
                    writes.append(r)
            rec.add(eng, fn, reads=reads, writes=writes)

        cl = [('c_norm', cn), ('c_gnw', gnw), ('c_sink', sink), ('c_cos', cos_t), ('c_sin', sin_t), ('c_dmat', dmat),
              ('c_xi', xi_t), ('c_zs', zs_t), ('c_mask', maskf), ('c_ident', identf)]
        for i, (name, tt) in enumerate(cl):
            rec.add('sp', lambda e, tt=tt, name=name: e.dma_start(out=tt[:], in_=dram[name]),
                    writes=[('c', name)], dma_sem=f'c{i}')
        A('dve', lambda e: e.tensor_copy(out=maskb[:], in_=maskf[:]), [('c', 'c_mask')], ['maskb'])
        A('dve', lambda e: e.tensor_copy(out=ident[:], in_=identf[:]), [('c', 'c_ident')], ['ident'])
        A('dve', lambda e: e.memset(onesd[:], 1.0 / D), [], ['onesd'])
        A('dve', lambda e: e.memset(ones[:], 1.0), [], ['ones'])
        A('act', lambda e: e.activation(out=sinke[:], in_=sink[:], func=AF.Exp), [('c', 'c_sink')], ['sinke'])
        for l in range(nl):
            A('dve', lambda e, l=l: e.memset(Rst[l][:], 0.0), [], [('R', l)])
            A('dve', lambda e, l=l: e.memset(RbC[l][:], 0.0), [], [('RbC', l)])
        ring.pump()

        npst = dict(pending=[], nmm=0)
        NB_STAT = NPS

        def np_start():
            npst['pending'] = []
            npst['nmm'] = 0

        def np_feed(k):
            A('act', lambda e, k=k: e.activation(out=big[:, S_SQ + k, :], in_=hT[:, k, :], func=AF.Square),
              [('hT', k)], bw(S_SQ + k))
            npst['pending'].append(k)

        def np_mm():
            for k in npst['pending']:
                i = npst['nmm']
                A('pe', lambda e, k=k, i=i: e.matmul(psf[NB_STAT][:], onesd[:], big[:, S_SQ + k, :], start=(i == 0), stop=(i == KC - 1)),
                  [('big', S_SQ + k), 'onesd'], [('ps', NB_STAT)])
                npst['nmm'] += 1
            npst['pending'] = []

        def np_finish(l, which, dst_is_h=False):
            cbase = (l * 24 + which * 8) if l is not None else nl * 24
            np_mm()
            assert npst['nmm'] == KC
            b = NB_STAT
            A('act', lambda e, b=b: e.activation(out=rstd[:], in_=psf[b][:], func=AF.Ln, bias=NORM_EPS, scale=1.0),
              [('ps', b)], ['rstd'])
            A('act', lambda e: e.activation(out=rstd[:], in_=rstd[:], func=AF.Exp, scale=-0.5), ['rstd'], ['rstd'])
            for k in range(KC):
                if dst_is_h:
                    A('dve', lambda e, k=k: e.scalar_tensor_tensor(out=hT[:, k, :], in0=hT[:, k, :], scalar=cn[:, cbase + k:cbase + k + 1],
                                                                    in1=rstd[:], op0=ALU.mult, op1=ALU.mult),
                      [('hT', k), 'rstd', ('c', 'c_norm')], [('hT', k)])
                else:
                    A('dve', lambda e, k=k: e.scalar_tensor_tensor(out=nT[:, k, :], in0=hT[:, k, :], scalar=cn[:, cbase + k:cbase + k + 1],
                                                                    in1=rstd[:], op0=ALU.mult, op1=ALU.mult),
                      [('hT', k), 'rstd', ('c', 'c_norm')], [('nT', k)])

        def ffn(t, l, which):
            gname = 'w_gu1' if which == 0 else 'w_gu2'
            dname = 'w_d1' if which == 0 else 'w_d2'
            np_finish(l, 0 if which == 0 else 2)
            for u in range(11):
                key = (t, l, gname, u)
                off = ring.get(key)
                for fcl in range(2):
                    fc = 2 * u + fcl
                    bg, bu = nb(), nb()
                    for gi, b in ((0, bg), (1, bu)):
                        for k in range(KC):
                            a = ((fcl * 2 + gi) * 8 + k) * 128
                            A('pe', lambda e, b=b, a=a, k=k, off=off: e.matmul(psf[b][:], ring_t[:, off + a:off + a + 128], nT[:, k, :],
                                                                                 start=(k == 0), stop=(k == KC - 1)),
                              ring.pages(key, a, a + 128) + [('nT', k)], [('ps', b)])
                    sg = sgt[fc % 2]
                    A('act', lambda e, sg=sg, bg=bg: e.activation(out=sg[:], in_=psf[bg][:], func=AF.Silu),
                      [('ps', bg)], [('sg', fc % 2)])
                    A('dve', lambda e, sg=sg, bu=bu, fc=fc: e.tensor_tensor(out=big[:, fc, :], in0=psf[bu][:], in1=sg[:], op=ALU.mult),
                      [('ps', bu), ('sg', fc % 2)], bw(fc))
                ring.release(key)
            np_start()
            for dc in range(KC):
                key = (t, l, dname, dc)
                off = ring.get(key)
                b = nb()
                for fc in range(FC):
                    a = fc * 128
                    A('pe', lambda e, b=b, a=a, fc=fc, off=off: e.matmul(psf[b][:], ring_t[:, off + a:off + a + 128], big[:, fc, :],
                                                                           start=(fc == 0), stop=(fc == FC - 1)),
                      ring.pages(key, a, a + 128) + [('big', fc)], [('ps', b)])
                np_mm()
                A('dve', lambda e, b=b, dc=dc: e.scalar_tensor_tensor(out=hT[:, dc, :], in0=psf[b][:], scalar=0.5, in1=hT[:, dc, :],
                                                                       op0=ALU.mult, op1=ALU.add),
                  [('ps', b), ('hT', dc)], [('hT', dc)])
                np_feed(dc)
                ring.release(key)

        def mixer(t, l):
            state['stage'] = 'mnorm'
            np_finish(l, 1)
            G0 = t * NB
            noret = 'noret' in flags
            state['stage'] = 'aq'
            key = (t, l, 'w_aq', 0)
            off = ring.get(key)
            for j in range(4):
                b = nb()
                for k in range(KC):
                    a = (k * 4 + j) * 128
                    A('pe', lambda e, b=b, a=a, k=k, off=off: e.matmul(psf[b][:], ring_t[:, off + a:off + a + 128], nT[:, k, :],
                                                                         start=(k == 0), stop=(k == KC - 1)),
                      ring.pages(key, a, a + 128) + [('nT', k)], [('ps', b)])
                A('act', lambda e, b=b, j=j: e.activation(out=big[:, S_AQ + j, :], in_=psf[b][:], func=AF.Copy),
                  [('ps', b)], bw(S_AQ + j))
            ring.release(key)
            state['stage'] = 'ak'
            key = (t, l, 'w_ak', 0)
            off = ring.get(key)
            for g in range(2):
                b = nb()
                for k in range(KC):
                    a = (k * 2 + g) * 128
                    A('pe', lambda e, b=b, a=a, k=k, off=off: e.matmul(psf[b][:], ring_t[:, off + a:off + a + 128], nT[:, k, :],
                                                                         start=(k == 0), stop=(k == KC - 1)),
                      ring.pages(key, a, a + 128) + [('nT', k)], [('ps', b)])
                A('act', lambda e, b=b, g=g: e.activation(out=akC[:, g, :], in_=psf[b][:], func=AF.Copy),
                  [('ps', b)], [('akC', g)])
            ring.release(key)
            state['stage'] = 'av'
            key = (t, l, 'w_av', 0)
            off = ring.get(key)
            b = nb()
            for blk in range(NB):
                for k in range(KC):
                    a = k * 128
                    A('pe', lambda e, b=b, a=a, k=k, blk=blk, off=off: e.matmul(psf[b][:, blk * 128:(blk + 1) * 128], nT[:, k, blk * 128:(blk + 1) * 128],
                                                                                  ring_t[:, off + a:off + a + 128], start=(k == 0), stop=(k == KC - 1)),
                      ring.pages(key, a, a + 128) + [('nT', k)], [('ps', b)])
            A('act', lambda e, b=b: e.activation(out=avC[:].rearrange("p a b -> p (a b)"), in_=psf[b][:], func=AF.Copy),
              [('ps', b)], ['avC'])
            ring.release(key)

            state['stage'] = 'att'
            items = [(g, qb) for qb in range(NB) for g in range(2)]
            for g_ in range(2):
                A('dve', lambda e, g_=g_: e.tensor_scalar(out=skb[:, g_, :].rearrange("p (a b) -> p a b", a=4),
                                                          in0=sinke[:, l * 8 + 4 * g_:l * 8 + 4 * g_ + 4].unsqueeze(2).to_broadcast([128, 4, 128]),
                                                          scalar1=1.0 / 128, scalar2=None, op0=ALU.mult),
                  ['sinke'], [('skb', g_)])

            def att_front(g, qb, ei):
                state['stage'] = 'att'
                G = G0 + qb
                kbs = []
                if G > 0:
                    kbs.append(1)
                kbs.append(0)
                outs = []
                for kind in kbs:
                    b = nb()
                    if kind == 0:
                        lk = akC[:, g, qb * 128:(qb + 1) * 128]
                        rk = [('akC', g)]
                    elif qb == 0:
                        lk = akP[l][:, g, :]
                        rk = [('akP', l)]
                    else:
                        lk = akC[:, g, (qb - 1) * 128:qb * 128]
                        rk = [('akC', g)]
                    rq = big[:, S_AQ:S_AQ + 4, qb * 128:(qb + 1) * 128]
                    A('pe', lambda e, b=b, lk=lk, rq=rq: e.matmul(psf[b][:], lk, rq, start=True, stop=False),
                      rk + [('big', S_AQ + j) for j in range(4)], [('ps', b)])
                    mb = maskb[:, kind * 128:(kind + 1) * 128].unsqueeze(1).to_broadcast([128, 4, 128])
                    A('pe', lambda e, b=b, mb=mb: e.matmul(psf[b][:], ident[:], mb, start=False, stop=True),
                      ['maskb', 'ident'], [('ps', b)])
                    ee = ei[0] % 4
                    ei[0] += 1
                    A('act', lambda e, b=b, ee=ee: e.activation(out=Et[ee][:], in_=psf[b][:], func=AF.Exp, scale=0.125),
                      [('ps', b)], [('E', ee)])
                    outs.append((kind, ee))
                return outs

            def att_back(g, qb, outs):
                state['stage'] = 'att'
                bpv, bden = nb(), nb()
                n = len(outs)
                for i, (kind, ee) in enumerate(outs):
                    if kind == 0:
                        lv = avC[:, qb, :]
                        rv = ['avC']
                    elif qb == 0:
                        lv = avP[l][:]
                        rv = [('avP', l)]
                    else:
                        lv = avC[:, qb - 1, :]
                        rv = ['avC']
                    A('pe', lambda e, lv=lv, ee=ee, i=i, n=n, bpv=bpv: e.matmul(psf[bpv][:], lv, Et[ee][:], start=(i == 0), stop=(i == n - 1)),
                      rv + [('E', ee)], [('ps', bpv)])
                for i, (kind, ee) in enumerate(outs):
                    A('pe', lambda e, ee=ee, i=i, n=n, bden=bden: e.matmul(psf[bden][:], ones[:], Et[ee][:], start=(i == 0), stop=False),
                      ['ones', ('E', ee)], [('ps', bden)])
                A('pe', lambda e, bden=bden: e.matmul(psf[bden][:], ones[:], skb[:, g, :], start=False, stop=True),
                  ['ones', ('skb', g)], [('ps', bden)])
                r0, r1 = g * 64, (g + 1) * 64
                tf = tmpf[(qb * 2 + g) % 2]
                tfk = ('tmpf', (qb * 2 + g) % 2)
                A('act', lambda e, tf=tf, bden=bden: e.activation(out=tf[r0:r1, :], in_=psf[bden][r0:r1, :], func=AF.Ln), [('ps', bden)], [tfk])
                A('act', lambda e, tf=tf: e.activation(out=tf[r0:r1, :], in_=tf[r0:r1, :], func=AF.Exp, scale=-1.0), [tfk], [tfk])
                A('dve', lambda e, tf=tf, bpv=bpv: e.tensor_tensor(out=big[r0:r1, S_CAT:S_CAT + 4, qb * 128:(qb + 1) * 128],
                                                                    in0=psf[bpv][r0:r1, :].rearrange("p (a b) -> p a b", a=4),
                                                                    in1=tf[r0:r1, :].rearrange("p (a b) -> p a b", a=4), op=ALU.mult),
                  [('ps', bpv), tfk], sum([bw(S_CAT + j, g) for j in range(4)], []))

            def tm_block(u, off, blk):
                state['stage'] = f'rp{u}'
                key = (t, l, 'w_r', u)
                b = nb()
                for k in range(KC):
                    a = k * 512
                    A('pe', lambda e, b=b, a=a, k=k, blk=blk, off=off: e.matmul(psf[b][:], nT[:, k, blk * 128:(blk + 1) * 128],
                                                                                  ring_t[:, off + a:off + a + 512], start=(k == 0), stop=(k == KC - 1)),
                      ring.pages(key, a, a + 512) + [('nT', k)], [('ps', b)])
                return b

            def rot_front(dst_slot, blk, b, ri):
                state['stage'] = 'rot'
                G = G0 + blk
                X = psf[b][:].rearrange("p (h two e) -> p h two e", h=4, two=2)
                x1, x2 = X[:, :, 0, :], X[:, :, 1, :]
                cb = cos_t[:, G * 64:(G + 1) * 64].unsqueeze(1).to_broadcast([128, 4, 64])
                sn = sin_t[:, G * 64:(G + 1) * 64].unsqueeze(1).to_broadcast([128, 4, 64])
                r3 = [rt[ri * 4 + i][:].rearrange("p (h e) -> p h e", h=4) for i in range(4)]
                rk_ = [('rt', ri * 4 + i) for i in range(4)]
                O = big[:, dst_slot + blk, :].rearrange("p (h two e) -> p h two e", h=4, two=2)
                cst = [('c', 'c_cos'), ('c', 'c_sin')]
                A('dve', lambda e: e.tensor_tensor(out=r3[0], in0=x1, in1=cb, op=ALU.mult), [('ps', b)] + cst, [rk_[0]])
                A('dve', lambda e: e.tensor_tensor(out=r3[1], in0=x2, in1=sn, op=ALU.mult), [('ps', b)] + cst, [rk_[1]])
                A('dve', lambda e: e.tensor_tensor(out=r3[2], in0=x1, in1=sn, op=ALU.mult), [('ps', b)] + cst, [rk_[2]])
                A('dve', lambda e: e.tensor_tensor(out=r3[3], in0=x2, in1=cb, op=ALU.mult), [('ps', b)] + cst, [rk_[3]])
                A('dve', lambda e: e.tensor_tensor(out=O[:, :, 0, :], in0=r3[0], in1=r3[1], op=ALU.subtract),
                  [rk_[0], rk_[1]], bw(dst_slot + blk, 'lo'))
                A('dve', lambda e: e.tensor_tensor(out=O[:, :, 1, :], in0=r3[2], in1=r3[3], op=ALU.add),
                  [rk_[2], rk_[3]], bw(dst_slot + blk, 'hi'))

            def rot_back(dst_slot, t_slot, x_slot, blk):
                state['stage'] = 'tr'
                hb = state['cnt'] % 2
                state['cnt'] += 1
                for h in range(4):
                    A('pe', lambda e, h=h, hb=hb: e.transpose(psbs[hb][:, h * 128:(h + 1) * 128],
                                                              big[:, dst_slot + blk, h * 128:(h + 1) * 128], ident[:]),
                      [('big', dst_slot + blk, 'lo'), ('big', dst_slot + blk, 'hi'), 'ident'], [('psb', hb)])
                pv = psbs[hb][:, 0:512].rearrange("p (h i) -> p h i", h=4)
                A('act', lambda e, pv=pv: e.activation(out=big[:, t_slot:t_slot + 4, blk * 128:(blk + 1) * 128], in_=pv, func=AF.Copy),
                  [('psb', hb)], sum([bw(t_slot + h, blk) for h in range(4)], []))
                if x_slot is not None:
                    A('dve', lambda e, pv=pv: e.tensor_tensor(out=big[:, x_slot:x_slot + 4, blk * 128:(blk + 1) * 128], in0=pv,
                                                               in1=xi_t[:].rearrange("p (h i) -> p h i", h=4), op=ALU.mult),
                      [('psb', hb), ('c', 'c_xi')], sum([bw(x_slot + h, blk) for h in range(4)], []))

            def v_post(blk, b):
                state['stage'] = 'vpost'
                A('act', lambda e: e.activation(out=big[:, S_VT + blk, :], in_=psf[b][:], func=AF.Copy),
                  [('ps', b)], bw(S_VT + blk))
                A('dve', lambda e: e.tensor_tensor(out=big[:, S_VZ + blk, :].rearrange("p (h v) -> p h v", h=4),
                                                    in0=psf[b][:].rearrange("p (h v) -> p h v", h=4),
                                                    in1=zs_t[:].unsqueeze(2).to_broadcast([128, 4, 128]), op=ALU.mult),
                  [('ps', b), ('c', 'c_zs')], bw(S_VZ + blk))

            def g_post(blk, b):
                state['stage'] = 'gpost'
                A('act', lambda e: e.activation(out=gs[:, blk, :], in_=psf[b][:], func=AF.Silu), [('ps', b)], [('gs', blk)])

            state['stage'] = 'chunks'
            def rb_before(blk):
                return (RbC[l], ('RbC', l)) if blk == 0 else (Rbt[blk - 1], ('Rbt', blk - 1))

            def ch_scores(blk):
                state['stage'] = 'chunks'
                cs = slice(blk * 128, (blk + 1) * 128)
                bs = nb()
                for h in range(4):
                    A('pe', lambda e, h=h, bs=bs: e.matmul(psf[bs][:, h * 128:(h + 1) * 128], big[:, S_KT + h, cs], big[:, S_QT + h, cs],
                                                           start=True, stop=True),
                      [('big', S_KT + h, blk), ('big', S_QT + h, blk)], [('ps', bs)])
                sd = Sd[blk]
                A('dve', lambda e, bs=bs, sd=sd: e.tensor_tensor(out=sd[:], in0=psf[bs][:], in1=dmat[:], op=ALU.mult),
                  [('ps', bs), ('c', 'c_dmat')], [('Sd', blk)])

            def ch_state(blk):
                state['stage'] = 'chunks'
                bk = nb()
                for h in range(4):
                    hs = slice(h * 128, (h + 1) * 128)
                    A('pe', lambda e, h=h, hs=hs, bk=bk: e.matmul(psf[bk][:, hs], big[:, S_KTOK + blk, hs], big[:, S_VZ + blk, hs], start=True, stop=True),
                      [('big', S_KTOK + blk, 'lo'), ('big', S_KTOK + blk, 'hi'), ('big', S_VZ + blk)], [('ps', bk)])
                for h in range(4):
                    hs = slice(h * 128, (h + 1) * 128)
                    A('dve', lambda e, h=h, hs=hs, bk=bk: e.scalar_tensor_tensor(out=Rst[l][:, hs], in0=Rst[l][:, hs], scalar=dec[h], in1=psf[bk][:, hs],
                                                                                   op0=ALU.mult, op1=ALU.add),
                      [('R', l), ('ps', bk)], [('R', l)])
                if blk < NB - 1:
                    A('pool', lambda e: e.tensor_copy(out=Rbt[blk][:], in_=Rst[l][:]), [('R', l)], [('Rbt', blk)])

            def ch_p1(blk):
                state['stage'] = 'chunks'
                n = G0 + blk
                cs = slice(blk * 128, (blk + 1) * 128)
                sd = Sd[blk]
                by = nb()
                rbp, rbk = rb_before(blk)
                for h in range(4):
                    hs = slice(h * 128, (h + 1) * 128)
                    A('pe', lambda e, h=h, hs=hs, by=by, sd=sd: e.matmul(psf[by][:, hs], sd[:, hs], big[:, S_VT + blk, hs], start=True, stop=False),
                      [('Sd', blk), ('big', S_VT + blk)], [('ps', by)])
                    A('pe', lambda e, h=h, hs=hs, by=by, rbp=rbp: e.matmul(psf[by][:, hs], big[:, S_QX + h, cs], rbp[:, hs], start=False, stop=True),
                      [('big', S_QX + h, blk), rbk], [('ps', by)])
                st = stat[n % 2]
                sk = ('stat', n % 2)
                y3 = psf[by][:].rearrange("p (h v) -> p h v", h=4)
                yc = ycb[n % 2]
                yk = ('ycb', n % 2)
                yc3 = yc[:].rearrange("p (h v) -> p h v", h=4)
                ysq = tmpf[2]
                A('dve', lambda e, st=st, y3=y3: e.tensor_reduce(out=st[:, 0:4], in_=y3, axis=AX.X, op=ALU.add), [('ps', by)], [sk])
                A('dve', lambda e, st=st: e.tensor_scalar(out=st[:, 0:4], in0=st[:, 0:4], scalar1=1.0 / 128, scalar2=None, op0=ALU.mult), [sk], [sk])
                A('dve', lambda e, st=st, y3=y3, yc3=yc3: e.tensor_tensor(out=yc3, in0=y3, in1=st[:, 0:4].unsqueeze(2).to_broadcast([128, 4, 128]), op=ALU.subtract),
                  [('ps', by), sk], [yk])
                A('pool', lambda e, yc=yc: e.tensor_tensor(out=ysq[:], in0=yc[:], in1=yc[:], op=ALU.mult), [yk], [('tmpf', 2)])
                A('dve', lambda e, st=st: e.tensor_reduce(out=st[:, 4:8], in_=ysq[:].rearrange("p (h v) -> p h v", h=4), axis=AX.X, op=ALU.add),
                  [('tmpf', 2)], [sk])

            def ch_p2(blk):
                state['stage'] = 'chunks'
                n = G0 + blk
                st = stat[n % 2]
                sk = ('stat', n % 2)
                A('act', lambda e, st=st: e.activation(out=st[:, 4:8], in_=st[:, 4:8], func=AF.Ln, bias=GN_EPS, scale=1.0 / 128), [sk], [sk])
                A('act', lambda e, st=st: e.activation(out=st[:, 4:8], in_=st[:, 4:8], func=AF.Exp, scale=-0.5), [sk], [sk])

            def ch_p3(blk):
                state['stage'] = 'chunks'
                n = G0 + blk
                st = stat[n % 2]
                sk = ('stat', n % 2)
                yc = ycb[n % 2]
                yk = ('ycb', n % 2)
                yc3 = yc[:].rearrange("p (h v) -> p h v", h=4)
                A('dve', lambda e, st=st, yc3=yc3: e.tensor_tensor(out=yc3, in0=yc3, in1=st[:, 4:8].unsqueeze(2).to_broadcast([128, 4, 128]), op=ALU.mult),
                  [yk, sk], [yk])
                ro = rtok[n % 2]
                A('dve', lambda e, ro=ro, yc=yc: e.tensor_tensor(out=ro[:], in0=yc[:], in1=gs[:, blk, :], op=ALU.mult),
                  [yk, ('gs', blk)], [('rtok', n % 2)])

            def ch_tr(blk):
                state['stage'] = 'chunks'
                n = G0 + blk
                cs = slice(blk * 128, (blk + 1) * 128)
                ro = rtok[n % 2]
                hb = state['cnt'] % 2
                state['cnt'] += 1
                for h in range(4):
                    A('pe', lambda e, h=h, hb=hb, ro=ro: e.transpose(psbs[hb][:, h * 128:(h + 1) * 128], ro[:, h * 128:(h + 1) * 128], ident[:]),
                      [('rtok', n % 2), 'ident'], [('psb', hb)])
                for h in range(4):
                    A('act', lambda e, h=h, hb=hb: e.activation(out=big[:, S_CAT + 4 + h, cs], in_=psbs[hb][:, h * 128:(h + 1) * 128], func=AF.Copy,
                                                                scale=gnw[:, l * 4 + h:l * 4 + h + 1]),
                      [('psb', hb), ('c', 'c_gnw')], bw(S_CAT + 4 + h, blk))

            ei = [0]
            pend = None
            pend_tr = []
            roff = {}
            if noret:
                for u in range(4):
                    ring.get((t, l, 'w_r', u)); ring.release((t, l, 'w_r', u))
                for h in range(4):
                    A('dve', lambda e, h=h: e.memset(big[:, S_CAT + 4 + h, :], 0.0), [], bw(S_CAT + 4 + h))
            if 'noatt' in flags:
                for j in range(4):
                    A('dve', lambda e, j=j: e.memset(big[:, S_CAT + j, :], 0.0), [], bw(S_CAT + j))
            csched = {
                4: [(ch_scores, 0), (ch_scores, 1)],
                5: [(ch_scores, 2), (ch_scores, 3), (ch_state, 0), (ch_p1, 0)],
                6: [(ch_state, 1), (ch_p2, 0), (ch_p1, 1)],
                7: [(ch_state, 2), (ch_p3, 0), (ch_p2, 1), (ch_p1, 2)],
                8: [(ch_state, 3), (ch_tr, 0), (ch_p3, 1), (ch_p2, 2), (ch_p1, 3)],
                9: [(ch_tr, 1), (ch_p3, 2), (ch_p2, 3)],
                10: [(ch_tr, 2), (ch_p3, 3)],
                11: [(ch_tr, 3)],
            }
            for s_i, (g, qb) in enumerate(items):
                if not noret and s_i == 0:
                    roff[0] = ring.get((t, l, 'w_r', 0))
                    roff[1] = ring.get((t, l, 'w_r', 1))
                if not noret and s_i == 4:
                    roff[2] = ring.get((t, l, 'w_r', 2))
                    roff[3] = ring.get((t, l, 'w_r', 3))
                outs = None
                if 'noatt' not in flags:
                    outs = att_front(g, qb, ei)
                new_tr = []
                if not noret:
                    if s_i < 4:
                        blk = s_i
                        bq = tm_block(0, roff[0], blk)
                        rot_front(S_QTOK, blk, bq, 0)
                        bk_ = tm_block(1, roff[1], blk)
                        rot_front(S_KTOK, blk, bk_, 1)
                        new_tr = [(S_QTOK, S_QT, S_QX, blk), (S_KTOK, S_KT, None, blk)]
                    else:
                        blk = s_i - 4
                        bv = tm_block(2, roff[2], blk)
                        v_post(blk, bv)
                        bg_ = tm_block(3, roff[3], blk)
                        g_post(blk, bg_)
                if 'noatt' not in flags:
                    if pend is not None:
                        att_back(*pend)
                    pend = (g, qb, outs)
                for tr in pend_tr:
                    rot_back(*tr)
                pend_tr = new_tr
                if not noret:
                    for fn, blk_ in csched.get(s_i, []):
                        fn(blk_)
                if not noret and s_i == 3:
                    ring.release((t, l, 'w_r', 0))
                    ring.release((t, l, 'w_r', 1))
                if not noret and s_i == 7:
                    ring.release((t, l, 'w_r', 2))
                    ring.release((t, l, 'w_r', 3))
            if pend is not None:
                att_back(*pend)
            for tr in pend_tr:
                rot_back(*tr)
            state['stage'] = 'att'
            if t + 1 < nt:
                A('act', lambda e: e.activation(out=akP[l][:], in_=akC[:, :, (NB - 1) * 128:NB * 128], func=AF.Copy),
                  [('akC', 0), ('akC', 1)], [('akP', l)])
                A('act', lambda e: e.activation(out=avP[l][:], in_=avC[:, NB - 1, :], func=AF.Copy),
                  ['avC'], [('avP', l)])

            offs = {}
            for u in range(2):
                offs[u] = ring.get((t, l, 'w_o', u))
            cat_reads = {}
            for cc in range(8):
                if cc < 4:
                    cat_reads[cc] = [('big', S_CAT + cc, 0), ('big', S_CAT + cc, 1)]
                else:
                    cat_reads[cc] = [('big', S_CAT + cc, blk) for blk in range(NB)]

            def wo_half(dc, half):
                state['stage'] = 'wo'
                b = nb()
                for cc in range(half * 4, half * 4 + 4):
                    u, ccl = cc // 4, cc % 4
                    key = (t, l, 'w_o', u)
                    a = ccl * 1024 + dc * 128
                    A('pe', lambda e, b=b, a=a, cc=cc, u=u: e.matmul(psf[b][:], ring_t[:, offs[u] + a:offs[u] + a + 128], big[:, S_CAT + cc, :],
                                                                       start=(cc % 4 == 0), stop=(cc % 4 == 3)),
                      ring.pages(key, a, a + 128) + cat_reads[cc], [('ps', b)])
                return b

            def wo_add(dc, b):
                state['stage'] = 'wo'
                A('dve', lambda e, b=b, dc=dc: e.tensor_tensor(out=hT[:, dc, :], in0=psf[b][:], in1=hT[:, dc, :], op=ALU.add),
                  [('ps', b), ('hT', dc)], [('hT', dc)])

            for s_i in range(8, 12):
                dcs = [2 * (s_i - 8), 2 * (s_i - 8) + 1]
                bs_ = [wo_half(dc, 0) for dc in dcs]
                if not noret:
                    for fn, blk_ in csched.get(s_i, []):
                        fn(blk_)
                for dc, b_ in zip(dcs, bs_):
                    wo_add(dc, b_)
            if not noret and t + 1 < nt:
                state['stage'] = 'chunks'
                A('pool', lambda e: e.tensor_copy(out=RbC[l][:], in_=Rst[l][:]), [('R', l)], [('RbC', l)])
            np_start()
            for dc in range(KC):
                b_ = wo_half(dc, 1)
                np_mm()
                wo_add(dc, b_)
                np_feed(dc)
            ring.release((t, l, 'w_o', 0))
            ring.release((t, l, 'w_o', 1))
            state['stage'] = 'post'

        for t in range(nt):
            ts = slice(t * TS, (t + 1) * TS)
            rec.add('sp', lambda e, ts=ts: e.dma_start(out=hT[:], in_=xT[:, :, ts].rearrange("k p s -> p k s")),
                    writes=[('hT', k) for k in range(KC)], dma_sem='x')
            np_start()
            for k_ in range(KC):
                np_feed(k_)
            for l in range(nl):
                if 'noffn' not in flags:
                    ffn(t, l, 0)
                else:
                    for u in range(11):
                        ring.get((t, l, 'w_gu1', u)); ring.release((t, l, 'w_gu1', u))
                    for u in range(8):
                        ring.get((t, l, 'w_d1', u)); ring.release((t, l, 'w_d1', u))
                if 'nomix' not in flags:
                    mixer(t, l)
                    if dbg is not None and t == 0 and l == 0:
                        allr = []
                        for s_ in range(40):
                            allr += br(s_)
                        rec.add('sp', lambda e: e.dma_start(out=dbg.rearrange("p (a b) -> p a b", a=40), in_=big[:, 0:40, :]), reads=allr, writes=['dbg'], dma_sem='dbg')
                else:
                    for name, nu in (('w_aq', 1), ('w_ak', 1), ('w_av', 1), ('w_r', 4), ('w_o', 2)):
                        for u in range(nu):
                            ring.get((t, l, name, u)); ring.release((t, l, name, u))
                if 'noffn' not in flags:
                    ffn(t, l, 1)
                else:
                    for u in range(11):
                        ring.get((t, l, 'w_gu2', u)); ring.release((t, l, 'w_gu2', u))
                    for u in range(8):
                        ring.get((t, l, 'w_d2', u)); ring.release((t, l, 'w_d2', u))
            np_finish(None, 0, dst_is_h=True)
            rec.add('sp', lambda e, ts=ts: e.dma_start(out=yT[:, :, ts].rearrange("k p s -> p k s"), in_=hT[:]),
                    reads=[('hT', k) for k in range(KC)], writes=[('yT', t), 'ysem'], dma_sem='y')
        rec.add('sp', None, reads=[('yT', t) for t in range(nt)] + (['dbg'] if dbg is not None else []))
        if use_scratch:
            rec.add('sp', None, reads=[('ssem', j) for j in range(4)])

        semkeys = rec.finalize()
        sems = {k: es.enter_context(nc.semaphore(f"s{n}")) for n, k in enumerate(semkeys)}
        with nc.Block() as block:
            @block.sync
            def _(e):
                rec.emit_engine('sp', e, sems)

            @block.gpsimd
            def _(e):
                rec.emit_engine('pool', e, sems)

            @block.tensor
            def _(e):
                rec.emit_engine('pe', e, sems)

            @block.vector
            def _(e):
                rec.emit_engine('dve', e, sems)

            @block.scalar
            def _(e):
                rec.emit_engine('act', e, sems)
    return nc


def make_in_maps(inputs, nl, nt, ncore):
    S = nt * TS
    w = prep_weights(inputs, nl)
    tabs = const_tables(nt)
    shared = dict(w)
    for k, v in tabs.items():
        if k != 'dec':
            shared[k] = v
    x = np.asarray(inputs['x'], dtype=np.float32)
    maps = []
    for c in range(ncore):
        m = dict(shared)
        m['xT'] = _c(x[c, :S, :].T).reshape(KC, 128, S)
        maps.append(m)
    return maps


def run(inputs, nl=NLAYER, nt=SEQ // TS, ncore=NCORE, flags=(), trace=False):
    inputs = {k: np.asarray(v) for k, v in inputs.items()}
    nc = build(nl, nt, flags=flags)
    maps = make_in_maps(inputs, nl, nt, ncore)
    res = run_bass_kernel_spmd(nc, maps, core_ids=list(range(ncore)), trace=trace)
    S = nt * TS
    out = np.stack([r['yT'].reshape(D, S).T for r in res.results], axis=0)
    return np.ascontiguousarray(out, dtype=np.float32), res


def kernel(**inputs):
    out, _ = run(inputs)
    return out
```

```python
import numpy as np
from contextlib import ExitStack
import concourse.bass as bass
import concourse.mybir as mybir
from concourse.bass_utils import run_bass_kernel_spmd

F32 = mybir.dt.float32
BF16 = mybir.dt.bfloat16
AF = mybir.ActivationFunctionType
ALU = mybir.AluOpType
AX = mybir.AxisListType

D = 1024
KC = 8
FF = 2816
FC = 22
TS = 512
NB = 4
NLAYER = 4
SEQ = 2048
NCORE = 8
NORM_EPS = 1e-6
GN_EPS = 1e-5
PAGE = 512
RING_CAP = 20480
NWSEM = 12
MASK_NEG = -2400.0


class Rec:
    def __init__(self):
        self.ops = []
        self.last_w = {}
        self.readers = {}

    def add(self, eng, emit, reads=(), writes=(), dma_sem=None):
        idx = len(self.ops)
        deps = set()
        for r in reads:
            w = self.last_w.get(r)
            if w is not None:
                deps.add(w)
        for r in writes:
            w = self.last_w.get(r)
            if w is not None:
                deps.add(w)
            rs = self.readers.get(r)
            if rs:
                deps.update(rs)
        for r in reads:
            self.readers.setdefault(r, []).append(idx)
        for r in writes:
            self.last_w[r] = idx
            self.readers[r] = []
        self.ops.append(dict(eng=eng, emit=emit, deps=deps, dma_sem=dma_sem, sig=False, val=None, semkey=None))
        return idx

    def finalize(self):
        ops = self.ops
        for op in ops:
            keep = set()
            for d in op['deps']:
                dop = ops[d]
                if (dop['dma_sem'] is None and op['dma_sem'] is None
                        and dop['eng'] == 'pe' and op['eng'] == 'pe'):
                    continue
                keep.add(d)
                dop['sig'] = True
            op['deps'] = keep
        cnt = {}
        for op in ops:
            if op['dma_sem'] is not None:
                key = ('dma', op['dma_sem'])
                cnt[key] = cnt.get(key, 0) + 16
                op['val'] = cnt[key]
                op['semkey'] = key
                op['sig'] = True
            elif op['sig']:
                key = ('eng', op['eng'])
                cnt[key] = cnt.get(key, 0) + 1
                op['val'] = cnt[key]
                op['semkey'] = key
        return sorted({op['semkey'] for op in ops if op['sig']}, key=str)

    def emit_engine(self, engname, eng, sems):
        waited = {}
        for op in self.ops:
            if op['eng'] != engname:
                continue
            need = {}
            for d in op['deps']:
                dop = self.ops[d]
                k = dop['semkey']
                if dop['val'] > need.get(k, 0):
                    need[k] = dop['val']
            for k, v in need.items():
                if v > waited.get(k, 0):
                    eng.wait_ge(sems[k], v)
                    waited[k] = v
            if op['emit'] is not None:
                ins = op['emit'](eng)
                if op['sig']:
                    ins.then_inc(sems[op['semkey']], 16 if op['dma_sem'] is not None else 1)


class Ring:
    def __init__(self, rec, ring_t, cap, sched, use_scratch):
        self.rec = rec
        self.t = ring_t
        self.cap = cap
        self.sched = sched
        self.next_load = 0
        self.next_use = 0
        self.live = []
        self.head = 0
        self.off = {}
        self.ndma = 0
        self.nscr = 0
        self.use_scratch = use_scratch

    def _try_alloc(self, size):
        if not self.live:
            self.head = size
            return 0
        tail = self.live[0][1]
        if self.head > tail or (self.head == tail and False):
            if self.head + size <= self.cap:
                off = self.head
            elif size < tail:
                off = 0
            else:
                return None
        else:
            if self.head + size < tail:
                off = self.head
            else:
                return None
        self.head = off + size
        return off

    def pump(self):
        while self.next_load < len(self.sched):
            u = self.sched[self.next_load]
            size = u['size']
            asz = ((size + PAGE - 1) // PAGE) * PAGE
            off = self._try_alloc(asz)
            if off is None:
                return
            self.live.append((u['key'], off, asz))
            self.off[u['key']] = off
            pages = [('rg', p) for p in range(off // PAGE, (off + asz) // PAGE)]
            si = self.ndma % NWSEM
            self.ndma += 1
            dst = self.t[:, off:off + size]
            if u['first'] or not self.use_scratch:
                src = u['src32']
                self.rec.add('pool', lambda e, dst=dst, src=src: e.dma_start(out=dst, in_=src),
                             writes=pages + [('wsem', si)], dma_sem=f'w{si}')
                if self.use_scratch:
                    sj = self.nscr % 4
                    self.nscr += 1
                    scr = u['scr']
                    self.rec.add('sp', lambda e, dst=dst, scr=scr: e.dma_start(out=scr, in_=dst),
                                 reads=pages, writes=[('scr', u['skey']), ('ssem', sj)], dma_sem=f'sc{sj}')
            else:
                scr = u['scr']
                self.rec.add('pool', lambda e, dst=dst, scr=scr: e.dma_start(out=dst, in_=scr),
                             reads=[('scr', u['skey'])], writes=pages + [('wsem', si)], dma_sem=f'w{si}')
            self.next_load += 1

    def get(self, key):
        u = self.sched[self.next_use]
        assert u['key'] == key, (u['key'], key)
        assert self.next_load > self.next_use, "ring too small: unit not loaded " + str(key)
        self.next_use += 1
        return self.off[key]

    def pages(self, key, a, b):
        off = self.off[key]
        return [('rg', p) for p in range((off + a) // PAGE, (off + b - 1) // PAGE + 1)]

    def release(self, key):
        assert self.live[0][0] == key, (self.live[0][0], key)
        self.live.pop(0)
        self.pump()


def _c(a):
    return np.ascontiguousarray(a, dtype=np.float32)


def prep_weights(inp, nl):
    L = nl
    out = {}

    def gu(wg, wu):
        g = wg[:L].reshape(L, 8, 128, 11, 2, 128)
        u_ = wu[:L].reshape(L, 8, 128, 11, 2, 128)
        st = np.stack([g, u_], axis=0)
        st = st.transpose(1, 4, 3, 5, 0, 2, 6)
        return _c(st).reshape(L, 11, 128, 4096)

    def dn(wd):
        d = wd[:L].reshape(L, 22, 128, 8, 128)
        d = d.transpose(0, 3, 2, 1, 4)
        return _c(d).reshape(L, 8, 128, 2816)

    out['w_gu1'] = gu(inp['ffn1_w_gate'], inp['ffn1_w_up'])
    out['w_d1'] = dn(inp['ffn1_w_down'])
    out['w_gu2'] = gu(inp['ffn2_w_gate'], inp['ffn2_w_up'])
    out['w_d2'] = dn(inp['ffn2_w_down'])
    win = inp['w_in'][:L].reshape(L, 8, 128, 2816)
    aq = win[..., 0:512].reshape(L, 8, 128, 8, 64)
    aqt = np.concatenate([aq[:, :, :, 0:4, :], aq[:, :, :, 4:8, :]], axis=-1)
    out['w_aq'] = _c(aqt.transpose(0, 2, 1, 3, 4)).reshape(L, 1, 128, 4096)
    ak = win[..., 512:640]
    akp = np.zeros((L, 8, 128, 2, 128), np.float32)
    akp[:, :, :, 0, 0:64] = ak[..., 0:64]
    akp[:, :, :, 1, 64:128] = ak[..., 64:128]
    out['w_ak'] = _c(akp.transpose(0, 2, 1, 3, 4)).reshape(L, 1, 128, 2048)
    out['w_av'] = _c(win[..., 640:768].transpose(0, 2, 1, 3)).reshape(L, 1, 128, 1024)
    r = win[..., 768:2816].reshape(L, 8, 128, 4, 512)
    out['w_r'] = _c(r.transpose(0, 3, 2, 1, 4)).reshape(L, 4, 128, 4096)
    wo = inp['w_out'][:L]
    wa = wo[:, 0:512].reshape(L, 8, 64, 1024)
    wat = np.concatenate([wa[:, 0:4], wa[:, 4:8]], axis=2)
    wr = wo[:, 512:1024].reshape(L, 4, 128, 1024)
    woc = np.concatenate([wat, wr], axis=1)
    woc = woc.reshape(L, 2, 4, 128, 1024).transpose(0, 1, 3, 2, 4)
    out['w_o'] = _c(woc).reshape(L, 2, 128, 4096)
    nrm = np.stack([inp['ffn1_norm'][:L], inp['mix_norm'][:L], inp['ffn2_norm'][:L]], axis=1)
    nrm = nrm.reshape(L, 3, 8, 128).transpose(3, 0, 1, 2)
    fin = inp['final_norm'].reshape(8, 128).T
    out['c_norm'] = _c(np.concatenate([nrm.reshape(128, L * 24), fin], axis=1))
    out['c_gnw'] = _c(inp['ret_gn_w'][:L].reshape(L * 4, 128).T)
    out['c_sink'] = _c(np.broadcast_to(inp['attn_sinks'][:L].reshape(1, L * 8), (128, L * 8)))
    return out


def const_tables(nt):
    S = nt * TS
    pos = np.arange(S, dtype=np.float32)
    inv_freq = (10000.0 ** (-np.arange(0, 128, 2, dtype=np.float32) / 128.0)).astype(np.float32)
    ang = pos[:, None] * inv_freq[None, :]
    cos = np.cos(ang).astype(np.float32).reshape(S // 128, 128, 64).transpose(1, 0, 2)
    sin = np.sin(ang).astype(np.float32).reshape(S // 128, 128, 64).transpose(1, 0, 2)
    t = {}
    t['c_cos'] = _c(cos).reshape(128, (S // 128) * 64)
    t['c_sin'] = _c(sin).reshape(128, (S // 128) * 64)
    H = 4
    lg = np.log(1.0 - 2.0 ** (-5.0 - np.arange(H, dtype=np.float32))).astype(np.float32)
    idx = np.arange(128, dtype=np.float32)
    dif = idx[:, None] - idx[None, :]
    dm = np.where(dif[None] >= 0, np.exp(np.maximum(dif, 0.0)[None] * lg[:, None, None]), 0.0)
    dmt = dm.transpose(2, 0, 1) * np.float32(128.0 ** -0.5)
    t['c_dmat'] = _c(dmt).reshape(128, 512)
    zeta = np.exp((127.0 - idx)[None, :] * lg[:, None]) * np.float32(128.0 ** -0.5)
    xi = np.exp((idx + 1.0)[None, :] * lg[:, None])
    t['c_xi'] = _c(np.broadcast_to(xi.reshape(1, 512), (128, 512)))
    dec = np.exp(128.0 * lg)
    t['c_zs'] = _c(zeta.T)
    t['dec'] = [float(v) for v in dec.astype(np.float32)]
    kk = np.arange(128)[:, None]
    qq = np.arange(128)[None, :]
    m0 = np.where(kk <= qq, 0.0, MASK_NEG)
    m1 = np.where(qq < kk, 0.0, MASK_NEG)
    t['c_mask'] = _c(np.stack([m0, m1], axis=1)).reshape(128, 256)
    t['c_ident'] = _c(np.eye(128))
    return t


WSPEC = [('w_gu1', 11, 4096), ('w_d1', 8, 2816), ('w_aq', 1, 4096), ('w_ak', 1, 2048), ('w_av', 1, 1024),
         ('w_r', 4, 4096), ('w_o', 2, 4096), ('w_gu2', 11, 4096), ('w_d2', 8, 2816)]


def build(nl, nt, first_layer_only_ffn=False, flags=()):
    nc = bass.Bass("TRN2", target_bir_lowering=False)
    S = nt * TS
    NBLK = S // 128
    tabs = const_tables(nt)
    dec = tabs['dec']
    dram = {}
    for name, nu, sz in WSPEC:
        dram[name] = nc.dram_tensor(name, [nl, nu, 128, sz], F32, kind="ExternalInput").ap()
    use_scratch = nt > 1
    scr = {}
    if use_scratch:
        for name, nu, sz in WSPEC:
            scr[name] = nc.dram_tensor("s_" + name, [nl, nu, 128, sz], BF16, kind="Internal").ap()
    cshape = {'c_norm': nl * 24 + 8, 'c_gnw': nl * 4, 'c_sink': nl * 8, 'c_cos': NBLK * 64, 'c_sin': NBLK * 64,
              'c_dmat': 512, 'c_xi': 512, 'c_zs': 4, 'c_mask': 256, 'c_ident': 128}
    for name, w in cshape.items():
        dram[name] = nc.dram_tensor(name, [128, w], F32, kind="ExternalInput").ap()
    xT = nc.dram_tensor("xT", [KC, 128, S], F32, kind="ExternalInput").ap()
    yT = nc.dram_tensor("yT", [KC, 128, S], F32, kind="ExternalOutput").ap()
    dbg = nc.dram_tensor("dbg", [128, 40 * TS], BF16, kind="ExternalOutput").ap() if 'dbg' in flags else None

    sched = []
    for t in range(nt):
        for l in range(nl):
            order = [('w_gu1', 11), ('w_d1', 8), ('w_aq', 1), ('w_ak', 1), ('w_av', 1), ('w_r', 4), ('w_o', 2),
                     ('w_gu2', 11), ('w_d2', 8)]
            for name, nu in order:
                sz = dict((n, s) for n, _, s in WSPEC)[name]
                for u in range(nu):
                    sched.append(dict(key=(t, l, name, u), skey=(l, name, u), size=sz, first=(t == 0),
                                      src32=dram[name][l, u], scr=(scr[name][l, u] if use_scratch else None)))

    with ExitStack() as es:
        def sb(name, shape, dt):
            return es.enter_context(nc.sbuf_tensor(name, shape, dt))

        rec = Rec()
        ring_t = sb("ring", [128, RING_CAP], BF16)
        ring = Ring(rec, ring_t, RING_CAP, sched, use_scratch)
        hT = sb("hT", [128, KC, TS], F32)
        nT = sb("nT", [128, KC, TS], BF16)
        big = sb("big", [128, 48, TS], BF16)
        sgt = [sb(f"sg{i}", [128, TS], F32) for i in range(2)]
        rstd = sb("rstd", [128, TS], F32)
        cn = sb("cn", [128, cshape['c_norm']], F32)
        gnw = sb("gnw", [128, nl * 4], F32)
        sink = sb("sink", [128, nl * 8], F32)
        sinke = sb("sinke", [128, nl * 8], F32)
        cos_t = sb("cos", [128, NBLK * 64], F32)
        sin_t = sb("sin", [128, NBLK * 64], F32)
        dmat = sb("dmat", [128, 512], F32)
        xi_t = sb("xi", [128, 512], F32)
        zs_t = sb("zs", [128, 4], F32)
        maskf = sb("maskf", [128, 256], F32)
        maskb = sb("maskb", [128, 256], BF16)
        identf = sb("identf", [128, 128], F32)
        ident = sb("ident", [128, 128], BF16)
        onesd = sb("onesd", [128, 128], BF16)
        ones = sb("ones", [128, 128], BF16)
        akP = [sb(f"akP{l}", [128, 2, 128], BF16) for l in range(nl)]
        avP = [sb(f"avP{l}", [128, 128], BF16) for l in range(nl)]
        Rst = [sb(f"R{l}", [128, 512], F32) for l in range(nl)]
        RbC = [sb(f"RbC{l}", [128, 512], BF16) for l in range(nl)]
        Rbt = [sb(f"Rbt{i}", [128, 512], BF16) for i in range(3)]
        akC = sb("akC", [128, 2, TS], BF16)
        skb = sb("skb", [128, 2, TS], BF16)
        avC = sb("avC", [128, NB, 128], BF16)
        rt = [sb(f"rt{i}", [128, 256], F32) for i in range(8)]
        Et = [sb(f"E{i}", [128, TS], BF16) for i in range(4)]
        Sd = [sb(f"Sd{i}", [128, TS], BF16) for i in range(4)]
        gs = sb("gs", [128, NB, TS], F32)
        tmpf = [sb(f"tmpf{i}", [128, TS], F32) for i in range(3)]
        ycb = [sb(f"ycb{i}", [128, TS], F32) for i in range(2)]
        stat = [sb(f"stat{i}", [128, 16], F32) for i in range(2)]
        rtok = [sb(f"rtok{i}", [128, TS], BF16) for i in range(2)]
        NPS = 5
        psf = [es.enter_context(nc.psum_tensor(f"ps{i}", [128, TS], F32)) for i in range(NPS + 1)]
        psbs = [es.enter_context(nc.psum_tensor(f"psb{i}", [128, 2 * TS], BF16)) for i in range(2)]

        state = dict(bank=0, cnt=0)
        def slot(s0, n):
            return big[:, s0:s0 + n, :]
        aT = lambda fc: big[:, fc, :]
        S_AQ, S_QTOK, S_KTOK, S_QT, S_QX, S_KT, S_VT, S_VZ, S_CAT, S_SQ = 0, 4, 8, 12, 16, 20, 24, 28, 32, 40
        SUBS = {}
        for s_ in range(4, 12):
            SUBS[s_] = ['lo', 'hi']
        for s_ in range(12, 24):
            SUBS[s_] = [0, 1, 2, 3]
        for s_ in range(32, 36):
            SUBS[s_] = [0, 1]
        for s_ in range(36, 40):
            SUBS[s_] = [0, 1, 2, 3]

        def bw(slot_, sub=None):
            if sub is None:
                return [('big', slot_)] + [('big', slot_, x) for x in SUBS.get(slot_, [])]
            return [('big', slot_), ('big', slot_, sub)]

        def br(slot_, sub=None):
            if sub is None:
                if slot_ in SUBS:
                    return [('big', slot_, x) for x in SUBS[slot_]]
                return [('big', slot_)]
            return [('big', slot_, sub)]


        def nb():
            b = state['bank']
            state['bank'] = (b + 1) % NPS
            return b

        disabled = {f[4:] for f in flags if f.startswith('off_')}
        state['stage'] = 'init'

        def A(eng, fn, reads=(), writes=()):
            if state['stage'] in disabled:
                return
            reads = list(reads)
            writes = list(writes)
            for r in reads:
                if isinstance(r, tuple) and r[0] in ('ps', 'psb') and r not in writes:
                    writes.append(r)
            rec.add(eng, fn, reads=reads, writes=writes)

        cl = [('c_norm', cn), ('c_gnw', gnw), ('c_sink', sink), ('c_cos', cos_t), ('c_sin', sin_t), ('c_dmat', dmat),
              ('c_xi', xi_t), ('c_zs', zs_t), ('c_mask', maskf), ('c_ident', identf)]
        for i, (name, tt) in enumerate(cl):
            rec.add('sp', lambda e, tt=tt, name=name: e.dma_start(out=tt[:], in_=dram[name]),
                    writes=[('c', name)], dma_sem=f'c{i}')
        A('dve', lambda e: e.tensor_copy(out=maskb[:], in_=maskf[:]), [('c', 'c_mask')], ['maskb'])
        A('dve', lambda e: e.tensor_copy(out=ident[:], in_=identf[:]), [('c', 'c_ident')], ['ident'])
        A('dve', lambda e: e.memset(onesd[:], 1.0 / D), [], ['onesd'])
        A('dve', lambda e: e.memset(ones[:], 1.0), [], ['ones'])
        A('act', lambda e: e.activation(out=sinke[:], in_=sink[:], func=AF.Exp), [('c', 'c_sink')], ['sinke'])
        for l in range(nl):
            A('dve', lambda e, l=l: e.memset(Rst[l][:], 0.0), [], [('R', l)])
            A('dve', lambda e, l=l: e.memset(RbC[l][:], 0.0), [], [('RbC', l)])
        ring.pump()

        npst = dict(pending=[], nmm=0)
        NB_STAT = NPS

        def np_start():
            npst['pending'] = []
            npst['nmm'] = 0

        def np_feed(k):
            A('act', lambda e, k=k: e.activation(out=big[:, S_SQ + k, :], in_=hT[:, k, :], func=AF.Square),
              [('hT', k)], bw(S_SQ + k))
            npst['pending'].append(k)

        def np_mm():
            for k in npst['pending']:
                i = npst['nmm']
                A('pe', lambda e, k=k, i=i: e.matmul(psf[NB_STAT][:], onesd[:], big[:, S_SQ + k, :], start=(i == 0), stop=(i == KC - 1)),
                  [('big', S_SQ + k), 'onesd'], [('ps', NB_STAT)])
                npst['nmm'] += 1
            npst['pending'] = []

        def np_finish(l, which, dst_is_h=False):
            cbase = (l * 24 + which * 8) if l is not None else nl * 24
            np_mm()
            assert npst['nmm'] == KC
            b = NB_STAT
            A('act', lambda e, b=b: e.activation(out=rstd[:], in_=psf[b][:], func=AF.Ln, bias=NORM_EPS, scale=1.0),
              [('ps', b)], ['rstd'])
            A('act', lambda e: e.activation(out=rstd[:], in_=rstd[:], func=AF.Exp, scale=-0.5), ['rstd'], ['rstd'])
            for k in range(KC):
                if dst_is_h:
                    A('dve', lambda e, k=k: e.scalar_tensor_tensor(out=hT[:, k, :], in0=hT[:, k, :], scalar=cn[:, cbase + k:cbase + k + 1],
                                                                    in1=rstd[:], op0=ALU.mult, op1=ALU.mult),
                      [('hT', k), 'rstd', ('c', 'c_norm')], [('hT', k)])
                else:
                    A('dve', lambda e, k=k: e.scalar_tensor_tensor(out=nT[:, k, :], in0=hT[:, k, :], scalar=cn[:, cbase + k:cbase + k + 1],
                                                                    in1=rstd[:], op0=ALU.mult, op1=ALU.mult),
                      [('hT', k), 'rstd', ('c', 'c_norm')], [('nT', k)])

        def ffn(t, l, which):
            gname = 'w_gu1' if which == 0 else 'w_gu2'
            dname = 'w_d1' if which == 0 else 'w_d2'
            np_finish(l, 0 if which == 0 else 2)
            for u in range(11):
                key = (t, l, gname, u)
                off = ring.get(key)
                if u == 0:
                    banks0 = [[nb(), nb()], [nb(), nb()]]
                    for k in range(KC):
                        for fcl in range(2):
                            for gi in range(2):
                                b = banks0[fcl][gi]
                                a = ((fcl * 2 + gi) * 8 + k) * 128
                                A('pe', lambda e, b=b, a=a, k=k, off=off: e.matmul(psf[b][:], ring_t[:, off + a:off + a + 128], nT[:, k, :],
                                                                                     start=(k == 0), stop=(k == KC - 1)),
                                  ring.pages(key, a, a + 128) + [('nT', k)], [('ps', b)])
                for fcl in range(2):
                    fc = 2 * u + fcl
                    if u == 0:
                        bg, bu = banks0[fcl]
                    else:
                        bg, bu = nb(), nb()
                        for gi, b in ((0, bg), (1, bu)):
                            for k in range(KC):
                                a = ((fcl * 2 + gi) * 8 + k) * 128
                                A('pe', lambda e, b=b, a=a, k=k, off=off: e.matmul(psf[b][:], ring_t[:, off + a:off + a + 128], nT[:, k, :],
                                                                                     start=(k == 0), stop=(k == KC - 1)),
                                  ring.pages(key, a, a + 128) + [('nT', k)], [('ps', b)])
                    sg = sgt[fc % 2]
                    A('act', lambda e, sg=sg, bg=bg: e.activation(out=sg[:], in_=psf[bg][:], func=AF.Silu),
                      [('ps', bg)], [('sg', fc % 2)])
                    A('dve', lambda e, sg=sg, bu=bu, fc=fc: e.tensor_tensor(out=big[:, fc, :], in0=psf[bu][:], in1=sg[:], op=ALU.mult),
                      [('ps', bu), ('sg', fc % 2)], bw(fc))
                ring.release(key)
            np_start()
            for dc in range(KC):
                key = (t, l, dname, dc)
                off = ring.get(key)
                b = nb()
                for fc in range(FC):
                    a = fc * 128
                    A('pe', lambda e, b=b, a=a, fc=fc, off=off: e.matmul(psf[b][:], ring_t[:, off + a:off + a + 128], big[:, fc, :],
                                                                           start=(fc == 0), stop=(fc == FC - 1)),
                      ring.pages(key, a, a + 128) + [('big', fc)], [('ps', b)])
                np_mm()
                A('dve', lambda e, b=b, dc=dc: e.scalar_tensor_tensor(out=hT[:, dc, :], in0=psf[b][:], scalar=0.5, in1=hT[:, dc, :],
                                                                       op0=ALU.mult, op1=ALU.add),
                  [('ps', b), ('hT', dc)], [('hT', dc)])
                np_feed(dc)
                ring.release(key)

        def mixer(t, l):
            state['stage'] = 'mnorm'
            np_finish(l, 1)
            G0 = t * NB
            noret = 'noret' in flags
            state['stage'] = 'aq'
            key = (t, l, 'w_aq', 0)
            off = ring.get(key)
            banksq = [nb() for j in range(4)]
            for k in range(KC):
                for j in range(4):
                    b = banksq[j]
                    a = (k * 4 + j) * 128
                    A('pe', lambda e, b=b, a=a, k=k, off=off: e.matmul(psf[b][:], ring_t[:, off + a:off + a + 128], nT[:, k, :],
                                                                         start=(k == 0), stop=(k == KC - 1)),
                      ring.pages(key, a, a + 128) + [('nT', k)], [('ps', b)])
            for j in range(4):
                b = banksq[j]
                A('act', lambda e, b=b, j=j: e.activation(out=big[:, S_AQ + j, :], in_=psf[b][:], func=AF.Copy),
                  [('ps', b)], bw(S_AQ + j))
            ring.release(key)
            state['stage'] = 'ak'
            key = (t, l, 'w_ak', 0)
            off = ring.get(key)
            for g in range(2):
                b = nb()
                for k in range(KC):
                    a = (k * 2 + g) * 128
                    A('pe', lambda e, b=b, a=a, k=k, off=off: e.matmul(psf[b][:], ring_t[:, off + a:off + a + 128], nT[:, k, :],
                                                                         start=(k == 0), stop=(k == KC - 1)),
                      ring.pages(key, a, a + 128) + [('nT', k)], [('ps', b)])
                A('act', lambda e, b=b, g=g: e.activation(out=akC[:, g, :], in_=psf[b][:], func=AF.Copy),
                  [('ps', b)], [('akC', g)])
            ring.release(key)
            state['stage'] = 'av'
            key = (t, l, 'w_av', 0)
            off = ring.get(key)
            b = nb()
            for blk in range(NB):
                for k in range(KC):
                    a = k * 128
                    A('pe', lambda e, b=b, a=a, k=k, blk=blk, off=off: e.matmul(psf[b][:, blk * 128:(blk + 1) * 128], nT[:, k, blk * 128:(blk + 1) * 128],
                                                                                  ring_t[:, off + a:off + a + 128], start=(k == 0), stop=(k == KC - 1)),
                      ring.pages(key, a, a + 128) + [('nT', k)], [('ps', b)])
            A('act', lambda e, b=b: e.activation(out=avC[:].rearrange("p a b -> p (a b)"), in_=psf[b][:], func=AF.Copy),
              [('ps', b)], ['avC'])
            ring.release(key)

            state['stage'] = 'att'
            items = [(g, qb) for qb in range(NB) for g in range(2)]
            for g_ in range(2):
                A('dve', lambda e, g_=g_: e.tensor_scalar(out=skb[:, g_, :].rearrange("p (a b) -> p a b", a=4),
                                                          in0=sinke[:, l * 8 + 4 * g_:l * 8 + 4 * g_ + 4].unsqueeze(2).to_broadcast([128, 4, 128]),
                                                          scalar1=1.0 / 128, scalar2=None, op0=ALU.mult),
                  ['sinke'], [('skb', g_)])

            def att_front(g, qb, ei):
                state['stage'] = 'att'
                G = G0 + qb
                kbs = []
                if G > 0:
                    kbs.append(1)
                kbs.append(0)
                outs = []
                for kind in kbs:
                    b = nb()
                    if kind == 0:
                        lk = akC[:, g, qb * 128:(qb + 1) * 128]
                        rk = [('akC', g)]
                    elif qb == 0:
                        lk = akP[l][:, g, :]
                        rk = [('akP', l)]
                    else:
                        lk = akC[:, g, (qb - 1) * 128:qb * 128]
                        rk = [('akC', g)]
                    rq = big[:, S_AQ:S_AQ + 4, qb * 128:(qb + 1) * 128]
                    A('pe', lambda e, b=b, lk=lk, rq=rq: e.matmul(psf[b][:], lk, rq, start=True, stop=False),
                      rk + [('big', S_AQ + j) for j in range(4)], [('ps', b)])
                    mb = maskb[:, kind * 128:(kind + 1) * 128].unsqueeze(1).to_broadcast([128, 4, 128])
                    A('pe', lambda e, b=b, mb=mb: e.matmul(psf[b][:], ident[:], mb, start=False, stop=True),
                      ['maskb', 'ident'], [('ps', b)])
                    ee = ei[0] % 4
                    ei[0] += 1
                    A('act', lambda e, b=b, ee=ee: e.activation(out=Et[ee][:], in_=psf[b][:], func=AF.Exp, scale=0.125),
                      [('ps', b)], [('E', ee)])
                    outs.append((kind, ee))
                return outs

            def att_back(g, qb, outs):
                state['stage'] = 'att'
                bpv, bden = nb(), nb()
                n = len(outs)
                for i, (kind, ee) in enumerate(outs):
                    if kind == 0:
                        lv = avC[:, qb, :]
                        rv = ['avC']
                    elif qb == 0:
                        lv = avP[l][:]
                        rv = [('avP', l)]
                    else:
                        lv = avC[:, qb - 1, :]
                        rv = ['avC']
                    A('pe', lambda e, lv=lv, ee=ee, i=i, n=n, bpv=bpv: e.matmul(psf[bpv][:], lv, Et[ee][:], start=(i == 0), stop=(i == n - 1)),
                      rv + [('E', ee)], [('ps', bpv)])
                for i, (kind, ee) in enumerate(outs):
                    A('pe', lambda e, ee=ee, i=i, n=n, bden=bden: e.matmul(psf[bden][:], ones[:], Et[ee][:], start=(i == 0), stop=False),
                      ['ones', ('E', ee)], [('ps', bden)])
                A('pe', lambda e, bden=bden: e.matmul(psf[bden][:], ones[:], skb[:, g, :], start=False, stop=True),
                  ['ones', ('skb', g)], [('ps', bden)])
                r0, r1 = g * 64, (g + 1) * 64
                tf = tmpf[(qb * 2 + g) % 2]
                tfk = ('tmpf', (qb * 2 + g) % 2)
                A('act', lambda e, tf=tf, bden=bden: e.activation(out=tf[r0:r1, :], in_=psf[bden][r0:r1, :], func=AF.Ln), [('ps', bden)], [tfk])
                A('act', lambda e, tf=tf: e.activation(out=tf[r0:r1, :], in_=tf[r0:r1, :], func=AF.Exp, scale=-1.0), [tfk], [tfk])
                A('dve', lambda e, tf=tf, bpv=bpv: e.tensor_tensor(out=big[r0:r1, S_CAT:S_CAT + 4, qb * 128:(qb + 1) * 128],
                                                                    in0=psf[bpv][r0:r1, :].rearrange("p (a b) -> p a b", a=4),
                                                                    in1=tf[r0:r1, :].rearrange("p (a b) -> p a b", a=4), op=ALU.mult),
                  [('ps', bpv), tfk], sum([bw(S_CAT + j, g) for j in range(4)], []))

            def tm_block(u, off, blk):
                state['stage'] = f'rp{u}'
                key = (t, l, 'w_r', u)
                b = nb()
                for k in range(KC):
                    a = k * 512
                    A('pe', lambda e, b=b, a=a, k=k, blk=blk, off=off: e.matmul(psf[b][:], nT[:, k, blk * 128:(blk + 1) * 128],
                                                                                  ring_t[:, off + a:off + a + 512], start=(k == 0), stop=(k == KC - 1)),
                      ring.pages(key, a, a + 512) + [('nT', k)], [('ps', b)])
                return b

            def rot_front(dst_slot, blk, b, ri):
                state['stage'] = 'rot'
                G = G0 + blk
                X = psf[b][:].rearrange("p (h two e) -> p h two e", h=4, two=2)
                x1, x2 = X[:, :, 0, :], X[:, :, 1, :]
                cb = cos_t[:, G * 64:(G + 1) * 64].unsqueeze(1).to_broadcast([128, 4, 64])
                sn = sin_t[:, G * 64:(G + 1) * 64].unsqueeze(1).to_broadcast([128, 4, 64])
                r3 = [rt[ri * 4 + i][:].rearrange("p (h e) -> p h e", h=4) for i in range(4)]
                rk_ = [('rt', ri * 4 + i) for i in range(4)]
                O = big[:, dst_slot + blk, :].rearrange("p (h two e) -> p h two e", h=4, two=2)
                cst = [('c', 'c_cos'), ('c', 'c_sin')]
                A('dve', lambda e: e.tensor_tensor(out=r3[0], in0=x1, in1=cb, op=ALU.mult), [('ps', b)] + cst, [rk_[0]])
                A('dve', lambda e: e.tensor_tensor(out=r3[1], in0=x2, in1=sn, op=ALU.mult), [('ps', b)] + cst, [rk_[1]])
                A('dve', lambda e: e.tensor_tensor(out=r3[2], in0=x1, in1=sn, op=ALU.mult), [('ps', b)] + cst, [rk_[2]])
                A('dve', lambda e: e.tensor_tensor(out=r3[3], in0=x2, in1=cb, op=ALU.mult), [('ps', b)] + cst, [rk_[3]])
                A('dve', lambda e: e.tensor_tensor(out=O[:, :, 0, :], in0=r3[0], in1=r3[1], op=ALU.subtract),
                  [rk_[0], rk_[1]], bw(dst_slot + blk, 'lo'))
                A('dve', lambda e: e.tensor_tensor(out=O[:, :, 1, :], in0=r3[2], in1=r3[3], op=ALU.add),
                  [rk_[2], rk_[3]], bw(dst_slot + blk, 'hi'))

            def rot_back(dst_slot, t_slot, x_slot, blk):
                state['stage'] = 'tr'
                hb = state['cnt'] % 2
                state['cnt'] += 1
                for h in range(4):
                    A('pe', lambda e, h=h, hb=hb: e.transpose(psbs[hb][:, h * 128:(h + 1) * 128],
                                                              big[:, dst_slot + blk, h * 128:(h + 1) * 128], ident[:]),
                      [('big', dst_slot + blk, 'lo'), ('big', dst_slot + blk, 'hi'), 'ident'], [('psb', hb)])
                pv = psbs[hb][:, 0:512].rearrange("p (h i) -> p h i", h=4)
                A('act', lambda e, pv=pv: e.activation(out=big[:, t_slot:t_slot + 4, blk * 128:(blk + 1) * 128], in_=pv, func=AF.Copy),
                  [('psb', hb)], sum([bw(t_slot + h, blk) for h in range(4)], []))
                if x_slot is not None:
                    A('dve', lambda e, pv=pv: e.tensor_tensor(out=big[:, x_slot:x_slot + 4, blk * 128:(blk + 1) * 128], in0=pv,
                                                               in1=xi_t[:].rearrange("p (h i) -> p h i", h=4), op=ALU.mult),
                      [('psb', hb), ('c', 'c_xi')], sum([bw(x_slot + h, blk) for h in range(4)], []))

            def v_post(blk, b):
                state['stage'] = 'vpost'
                A('act', lambda e: e.activation(out=big[:, S_VT + blk, :], in_=psf[b][:], func=AF.Copy),
                  [('ps', b)], bw(S_VT + blk))
                A('dve', lambda e: e.tensor_tensor(out=big[:, S_VZ + blk, :].rearrange("p (h v) -> p h v", h=4),
                                                    in0=psf[b][:].rearrange("p (h v) -> p h v", h=4),
                                                    in1=zs_t[:].unsqueeze(2).to_broadcast([128, 4, 128]), op=ALU.mult),
                  [('ps', b), ('c', 'c_zs')], bw(S_VZ + blk))

            def g_post(blk, b):
                state['stage'] = 'gpost'
                A('act', lambda e: e.activation(out=gs[:, blk, :], in_=psf[b][:], func=AF.Silu), [('ps', b)], [('gs', blk)])

            state['stage'] = 'chunks'
            def rb_before(blk):
                return (RbC[l], ('RbC', l)) if blk == 0 else (Rbt[blk - 1], ('Rbt', blk - 1))

            def ch_scores(blk):
                state['stage'] = 'chunks'
                cs = slice(blk * 128, (blk + 1) * 128)
                bs = nb()
                for h in range(4):
                    A('pe', lambda e, h=h, bs=bs: e.matmul(psf[bs][:, h * 128:(h + 1) * 128], big[:, S_KT + h, cs], big[:, S_QT + h, cs],
                                                           start=True, stop=True),
                      [('big', S_KT + h, blk), ('big', S_QT + h, blk)], [('ps', bs)])
                sd = Sd[blk]
                A('dve', lambda e, bs=bs, sd=sd: e.tensor_tensor(out=sd[:], in0=psf[bs][:], in1=dmat[:], op=ALU.mult),
                  [('ps', bs), ('c', 'c_dmat')], [('Sd', blk)])

            def ch_state(blk):
                state['stage'] = 'chunks'
                bk = nb()
                for h in range(4):
                    hs = slice(h * 128, (h + 1) * 128)
                    A('pe', lambda e, h=h, hs=hs, bk=bk: e.matmul(psf[bk][:, hs], big[:, S_KTOK + blk, hs], big[:, S_VZ + blk, hs], start=True, stop=True),
                      [('big', S_KTOK + blk, 'lo'), ('big', S_KTOK + blk, 'hi'), ('big', S_VZ + blk)], [('ps', bk)])
                for h in range(4):
                    hs = slice(h * 128, (h + 1) * 128)
                    A('dve', lambda e, h=h, hs=hs, bk=bk: e.scalar_tensor_tensor(out=Rst[l][:, hs], in0=Rst[l][:, hs], scalar=dec[h], in1=psf[bk][:, hs],
                                                                                   op0=ALU.mult, op1=ALU.add),
                      [('R', l), ('ps', bk)], [('R', l)])
                if blk < NB - 1:
                    A('act', lambda e: e.activation(out=Rbt[blk][:], in_=Rst[l][:], func=AF.Copy), [('R', l)], [('Rbt', blk)])

            def ch_p1(blk):
                state['stage'] = 'chunks'
                n = G0 + blk
                cs = slice(blk * 128, (blk + 1) * 128)
                sd = Sd[blk]
                by = nb()
                rbp, rbk = rb_before(blk)
                for h in range(4):
                    hs = slice(h * 128, (h + 1) * 128)
                    A('pe', lambda e, h=h, hs=hs, by=by, sd=sd: e.matmul(psf[by][:, hs], sd[:, hs], big[:, S_VT + blk, hs], start=True, stop=False),
                      [('Sd', blk), ('big', S_VT + blk)], [('ps', by)])
                    A('pe', lambda e, h=h, hs=hs, by=by, rbp=rbp: e.matmul(psf[by][:, hs], big[:, S_QX + h, cs], rbp[:, hs], start=False, stop=True),
                      [('big', S_QX + h, blk), rbk], [('ps', by)])
                st = stat[n % 2]
                sk = ('stat', n % 2)
                y3 = psf[by][:].rearrange("p (h v) -> p h v", h=4)
                yc = ycb[n % 2]
                yk = ('ycb', n % 2)
                yc3 = yc[:].rearrange("p (h v) -> p h v", h=4)
                ysq = tmpf[2]
                A('dve', lambda e, st=st, y3=y3: e.tensor_reduce(out=st[:, 0:4], in_=y3, axis=AX.X, op=ALU.add), [('ps', by)], [sk])
                A('dve', lambda e, st=st: e.tensor_scalar(out=st[:, 0:4], in0=st[:, 0:4], scalar1=1.0 / 128, scalar2=None, op0=ALU.mult), [sk], [sk])
                A('dve', lambda e, st=st, y3=y3, yc3=yc3: e.tensor_tensor(out=yc3, in0=y3, in1=st[:, 0:4].unsqueeze(2).to_broadcast([128, 4, 128]), op=ALU.subtract),
                  [('ps', by), sk], [yk])
                A('act', lambda e, yc=yc: e.activation(out=ysq[:], in_=yc[:], func=AF.Square), [yk], [('tmpf', 2)])
                A('dve', lambda e, st=st: e.tensor_reduce(out=st[:, 4:8], in_=ysq[:].rearrange("p (h v) -> p h v", h=4), axis=AX.X, op=ALU.add),
                  [('tmpf', 2)], [sk])

            def ch_p2(blk):
                state['stage'] = 'chunks'
                n = G0 + blk
                st = stat[n % 2]
                sk = ('stat', n % 2)
                A('act', lambda e, st=st: e.activation(out=st[:, 4:8], in_=st[:, 4:8], func=AF.Ln, bias=GN_EPS, scale=1.0 / 128), [sk], [sk])
                A('act', lambda e, st=st: e.activation(out=st[:, 4:8], in_=st[:, 4:8], func=AF.Exp, scale=-0.5), [sk], [sk])

            def ch_p3(blk):
                state['stage'] = 'chunks'
                n = G0 + blk
                st = stat[n % 2]
                sk = ('stat', n % 2)
                yc = ycb[n % 2]
                yk = ('ycb', n % 2)
                yc3 = yc[:].rearrange("p (h v) -> p h v", h=4)
                A('dve', lambda e, st=st, yc3=yc3: e.tensor_tensor(out=yc3, in0=yc3, in1=st[:, 4:8].unsqueeze(2).to_broadcast([128, 4, 128]), op=ALU.mult),
                  [yk, sk], [yk])
                ro = rtok[n % 2]
                A('dve', lambda e, ro=ro, yc=yc: e.tensor_tensor(out=ro[:], in0=yc[:], in1=gs[:, blk, :], op=ALU.mult),
                  [yk, ('gs', blk)], [('rtok', n % 2)])

            def ch_tr(blk):
                state['stage'] = 'chunks'
                n = G0 + blk
                cs = slice(blk * 128, (blk + 1) * 128)
                ro = rtok[n % 2]
                hb = state['cnt'] % 2
                state['cnt'] += 1
                for h in range(4):
                    A('pe', lambda e, h=h, hb=hb, ro=ro: e.transpose(psbs[hb][:, h * 128:(h + 1) * 128], ro[:, h * 128:(h + 1) * 128], ident[:]),
                      [('rtok', n % 2), 'ident'], [('psb', hb)])
                for h in range(4):
                    A('act', lambda e, h=h, hb=hb: e.activation(out=big[:, S_CAT + 4 + h, cs], in_=psbs[hb][:, h * 128:(h + 1) * 128], func=AF.Copy,
                                                                scale=gnw[:, l * 4 + h:l * 4 + h + 1]),
                      [('psb', hb), ('c', 'c_gnw')], bw(S_CAT + 4 + h, blk))

            ei = [0]
            pend = None
            pend_tr = []
            roff = {}
            if noret:
                for u in range(4):
                    ring.get((t, l, 'w_r', u)); ring.release((t, l, 'w_r', u))
                for h in range(4):
                    A('dve', lambda e, h=h: e.memset(big[:, S_CAT + 4 + h, :], 0.0), [], bw(S_CAT + 4 + h))
            if 'noatt' in flags:
                for j in range(4):
                    A('dve', lambda e, j=j: e.memset(big[:, S_CAT + j, :], 0.0), [], bw(S_CAT + j))
            csched = {
                4: [(ch_scores, 0), (ch_scores, 1)],
                5: [(ch_scores, 2), (ch_scores, 3), (ch_state, 0), (ch_p1, 0)],
                6: [(ch_state, 1), (ch_p2, 0), (ch_p1, 1)],
                7: [(ch_state, 2), (ch_p3, 0), (ch_p2, 1), (ch_p1, 2)],
                8: [(ch_state, 3), (ch_tr, 0), (ch_p3, 1), (ch_p2, 2), (ch_p1, 3)],
                9: [(ch_tr, 1), (ch_p3, 2), (ch_p2, 3)],
                10: [(ch_tr, 2), (ch_p3, 3)],
                11: [(ch_tr, 3)],
            }
            for s_i, (g, qb) in enumerate(items):
                if not noret and s_i == 0:
                    roff[0] = ring.get((t, l, 'w_r', 0))
                    roff[1] = ring.get((t, l, 'w_r', 1))
                if not noret and s_i == 4:
                    roff[2] = ring.get((t, l, 'w_r', 2))
                    roff[3] = ring.get((t, l, 'w_r', 3))
                outs = None
                if 'noatt' not in flags:
                    outs = att_front(g, qb, ei)
                new_tr = []
                if not noret:
                    if s_i < 4:
                        blk = s_i
                        bq = tm_block(0, roff[0], blk)
                        rot_front(S_QTOK, blk, bq, 0)
                        bk_ = tm_block(1, roff[1], blk)
                        rot_front(S_KTOK, blk, bk_, 1)
                        new_tr = [(S_QTOK, S_QT, S_QX, blk), (S_KTOK, S_KT, None, blk)]
                    else:
                        blk = s_i - 4
                        bv = tm_block(2, roff[2], blk)
                        v_post(blk, bv)
                        bg_ = tm_block(3, roff[3], blk)
                        g_post(blk, bg_)
                if 'noatt' not in flags:
                    if pend is not None:
                        att_back(*pend)
                    pend = (g, qb, outs)
                for tr in pend_tr:
                    rot_back(*tr)
                pend_tr = new_tr
                if not noret:
                    for fn, blk_ in csched.get(s_i, []):
                        fn(blk_)
                if not noret and s_i == 3:
                    ring.release((t, l, 'w_r', 0))
                    ring.release((t, l, 'w_r', 1))
                if not noret and s_i == 7:
                    ring.release((t, l, 'w_r', 2))
                    ring.release((t, l, 'w_r', 3))
            if pend is not None:
                att_back(*pend)
            for tr in pend_tr:
                rot_back(*tr)
            state['stage'] = 'att'
            if t + 1 < nt:
                A('act', lambda e: e.activation(out=akP[l][:], in_=akC[:, :, (NB - 1) * 128:NB * 128], func=AF.Copy),
                  [('akC', 0), ('akC', 1)], [('akP', l)])
                A('act', lambda e: e.activation(out=avP[l][:], in_=avC[:, NB - 1, :], func=AF.Copy),
                  ['avC'], [('avP', l)])

            offs = {}
            for u in range(2):
                offs[u] = ring.get((t, l, 'w_o', u))
            cat_reads = {}
            for cc in range(8):
                if cc < 4:
                    cat_reads[cc] = [('big', S_CAT + cc, 0), ('big', S_CAT + cc, 1)]
                else:
                    cat_reads[cc] = [('big', S_CAT + cc, blk) for blk in range(NB)]

            def wo_half(dc, half):
                state['stage'] = 'wo'
                b = nb()
                for cc in range(half * 4, half * 4 + 4):
                    u, ccl = cc // 4, cc % 4
                    key = (t, l, 'w_o', u)
                    a = ccl * 1024 + dc * 128
                    A('pe', lambda e, b=b, a=a, cc=cc, u=u: e.matmul(psf[b][:], ring_t[:, offs[u] + a:offs[u] + a + 128], big[:, S_CAT + cc, :],
                                                                       start=(cc % 4 == 0), stop=(cc % 4 == 3)),
                      ring.pages(key, a, a + 128) + cat_reads[cc], [('ps', b)])
                return b

            def wo_add(dc, b):
                state['stage'] = 'wo'
                A('dve', lambda e, b=b, dc=dc: e.tensor_tensor(out=hT[:, dc, :], in0=psf[b][:], in1=hT[:, dc, :], op=ALU.add),
                  [('ps', b), ('hT', dc)], [('hT', dc)])

            for s_i in range(8, 12):
                dcs = [2 * (s_i - 8), 2 * (s_i - 8) + 1]
                bs_ = [wo_half(dc, 0) for dc in dcs]
                if not noret:
                    for fn, blk_ in csched.get(s_i, []):
                        fn(blk_)
                for dc, b_ in zip(dcs, bs_):
                    wo_add(dc, b_)
            if not noret and t + 1 < nt:
                state['stage'] = 'chunks'
                A('act', lambda e: e.activation(out=RbC[l][:], in_=Rst[l][:], func=AF.Copy), [('R', l)], [('RbC', l)])
            np_start()
            for dc in range(KC):
                b_ = wo_half(dc, 1)
                np_mm()
                wo_add(dc, b_)
                np_feed(dc)
            ring.release((t, l, 'w_o', 0))
            ring.release((t, l, 'w_o', 1))
            state['stage'] = 'post'

        for t in range(nt):
            ts = slice(t * TS, (t + 1) * TS)
            rec.add('sp', lambda e, ts=ts: e.dma_start(out=hT[:], in_=xT[:, :, ts].rearrange("k p s -> p k s")),
                    writes=[('hT', k) for k in range(KC)], dma_sem='x')
            np_start()
            for k_ in range(KC):
                np_feed(k_)
            for l in range(nl):
                if 'noffn' not in flags:
                    ffn(t, l, 0)
                else:
                    for u in range(11):
                        ring.get((t, l, 'w_gu1', u)); ring.release((t, l, 'w_gu1', u))
                    for u in range(8):
                        ring.get((t, l, 'w_d1', u)); ring.release((t, l, 'w_d1', u))
                if 'nomix' not in flags:
                    mixer(t, l)
                    if dbg is not None and t == 0 and l == 0:
                        allr = []
                        for s_ in range(40):
                            allr += br(s_)
                        rec.add('sp', lambda e: e.dma_start(out=dbg.rearrange("p (a b) -> p a b", a=40), in_=big[:, 0:40, :]), reads=allr, writes=['dbg'], dma_sem='dbg')
                else:
                    for name, nu in (('w_aq', 1), ('w_ak', 1), ('w_av', 1), ('w_r', 4), ('w_o', 2)):
                        for u in range(nu):
                            ring.get((t, l, name, u)); ring.release((t, l, name, u))
                if 'noffn' not in flags:
                    ffn(t, l, 1)
                else:
                    for u in range(11):
                        ring.get((t, l, 'w_gu2', u)); ring.release((t, l, 'w_gu2', u))
                    for u in range(8):
                        ring.get((t, l, 'w_d2', u)); ring.release((t, l, 'w_d2', u))
            np_finish(None, 0, dst_is_h=True)
            rec.add('sp', lambda e, ts=ts: e.dma_start(out=yT[:, :, ts].rearrange("k p s -> p k s"), in_=hT[:]),
                    reads=[('hT', k) for k in range(KC)], writes=[('yT', t), 'ysem'], dma_sem='y')
        rec.add('sp', None, reads=[('yT', t) for t in range(nt)] + (['dbg'] if dbg is not None else []))
        if use_scratch:
            rec.add('sp', None, reads=[('ssem', j) for j in range(4)])

        semkeys = rec.finalize()
        sems = {k: es.enter_context(nc.semaphore(f"s{n}")) for n, k in enumerate(semkeys)}
        with nc.Block() as block:
            @block.sync
            def _(e):
                rec.emit_engine('sp', e, sems)

            @block.gpsimd
            def _(e):
                rec.emit_engine('pool', e, sems)

            @block.tensor
            def _(e):
                rec.emit_engine('pe', e, sems)

            @block.vector
            def _(e):
                rec.emit_engine('dve', e, sems)

            @block.scalar
            def _(e):
                rec.emit_engine('act', e, sems)
    return nc


def make_in_maps(inputs, nl, nt, ncore):
    S = nt * TS
    w = prep_weights(inputs, nl)
    tabs = const_tables(nt)
    shared = dict(w)
    for k, v in tabs.items():
        if k != 'dec':
            shared[k] = v
    x = np.asarray(inputs['x'], dtype=np.float32)
    maps = []
    for c in range(ncore):
        m = dict(shared)
        m['xT'] = _c(x[c, :S, :].T).reshape(KC, 128, S)
        maps.append(m)
    return maps


def run(inputs, nl=NLAYER, nt=SEQ // TS, ncore=NCORE, flags=(), trace=False):
    inputs = {k: np.asarray(v) for k, v in inputs.items()}
    nc = build(nl, nt, flags=flags)
    maps = make_in_maps(inputs, nl, nt, ncore)
    res = run_bass_kernel_spmd(nc, maps, core_ids=list(range(ncore)), trace=trace)
    S = nt * TS
    out = np.stack([r['yT'].reshape(D, S).T for r in res.results], axis=0)
    return np.ascontiguousarray(out, dtype=np.float32), res


def kernel(**inputs):
    out, _ = run(inputs)
    return out
```

```python
import numpy as np
from contextlib import ExitStack
import concourse.bass as bass
import concourse.mybir as mybir
from concourse.bass_utils import run_bass_kernel_spmd

F32 = mybir.dt.float32
BF16 = mybir.dt.bfloat16
AF = mybir.ActivationFunctionType
ALU = mybir.AluOpType
AX = mybir.AxisListType

D = 1024
KC = 8
FF = 2816
FC = 22
TS = 512
NB = 4
NLAYER = 4
SEQ = 2048
NCORE = 8
NORM_EPS = 1e-6
GN_EPS = 1e-5
PAGE = 512
RING_CAP = 20480
NWSEM = 12
MASK_NEG = -2400.0


class Rec:
    def __init__(self):
        self.ops = []
        self.last_w = {}
        self.readers = {}

    def add(self, eng, emit, reads=(), writes=(), dma_sem=None):
        idx = len(self.ops)
        deps = set()
        for r in reads:
            w = self.last_w.get(r)
            if w is not None:
                deps.add(w)
        for r in writes:
            w = self.last_w.get(r)
            if w is not None:
                deps.add(w)
            rs = self.readers.get(r)
            if rs:
                deps.update(rs)
        for r in reads:
            self.readers.setdefault(r, []).append(idx)
        for r in writes:
            self.last_w[r] = idx
            self.readers[r] = []
        self.ops.append(dict(eng=eng, emit=emit, deps=deps, dma_sem=dma_sem, sig=False, val=None, semkey=None))
        return idx

    def finalize(self):
        ops = self.ops
        for op in ops:
            keep = set()
            for d in op['deps']:
                dop = ops[d]
                if (dop['dma_sem'] is None and op['dma_sem'] is None
                        and dop['eng'] == 'pe' and op['eng'] == 'pe'):
                    continue
                keep.add(d)
                dop['sig'] = True
            op['deps'] = keep
        cnt = {}
        for op in ops:
            if op['dma_sem'] is not None:
                key = ('dma', op['dma_sem'])
                cnt[key] = cnt.get(key, 0) + 16
                op['val'] = cnt[key]
                op['semkey'] = key
                op['sig'] = True
            elif op['sig']:
                key = ('eng', op['eng'])
                cnt[key] = cnt.get(key, 0) + 1
                op['val'] = cnt[key]
                op['semkey'] = key
        return sorted({op['semkey'] for op in ops if op['sig']}, key=str)

    def emit_engine(self, engname, eng, sems):
        waited = {}
        for op in self.ops:
            if op['eng'] != engname:
                continue
            need = {}
            for d in op['deps']:
                dop = self.ops[d]
                k = dop['semkey']
                if dop['val'] > need.get(k, 0):
                    need[k] = dop['val']
            for k, v in need.items():
                if v > waited.get(k, 0):
                    eng.wait_ge(sems[k], v)
                    waited[k] = v
            if op['emit'] is not None:
                ins = op['emit'](eng)
                if op['sig']:
                    ins.then_inc(sems[op['semkey']], 16 if op['dma_sem'] is not None else 1)


class Ring:
    def __init__(self, rec, ring_t, cap, sched, use_scratch):
        self.rec = rec
        self.t = ring_t
        self.cap = cap
        self.sched = sched
        self.next_load = 0
        self.next_use = 0
        self.live = []
        self.head = 0
        self.off = {}
        self.ndma = 0
        self.nscr = 0
        self.use_scratch = use_scratch

    def _try_alloc(self, size):
        if not self.live:
            self.head = size
            return 0
        tail = self.live[0][1]
        if self.head > tail or (self.head == tail and False):
            if self.head + size <= self.cap:
                off = self.head
            elif size < tail:
                off = 0
            else:
                return None
        else:
            if self.head + size < tail:
                off = self.head
            else:
                return None
        self.head = off + size
        return off

    def pump(self):
        while self.next_load < len(self.sched):
            u = self.sched[self.next_load]
            size = u['size']
            asz = ((size + PAGE - 1) // PAGE) * PAGE
            off = self._try_alloc(asz)
            if off is None:
                return
            self.live.append((u['key'], off, asz))
            self.off[u['key']] = off
            pages = [('rg', p) for p in range(off // PAGE, (off + asz) // PAGE)]
            si = self.ndma % NWSEM
            self.ndma += 1
            dst = self.t[:, off:off + size]
            if u['first'] or not self.use_scratch:
                src = u['src32']
                self.rec.add('pool', lambda e, dst=dst, src=src: e.dma_start(out=dst, in_=src),
                             writes=pages + [('wsem', si)], dma_sem=f'w{si}')
                if self.use_scratch:
                    sj = self.nscr % 4
                    self.nscr += 1
                    scr = u['scr']
                    self.rec.add('sp', lambda e, dst=dst, scr=scr: e.dma_start(out=scr, in_=dst),
                                 reads=pages, writes=[('scr', u['skey']), ('ssem', sj)], dma_sem=f'sc{sj}')
            else:
                scr = u['scr']
                self.rec.add('pool', lambda e, dst=dst, scr=scr: e.dma_start(out=dst, in_=scr),
                             reads=[('scr', u['skey'])], writes=pages + [('wsem', si)], dma_sem=f'w{si}')
            self.next_load += 1

    def get(self, key):
        u = self.sched[self.next_use]
        assert u['key'] == key, (u['key'], key)
        assert self.next_load > self.next_use, "ring too small: unit not loaded " + str(key)
        self.next_use += 1
        return self.off[key]

    def pages(self, key, a, b):
        off = self.off[key]
        return [('rg', p) for p in range((off + a) // PAGE, (off + b - 1) // PAGE + 1)]

    def release(self, key):
        assert self.live[0][0] == key, (self.live[0][0], key)
        self.live.pop(0)
        self.pump()


def _c(a):
    return np.ascontiguousarray(a, dtype=np.float32)


def prep_weights(inp, nl):
    L = nl
    out = {}

    def gu(wg, wu):
        g = wg[:L].reshape(L, 8, 128, 11, 2, 128)
        u_ = wu[:L].reshape(L, 8, 128, 11, 2, 128)
        st = np.stack([g, u_], axis=0)
        st = st.transpose(1, 4, 3, 5, 0, 2, 6)
        return _c(st).reshape(L, 11, 128, 4096)

    def dn(wd):
        d = wd[:L].reshape(L, 22, 128, 8, 128)
        d = d.transpose(0, 3, 2, 1, 4)
        return _c(d).reshape(L, 8, 128, 2816)

    out['w_gu1'] = gu(inp['ffn1_w_gate'], inp['ffn1_w_up'])
    out['w_d1'] = dn(inp['ffn1_w_down'])
    out['w_gu2'] = gu(inp['ffn2_w_gate'], inp['ffn2_w_up'])
    out['w_d2'] = dn(inp['ffn2_w_down'])
    win = inp['w_in'][:L].reshape(L, 8, 128, 2816)
    aq = win[..., 0:512].reshape(L, 8, 128, 8, 64)
    aqt = np.concatenate([aq[:, :, :, 0:4, :], aq[:, :, :, 4:8, :]], axis=-1)
    out['w_aq'] = _c(aqt.transpose(0, 2, 1, 3, 4)).reshape(L, 1, 128, 4096)
    ak = win[..., 512:640]
    akp = np.zeros((L, 8, 128, 2, 128), np.float32)
    akp[:, :, :, 0, 0:64] = ak[..., 0:64]
    akp[:, :, :, 1, 64:128] = ak[..., 64:128]
    out['w_ak'] = _c(akp.transpose(0, 2, 1, 3, 4)).reshape(L, 1, 128, 2048)
    out['w_av'] = _c(win[..., 640:768].transpose(0, 2, 1, 3)).reshape(L, 1, 128, 1024)
    r = win[..., 768:2816].reshape(L, 8, 128, 4, 512)
    out['w_r'] = _c(r.transpose(0, 3, 2, 1, 4)).reshape(L, 4, 128, 4096)
    wo = inp['w_out'][:L]
    wa = wo[:, 0:512].reshape(L, 8, 64, 1024)
    wat = np.concatenate([wa[:, 0:4], wa[:, 4:8]], axis=2)
    wr = wo[:, 512:1024].reshape(L, 4, 128, 1024)
    woc = np.concatenate([wat, wr], axis=1)
    woc = woc.reshape(L, 2, 4, 128, 1024).transpose(0, 1, 3, 2, 4)
    out['w_o'] = _c(woc).reshape(L, 2, 128, 4096)
    nrm = np.stack([inp['ffn1_norm'][:L], inp['mix_norm'][:L], inp['ffn2_norm'][:L]], axis=1)
    nrm = nrm.reshape(L, 3, 8, 128).transpose(3, 0, 1, 2)
    fin = inp['final_norm'].reshape(8, 128).T
    out['c_norm'] = _c(np.concatenate([nrm.reshape(128, L * 24), fin], axis=1))
    out['c_gnw'] = _c(inp['ret_gn_w'][:L].reshape(L * 4, 128).T)
    out['c_sink'] = _c(np.broadcast_to(inp['attn_sinks'][:L].reshape(1, L * 8), (128, L * 8)))
    return out


def const_tables(nt):
    S = nt * TS
    pos = np.arange(S, dtype=np.float32)
    inv_freq = (10000.0 ** (-np.arange(0, 128, 2, dtype=np.float32) / 128.0)).astype(np.float32)
    ang = pos[:, None] * inv_freq[None, :]
    cos = np.cos(ang).astype(np.float32).reshape(S // 128, 128, 64).transpose(1, 0, 2)
    sin = np.sin(ang).astype(np.float32).reshape(S // 128, 128, 64).transpose(1, 0, 2)
    t = {}
    t['c_cos'] = _c(cos).reshape(128, (S // 128) * 64)
    t['c_sin'] = _c(sin).reshape(128, (S // 128) * 64)
    H = 4
    lg = np.log(1.0 - 2.0 ** (-5.0 - np.arange(H, dtype=np.float32))).astype(np.float32)
    idx = np.arange(128, dtype=np.float32)
    dif = idx[:, None] - idx[None, :]
    dm = np.where(dif[None] >= 0, np.exp(np.maximum(dif, 0.0)[None] * lg[:, None, None]), 0.0)
    dmt = dm.transpose(2, 0, 1) * np.float32(128.0 ** -0.5)
    t['c_dmat'] = _c(dmt).reshape(128, 512)
    zeta = np.exp((127.0 - idx)[None, :] * lg[:, None]) * np.float32(128.0 ** -0.5)
    xi = np.exp((idx + 1.0)[None, :] * lg[:, None])
    t['c_xi'] = _c(np.broadcast_to(xi.reshape(1, 512), (128, 512)))
    dec = np.exp(128.0 * lg)
    t['c_zs'] = _c(zeta.T)
    t['dec'] = [float(v) for v in dec.astype(np.float32)]
    kk = np.arange(128)[:, None]
    qq = np.arange(128)[None, :]
    m0 = np.where(kk <= qq, 0.0, MASK_NEG)
    m1 = np.where(qq < kk, 0.0, MASK_NEG)
    t['c_mask'] = _c(np.stack([m0, m1], axis=1)).reshape(128, 256)
    t['c_ident'] = _c(np.eye(128))
    return t


WSPEC = [('w_gu1', 11, 4096), ('w_d1', 8, 2816), ('w_aq', 1, 4096), ('w_ak', 1, 2048), ('w_av', 1, 1024),
         ('w_r', 4, 4096), ('w_o', 2, 4096), ('w_gu2', 11, 4096), ('w_d2', 8, 2816)]


def build(nl, nt, first_layer_only_ffn=False, flags=()):
    nc = bass.Bass("TRN2", target_bir_lowering=False)
    S = nt * TS
    NBLK = S // 128
    tabs = const_tables(nt)
    dec = tabs['dec']
    dram = {}
    for name, nu, sz in WSPEC:
        dram[name] = nc.dram_tensor(name, [nl, nu, 128, sz], F32, kind="ExternalInput").ap()
    use_scratch = nt > 1
    scr = {}
    if use_scratch:
        for name, nu, sz in WSPEC:
            scr[name] = nc.dram_tensor("s_" + name, [nl, nu, 128, sz], BF16, kind="Internal").ap()
    cshape = {'c_norm': nl * 24 + 8, 'c_gnw': nl * 4, 'c_sink': nl * 8, 'c_cos': NBLK * 64, 'c_sin': NBLK * 64,
              'c_dmat': 512, 'c_xi': 512, 'c_zs': 4, 'c_mask': 256, 'c_ident': 128}
    for name, w in cshape.items():
        dram[name] = nc.dram_tensor(name, [128, w], F32, kind="ExternalInput").ap()
    xT = nc.dram_tensor("xT", [KC, 128, S], F32, kind="ExternalInput").ap()
    yT = nc.dram_tensor("yT", [KC, 128, S], F32, kind="ExternalOutput").ap()
    dbg = nc.dram_tensor("dbg", [128, 40 * TS], BF16, kind="ExternalOutput").ap() if 'dbg' in flags else None

    sched = []
    for t in range(nt):
        for l in range(nl):
            order = [('w_gu1', 11), ('w_d1', 8), ('w_aq', 1), ('w_ak', 1), ('w_av', 1), ('w_r', 4), ('w_o', 2),
                     ('w_gu2', 11), ('w_d2', 8)]
            for name, nu in order:
                sz = dict((n, s) for n, _, s in WSPEC)[name]
                for u in range(nu):
                    sched.append(dict(key=(t, l, name, u), skey=(l, name, u), size=sz, first=(t == 0),
                                      src32=dram[name][l, u], scr=(scr[name][l, u] if use_scratch else None)))

    with ExitStack() as es:
        def sb(name, shape, dt):
            return es.enter_context(nc.sbuf_tensor(name, shape, dt))

        rec = Rec()
        ring_t = sb("ring", [128, RING_CAP], BF16)
        ring = Ring(rec, ring_t, RING_CAP, sched, use_scratch)
        hT = sb("hT", [128, KC, TS], F32)
        nT = sb("nT", [128, KC, TS], BF16)
        big = sb("big", [128, 48, TS], BF16)
        sgt = [sb(f"sg{i}", [128, TS], F32) for i in range(2)]
        rstd = sb("rstd", [128, TS], F32)
        cn = sb("cn", [128, cshape['c_norm']], F32)
        gnw = sb("gnw", [128, nl * 4], F32)
        sink = sb("sink", [128, nl * 8], F32)
        sinke = sb("sinke", [128, nl * 8], F32)
        cos_t = sb("cos", [128, NBLK * 64], F32)
        sin_t = sb("sin", [128, NBLK * 64], F32)
        dmat = sb("dmat", [128, 512], F32)
        xi_t = sb("xi", [128, 512], F32)
        zs_t = sb("zs", [128, 4], F32)
        maskf = sb("maskf", [128, 256], F32)
        maskb = sb("maskb", [128, 256], BF16)
        identf = sb("identf", [128, 128], F32)
        ident = sb("ident", [128, 128], BF16)
        onesd = sb("onesd", [128, 128], BF16)
        ones = sb("ones", [128, 128], BF16)
        akP = [sb(f"akP{l}", [128, 2, 128], BF16) for l in range(nl)]
        avP = [sb(f"avP{l}", [128, 128], BF16) for l in range(nl)]
        Rst = [sb(f"R{l}", [128, 512], F32) for l in range(nl)]
        RbC = [sb(f"RbC{l}", [128, 512], BF16) for l in range(nl)]
        Rbt = [sb(f"Rbt{i}", [128, 512], BF16) for i in range(3)]
        akC = sb("akC", [128, 2, TS], BF16)
        skb = sb("skb", [128, 2, TS], BF16)
        avC = sb("avC", [128, NB, 128], BF16)
        rt = [sb(f"rt{i}", [128, 256], F32) for i in range(8)]
        Et = [sb(f"E{i}", [128, TS], BF16) for i in range(4)]
        Sd = [sb(f"Sd{i}", [128, TS], BF16) for i in range(4)]
        gs = sb("gs", [128, NB, TS], F32)
        tmpf = [sb(f"tmpf{i}", [128, TS], F32) for i in range(3)]
        ycb = [sb(f"ycb{i}", [128, TS], F32) for i in range(2)]
        stat = [sb(f"stat{i}", [128, 16], F32) for i in range(2)]
        rtok = [sb(f"rtok{i}", [128, TS], BF16) for i in range(2)]
        NPS = 5
        psf = [es.enter_context(nc.psum_tensor(f"ps{i}", [128, TS], F32)) for i in range(NPS + 1)]
        psbs = [es.enter_context(nc.psum_tensor(f"psb{i}", [128, 2 * TS], BF16)) for i in range(2)]

        state = dict(bank=0, cnt=0)
        def slot(s0, n):
            return big[:, s0:s0 + n, :]
        aT = lambda fc: big[:, fc, :]
        S_AQ, S_QTOK, S_KTOK, S_QT, S_QX, S_KT, S_VT, S_VZ, S_CAT, S_SQ = 0, 4, 8, 12, 16, 20, 24, 28, 32, 40
        SUBS = {}
        for s_ in range(4, 12):
            SUBS[s_] = ['lo', 'hi']
        for s_ in range(12, 24):
            SUBS[s_] = [0, 1, 2, 3]
        for s_ in range(32, 36):
            SUBS[s_] = [0, 1]
        for s_ in range(36, 40):
            SUBS[s_] = [0, 1, 2, 3]

        def bw(slot_, sub=None):
            if sub is None:
                return [('big', slot_)] + [('big', slot_, x) for x in SUBS.get(slot_, [])]
            return [('big', slot_), ('big', slot_, sub)]

        def br(slot_, sub=None):
            if sub is None:
                if slot_ in SUBS:
                    return [('big', slot_, x) for x in SUBS[slot_]]
                return [('big', slot_)]
            return [('big', slot_, sub)]


        def nb():
            b = state['bank']
            state['bank'] = (b + 1) % NPS
            return b

        xbuf = big[:, 24:40, :].bitcast(F32).rearrange("p (k a) b -> p k (a b)", a=2)
        oT = big[:, 0:16, :].bitcast(F32).rearrange("p (k a) b -> p k (a b)", a=2)

        def xb_w(k):
            return bw(24 + 2 * k) + bw(25 + 2 * k)

        def xb_r(k):
            return br(24 + 2 * k) + br(25 + 2 * k)

        def ot_w(k):
            return bw(2 * k) + bw(2 * k + 1)

        def ot_r(k):
            return br(2 * k) + br(2 * k + 1)

        disabled = {f[4:] for f in flags if f.startswith('off_')}
        state['stage'] = 'init'

        def A(eng, fn, reads=(), writes=()):
            if state['stage'] in disabled:
                return
            reads = list(reads)
            writes = list(writes)
            for r in reads:
                if isinstance(r, tuple) and r[0] in ('ps', 'psb') and r not in writes:
                    writes.append(r)
            rec.add(eng, fn, reads=reads, writes=writes)

        cl = [('c_norm', cn), ('c_gnw', gnw), ('c_sink', sink), ('c_cos', cos_t), ('c_sin', sin_t), ('c_dmat', dmat),
              ('c_xi', xi_t), ('c_zs', zs_t), ('c_mask', maskf), ('c_ident', identf)]
        for i, (name, tt) in enumerate(cl):
            rec.add('sp', lambda e, tt=tt, name=name: e.dma_start(out=tt[:], in_=dram[name]),
                    writes=[('c', name)], dma_sem=f'c{i}')
        A('dve', lambda e: e.tensor_copy(out=maskb[:], in_=maskf[:]), [('c', 'c_mask')], ['maskb'])
        A('dve', lambda e: e.tensor_copy(out=ident[:], in_=identf[:]), [('c', 'c_ident')], ['ident'])
        A('dve', lambda e: e.memset(onesd[:], 1.0 / D), [], ['onesd'])
        A('dve', lambda e: e.memset(ones[:], 1.0), [], ['ones'])
        A('act', lambda e: e.activation(out=sinke[:], in_=sink[:], func=AF.Exp), [('c', 'c_sink')], ['sinke'])
        for l in range(nl):
            A('dve', lambda e, l=l: e.memset(Rst[l][:], 0.0), [], [('R', l)])
            A('dve', lambda e, l=l: e.memset(RbC[l][:], 0.0), [], [('RbC', l)])
        ring.pump()

        npst = dict(pending=[], nmm=0)
        NB_STAT = NPS

        def np_start():
            npst['pending'] = []
            npst['nmm'] = 0

        def np_feed(k, from_x=False):
            if from_x:
                A('act', lambda e, k=k: e.activation(out=big[:, S_SQ + k, :], in_=xbuf[:, k, :], func=AF.Square),
                  xb_r(k), bw(S_SQ + k))
            else:
                A('act', lambda e, k=k: e.activation(out=big[:, S_SQ + k, :], in_=hT[:, k, :], func=AF.Square),
                  [('hT', k)], bw(S_SQ + k))
            npst['pending'].append(k)

        def np_mm():
            for k in npst['pending']:
                i = npst['nmm']
                A('pe', lambda e, k=k, i=i: e.matmul(psf[NB_STAT][:], onesd[:], big[:, S_SQ + k, :], start=(i == 0), stop=(i == KC - 1)),
                  [('big', S_SQ + k), 'onesd'], [('ps', NB_STAT)])
                npst['nmm'] += 1
            npst['pending'] = []

        def np_finish(l, which, dst_is_h=False, from_x=False):
            cbase = (l * 24 + which * 8) if l is not None else nl * 24
            np_mm()
            assert npst['nmm'] == KC
            b = NB_STAT
            A('act', lambda e, b=b: e.activation(out=rstd[:], in_=psf[b][:], func=AF.Ln, bias=NORM_EPS, scale=1.0),
              [('ps', b)], ['rstd'])
            A('act', lambda e: e.activation(out=rstd[:], in_=rstd[:], func=AF.Exp, scale=-0.5), ['rstd'], ['rstd'])
            for k in range(KC):
                if dst_is_h:
                    A('dve', lambda e, k=k: e.scalar_tensor_tensor(out=oT[:, k, :], in0=hT[:, k, :], scalar=cn[:, cbase + k:cbase + k + 1],
                                                                    in1=rstd[:], op0=ALU.mult, op1=ALU.mult),
                      [('hT', k), 'rstd', ('c', 'c_norm')], ot_w(k))
                elif from_x:
                    A('dve', lambda e, k=k: e.scalar_tensor_tensor(out=nT[:, k, :], in0=xbuf[:, k, :], scalar=cn[:, cbase + k:cbase + k + 1],
                                                                    in1=rstd[:], op0=ALU.mult, op1=ALU.mult),
                      xb_r(k) + ['rstd', ('c', 'c_norm')], [('nT', k)])
                else:
                    A('dve', lambda e, k=k: e.scalar_tensor_tensor(out=nT[:, k, :], in0=hT[:, k, :], scalar=cn[:, cbase + k:cbase + k + 1],
                                                                    in1=rstd[:], op0=ALU.mult, op1=ALU.mult),
                      [('hT', k), 'rstd', ('c', 'c_norm')], [('nT', k)])

        def ffn(t, l, which):
            gname = 'w_gu1' if which == 0 else 'w_gu2'
            dname = 'w_d1' if which == 0 else 'w_d2'
            from_x = (l == 0 and which == 0)
            np_finish(l, 0 if which == 0 else 2, from_x=from_x)
            for u in range(11):
                key = (t, l, gname, u)
                off = ring.get(key)
                if u == 0:
                    banks0 = [[nb(), nb()], [nb(), nb()]]
                    for k in range(KC):
                        for fcl in range(2):
                            for gi in range(2):
                                b = banks0[fcl][gi]
                                a = ((fcl * 2 + gi) * 8 + k) * 128
                                A('pe', lambda e, b=b, a=a, k=k, off=off: e.matmul(psf[b][:], ring_t[:, off + a:off + a + 128], nT[:, k, :],
                                                                                     start=(k == 0), stop=(k == KC - 1)),
                                  ring.pages(key, a, a + 128) + [('nT', k)], [('ps', b)])
                for fcl in range(2):
                    fc = 2 * u + fcl
                    if u == 0:
                        bg, bu = banks0[fcl]
                    else:
                        bg, bu = nb(), nb()
                        for gi, b in ((0, bg), (1, bu)):
                            for k in range(KC):
                                a = ((fcl * 2 + gi) * 8 + k) * 128
                                A('pe', lambda e, b=b, a=a, k=k, off=off: e.matmul(psf[b][:], ring_t[:, off + a:off + a + 128], nT[:, k, :],
                                                                                     start=(k == 0), stop=(k == KC - 1)),
                                  ring.pages(key, a, a + 128) + [('nT', k)], [('ps', b)])
                    sg = sgt[fc % 2]
                    A('act', lambda e, sg=sg, bg=bg: e.activation(out=sg[:], in_=psf[bg][:], func=AF.Silu),
                      [('ps', bg)], [('sg', fc % 2)])
                    A('dve', lambda e, sg=sg, bu=bu, fc=fc: e.tensor_tensor(out=big[:, fc, :], in0=psf[bu][:], in1=sg[:], op=ALU.mult),
                      [('ps', bu), ('sg', fc % 2)], bw(fc))
                ring.release(key)
            np_start()
            for dc in range(KC):
                key = (t, l, dname, dc)
                off = ring.get(key)
                b = nb()
                for fc in range(FC):
                    a = fc * 128
                    A('pe', lambda e, b=b, a=a, fc=fc, off=off: e.matmul(psf[b][:], ring_t[:, off + a:off + a + 128], big[:, fc, :],
                                                                           start=(fc == 0), stop=(fc == FC - 1)),
                      ring.pages(key, a, a + 128) + [('big', fc)], [('ps', b)])
                np_mm()
                if from_x:
                    A('dve', lambda e, b=b, dc=dc: e.scalar_tensor_tensor(out=hT[:, dc, :], in0=psf[b][:], scalar=0.5, in1=xbuf[:, dc, :],
                                                                           op0=ALU.mult, op1=ALU.add),
                      [('ps', b)] + xb_r(dc), [('hT', dc)])
                else:
                    A('dve', lambda e, b=b, dc=dc: e.scalar_tensor_tensor(out=hT[:, dc, :], in0=psf[b][:], scalar=0.5, in1=hT[:, dc, :],
                                                                           op0=ALU.mult, op1=ALU.add),
                      [('ps', b), ('hT', dc)], [('hT', dc)])
                np_feed(dc)
                ring.release(key)

        def mixer(t, l):
            state['stage'] = 'mnorm'
            np_finish(l, 1)
            G0 = t * NB
            noret = 'noret' in flags
            state['stage'] = 'aq'
            key = (t, l, 'w_aq', 0)
            off = ring.get(key)
            banksq = [nb() for j in range(4)]
            for k in range(KC):
                for j in range(4):
                    b = banksq[j]
                    a = (k * 4 + j) * 128
                    A('pe', lambda e, b=b, a=a, k=k, off=off: e.matmul(psf[b][:], ring_t[:, off + a:off + a + 128], nT[:, k, :],
                                                                         start=(k == 0), stop=(k == KC - 1)),
                      ring.pages(key, a, a + 128) + [('nT', k)], [('ps', b)])
            for j in range(4):
                b = banksq[j]
                A('act', lambda e, b=b, j=j: e.activation(out=big[:, S_AQ + j, :], in_=psf[b][:], func=AF.Copy),
                  [('ps', b)], bw(S_AQ + j))
            ring.release(key)
            state['stage'] = 'ak'
            key = (t, l, 'w_ak', 0)
            off = ring.get(key)
            for g in range(2):
                b = nb()
                for k in range(KC):
                    a = (k * 2 + g) * 128
                    A('pe', lambda e, b=b, a=a, k=k, off=off: e.matmul(psf[b][:], ring_t[:, off + a:off + a + 128], nT[:, k, :],
                                                                         start=(k == 0), stop=(k == KC - 1)),
                      ring.pages(key, a, a + 128) + [('nT', k)], [('ps', b)])
                A('act', lambda e, b=b, g=g: e.activation(out=akC[:, g, :], in_=psf[b][:], func=AF.Copy),
                  [('ps', b)], [('akC', g)])
            ring.release(key)
            state['stage'] = 'av'
            key = (t, l, 'w_av', 0)
            off = ring.get(key)
            b = nb()
            for blk in range(NB):
                for k in range(KC):
                    a = k * 128
                    A('pe', lambda e, b=b, a=a, k=k, blk=blk, off=off: e.matmul(psf[b][:, blk * 128:(blk + 1) * 128], nT[:, k, blk * 128:(blk + 1) * 128],
                                                                                  ring_t[:, off + a:off + a + 128], start=(k == 0), stop=(k == KC - 1)),
                      ring.pages(key, a, a + 128) + [('nT', k)], [('ps', b)])
            A('act', lambda e, b=b: e.activation(out=avC[:].rearrange("p a b -> p (a b)"), in_=psf[b][:], func=AF.Copy),
              [('ps', b)], ['avC'])
            ring.release(key)

            state['stage'] = 'att'
            items = [(g, qb) for qb in range(NB) for g in range(2)]
            for g_ in range(2):
                A('dve', lambda e, g_=g_: e.tensor_scalar(out=skb[:, g_, :].rearrange("p (a b) -> p a b", a=4),
                                                          in0=sinke[:, l * 8 + 4 * g_:l * 8 + 4 * g_ + 4].unsqueeze(2).to_broadcast([128, 4, 128]),
                                                          scalar1=1.0 / 128, scalar2=None, op0=ALU.mult),
                  ['sinke'], [('skb', g_)])

            def att_front(g, qb, ei):
                state['stage'] = 'att'
                G = G0 + qb
                kbs = []
                if G > 0:
                    kbs.append(1)
                kbs.append(0)
                outs = []
                for kind in kbs:
                    b = nb()
                    if kind == 0:
                        lk = akC[:, g, qb * 128:(qb + 1) * 128]
                        rk = [('akC', g)]
                    elif qb == 0:
                        lk = akP[l][:, g, :]
                        rk = [('akP', l)]
                    else:
                        lk = akC[:, g, (qb - 1) * 128:qb * 128]
                        rk = [('akC', g)]
                    rq = big[:, S_AQ:S_AQ + 4, qb * 128:(qb + 1) * 128]
                    A('pe', lambda e, b=b, lk=lk, rq=rq: e.matmul(psf[b][:], lk, rq, start=True, stop=False),
                      rk + [('big', S_AQ + j) for j in range(4)], [('ps', b)])
                    mb = maskb[:, kind * 128:(kind + 1) * 128].unsqueeze(1).to_broadcast([128, 4, 128])
                    A('pe', lambda e, b=b, mb=mb: e.matmul(psf[b][:], ident[:], mb, start=False, stop=True),
                      ['maskb', 'ident'], [('ps', b)])
                    ee = ei[0] % 4
                    ei[0] += 1
                    A('act', lambda e, b=b, ee=ee: e.activation(out=Et[ee][:], in_=psf[b][:], func=AF.Exp, scale=0.125),
                      [('ps', b)], [('E', ee)])
                    outs.append((kind, ee))
                return outs

            def att_back(g, qb, outs):
                state['stage'] = 'att'
                bpv, bden = nb(), nb()
                n = len(outs)
                for i, (kind, ee) in enumerate(outs):
                    if kind == 0:
                        lv = avC[:, qb, :]
                        rv = ['avC']
                    elif qb == 0:
                        lv = avP[l][:]
                        rv = [('avP', l)]
                    else:
                        lv = avC[:, qb - 1, :]
                        rv = ['avC']
                    A('pe', lambda e, lv=lv, ee=ee, i=i, n=n, bpv=bpv: e.matmul(psf[bpv][:], lv, Et[ee][:], start=(i == 0), stop=(i == n - 1)),
                      rv + [('E', ee)], [('ps', bpv)])
                for i, (kind, ee) in enumerate(outs):
                    A('pe', lambda e, ee=ee, i=i, n=n, bden=bden: e.matmul(psf[bden][:], ones[:], Et[ee][:], start=(i == 0), stop=False),
                      ['ones', ('E', ee)], [('ps', bden)])
                A('pe', lambda e, bden=bden: e.matmul(psf[bden][:], ones[:], skb[:, g, :], start=False, stop=True),
                  ['ones', ('skb', g)], [('ps', bden)])
                r0, r1 = g * 64, (g + 1) * 64
                tf = tmpf[(qb * 2 + g) % 2]
                tfk = ('tmpf', (qb * 2 + g) % 2)
                A('act', lambda e, tf=tf, bden=bden: e.activation(out=tf[r0:r1, :], in_=psf[bden][r0:r1, :], func=AF.Ln), [('ps', bden)], [tfk])
                A('act', lambda e, tf=tf: e.activation(out=tf[r0:r1, :], in_=tf[r0:r1, :], func=AF.Exp, scale=-1.0), [tfk], [tfk])
                A('dve', lambda e, tf=tf, bpv=bpv: e.tensor_tensor(out=big[r0:r1, S_CAT:S_CAT + 4, qb * 128:(qb + 1) * 128],
                                                                    in0=psf[bpv][r0:r1, :].rearrange("p (a b) -> p a b", a=4),
                                                                    in1=tf[r0:r1, :].rearrange("p (a b) -> p a b", a=4), op=ALU.mult),
                  [('ps', bpv), tfk], sum([bw(S_CAT + j, g) for j in range(4)], []))

            def tm_block(u, off, blk):
                state['stage'] = f'rp{u}'
                key = (t, l, 'w_r', u)
                b = nb()
                for k in range(KC):
                    a = k * 512
                    A('pe', lambda e, b=b, a=a, k=k, blk=blk, off=off: e.matmul(psf[b][:], nT[:, k, blk * 128:(blk + 1) * 128],
                                                                                  ring_t[:, off + a:off + a + 512], start=(k == 0), stop=(k == KC - 1)),
                      ring.pages(key, a, a + 512) + [('nT', k)], [('ps', b)])
                return b

            def rot_front(dst_slot, blk, b, ri):
                state['stage'] = 'rot'
                G = G0 + blk
                X = psf[b][:].rearrange("p (h two e) -> p h two e", h=4, two=2)
                x1, x2 = X[:, :, 0, :], X[:, :, 1, :]
                cb = cos_t[:, G * 64:(G + 1) * 64].unsqueeze(1).to_broadcast([128, 4, 64])
                sn = sin_t[:, G * 64:(G + 1) * 64].unsqueeze(1).to_broadcast([128, 4, 64])
                r3 = [rt[ri * 4 + i][:].rearrange("p (h e) -> p h e", h=4) for i in range(4)]
                rk_ = [('rt', ri * 4 + i) for i in range(4)]
                O = big[:, dst_slot + blk, :].rearrange("p (h two e) -> p h two e", h=4, two=2)
                cst = [('c', 'c_cos'), ('c', 'c_sin')]
                A('dve', lambda e: e.tensor_tensor(out=r3[0], in0=x1, in1=cb, op=ALU.mult), [('ps', b)] + cst, [rk_[0]])
                A('dve', lambda e: e.tensor_tensor(out=r3[1], in0=x2, in1=sn, op=ALU.mult), [('ps', b)] + cst, [rk_[1]])
                A('dve', lambda e: e.tensor_tensor(out=r3[2], in0=x1, in1=sn, op=ALU.mult), [('ps', b)] + cst, [rk_[2]])
                A('dve', lambda e: e.tensor_tensor(out=r3[3], in0=x2, in1=cb, op=ALU.mult), [('ps', b)] + cst, [rk_[3]])
                A('dve', lambda e: e.tensor_tensor(out=O[:, :, 0, :], in0=r3[0], in1=r3[1], op=ALU.subtract),
                  [rk_[0], rk_[1]], bw(dst_slot + blk, 'lo'))
                A('dve', lambda e: e.tensor_tensor(out=O[:, :, 1, :], in0=r3[2], in1=r3[3], op=ALU.add),
                  [rk_[2], rk_[3]], bw(dst_slot + blk, 'hi'))

            def rot_back(dst_slot, t_slot, x_slot, blk):
                state['stage'] = 'tr'
                hb = state['cnt'] % 2
                state['cnt'] += 1
                for h in range(4):
                    A('pe', lambda e, h=h, hb=hb: e.transpose(psbs[hb][:, h * 128:(h + 1) * 128],
                                                              big[:, dst_slot + blk, h * 128:(h + 1) * 128], ident[:]),
                      [('big', dst_slot + blk, 'lo'), ('big', dst_slot + blk, 'hi'), 'ident'], [('psb', hb)])
                pv = psbs[hb][:, 0:512].rearrange("p (h i) -> p h i", h=4)
                A('act', lambda e, pv=pv: e.activation(out=big[:, t_slot:t_slot + 4, blk * 128:(blk + 1) * 128], in_=pv, func=AF.Copy),
                  [('psb', hb)], sum([bw(t_slot + h, blk) for h in range(4)], []))
                if x_slot is not None:
                    A('dve', lambda e, pv=pv: e.tensor_tensor(out=big[:, x_slot:x_slot + 4, blk * 128:(blk + 1) * 128], in0=pv,
                                                               in1=xi_t[:].rearrange("p (h i) -> p h i", h=4), op=ALU.mult),
                      [('psb', hb), ('c', 'c_xi')], sum([bw(x_slot + h, blk) for h in range(4)], []))

            def v_post(blk, b):
                state['stage'] = 'vpost'
                A('act', lambda e: e.activation(out=big[:, S_VT + blk, :], in_=psf[b][:], func=AF.Copy),
                  [('ps', b)], bw(S_VT + blk))
                A('dve', lambda e: e.tensor_tensor(out=big[:, S_VZ + blk, :].rearrange("p (h v) -> p h v", h=4),
                                                    in0=psf[b][:].rearrange("p (h v) -> p h v", h=4),
                                                    in1=zs_t[:].unsqueeze(2).to_broadcast([128, 4, 128]), op=ALU.mult),
                  [('ps', b), ('c', 'c_zs')], bw(S_VZ + blk))

            def g_post(blk, b):
                state['stage'] = 'gpost'
                A('act', lambda e: e.activation(out=gs[:, blk, :], in_=psf[b][:], func=AF.Silu), [('ps', b)], [('gs', blk)])

            state['stage'] = 'chunks'
            def rb_before(blk):
                return (RbC[l], ('RbC', l)) if blk == 0 else (Rbt[blk - 1], ('Rbt', blk - 1))

            def ch_scores(blk):
                state['stage'] = 'chunks'
                cs = slice(blk * 128, (blk + 1) * 128)
                bs = nb()
                for h in range(4):
                    A('pe', lambda e, h=h, bs=bs: e.matmul(psf[bs][:, h * 128:(h + 1) * 128], big[:, S_KT + h, cs], big[:, S_QT + h, cs],
                                                           start=True, stop=True),
                      [('big', S_KT + h, blk), ('big', S_QT + h, blk)], [('ps', bs)])
                sd = Sd[blk]
                A('dve', lambda e, bs=bs, sd=sd: e.tensor_tensor(out=sd[:], in0=psf[bs][:], in1=dmat[:], op=ALU.mult),
                  [('ps', bs), ('c', 'c_dmat')], [('Sd', blk)])

            def ch_state(blk):
                state['stage'] = 'chunks'
                bk = nb()
                for h in range(4):
                    hs = slice(h * 128, (h + 1) * 128)
                    A('pe', lambda e, h=h, hs=hs, bk=bk: e.matmul(psf[bk][:, hs], big[:, S_KTOK + blk, hs], big[:, S_VZ + blk, hs], start=True, stop=True),
                      [('big', S_KTOK + blk, 'lo'), ('big', S_KTOK + blk, 'hi'), ('big', S_VZ + blk)], [('ps', bk)])
                for h in range(4):
                    hs = slice(h * 128, (h + 1) * 128)
                    A('dve', lambda e, h=h, hs=hs, bk=bk: e.scalar_tensor_tensor(out=Rst[l][:, hs], in0=Rst[l][:, hs], scalar=dec[h], in1=psf[bk][:, hs],
                                                                                   op0=ALU.mult, op1=ALU.add),
                      [('R', l), ('ps', bk)], [('R', l)])
                if blk < NB - 1:
                    A('act', lambda e: e.activation(out=Rbt[blk][:], in_=Rst[l][:], func=AF.Copy), [('R', l)], [('Rbt', blk)])

            def ch_p1(blk):
                state['stage'] = 'chunks'
                n = G0 + blk
                cs = slice(blk * 128, (blk + 1) * 128)
                sd = Sd[blk]
                by = nb()
                rbp, rbk = rb_before(blk)
                for h in range(4):
                    hs = slice(h * 128, (h + 1) * 128)
                    A('pe', lambda e, h=h, hs=hs, by=by, sd=sd: e.matmul(psf[by][:, hs], sd[:, hs], big[:, S_VT + blk, hs], start=True, stop=False),
                      [('Sd', blk), ('big', S_VT + blk)], [('ps', by)])
                    A('pe', lambda e, h=h, hs=hs, by=by, rbp=rbp: e.matmul(psf[by][:, hs], big[:, S_QX + h, cs], rbp[:, hs], start=False, stop=True),
                      [('big', S_QX + h, blk), rbk], [('ps', by)])
                st = stat[n % 2]
                sk = ('stat', n % 2)
                y3 = psf[by][:].rearrange("p (h v) -> p h v", h=4)
                yc = ycb[n % 2]
                yk = ('ycb', n % 2)
                yc3 = yc[:].rearrange("p (h v) -> p h v", h=4)
                ysq = tmpf[2]
                A('dve', lambda e, st=st, y3=y3: e.tensor_reduce(out=st[:, 0:4], in_=y3, axis=AX.X, op=ALU.add), [('ps', by)], [sk])
                A('dve', lambda e, st=st: e.tensor_scalar(out=st[:, 0:4], in0=st[:, 0:4], scalar1=1.0 / 128, scalar2=None, op0=ALU.mult), [sk], [sk])
                A('dve', lambda e, st=st, y3=y3, yc3=yc3: e.tensor_tensor(out=yc3, in0=y3, in1=st[:, 0:4].unsqueeze(2).to_broadcast([128, 4, 128]), op=ALU.subtract),
                  [('ps', by), sk], [yk])
                A('act', lambda e, yc=yc: e.activation(out=ysq[:], in_=yc[:], func=AF.Square), [yk], [('tmpf', 2)])
                A('dve', lambda e, st=st: e.tensor_reduce(out=st[:, 4:8], in_=ysq[:].rearrange("p (h v) -> p h v", h=4), axis=AX.X, op=ALU.add),
                  [('tmpf', 2)], [sk])

            def ch_p2(blk):
                state['stage'] = 'chunks'
                n = G0 + blk
                st = stat[n % 2]
                sk = ('stat', n % 2)
                A('act', lambda e, st=st: e.activation(out=st[:, 4:8], in_=st[:, 4:8], func=AF.Ln, bias=GN_EPS, scale=1.0 / 128), [sk], [sk])
                A('act', lambda e, st=st: e.activation(out=st[:, 4:8], in_=st[:, 4:8], func=AF.Exp, scale=-0.5), [sk], [sk])

            def ch_p3(blk):
                state['stage'] = 'chunks'
                n = G0 + blk
                st = stat[n % 2]
                sk = ('stat', n % 2)
                yc = ycb[n % 2]
                yk = ('ycb', n % 2)
                yc3 = yc[:].rearrange("p (h v) -> p h v", h=4)
                A('dve', lambda e, st=st, yc3=yc3: e.tensor_tensor(out=yc3, in0=yc3, in1=st[:, 4:8].unsqueeze(2).to_broadcast([128, 4, 128]), op=ALU.mult),
                  [yk, sk], [yk])
                ro = rtok[n % 2]
                A('dve', lambda e, ro=ro, yc=yc: e.tensor_tensor(out=ro[:], in0=yc[:], in1=gs[:, blk, :], op=ALU.mult),
                  [yk, ('gs', blk)], [('rtok', n % 2)])

            def ch_tr(blk):
                state['stage'] = 'chunks'
                n = G0 + blk
                cs = slice(blk * 128, (blk + 1) * 128)
                ro = rtok[n % 2]
                hb = state['cnt'] % 2
                state['cnt'] += 1
                for h in range(4):
                    A('pe', lambda e, h=h, hb=hb, ro=ro: e.transpose(psbs[hb][:, h * 128:(h + 1) * 128], ro[:, h * 128:(h + 1) * 128], ident[:]),
                      [('rtok', n % 2), 'ident'], [('psb', hb)])
                for h in range(4):
                    A('act', lambda e, h=h, hb=hb: e.activation(out=big[:, S_CAT + 4 + h, cs], in_=psbs[hb][:, h * 128:(h + 1) * 128], func=AF.Copy,
                                                                scale=gnw[:, l * 4 + h:l * 4 + h + 1]),
                      [('psb', hb), ('c', 'c_gnw')], bw(S_CAT + 4 + h, blk))

            ei = [0]
            pend = None
            pend_tr = []
            roff = {}
            if noret:
                for u in range(4):
                    ring.get((t, l, 'w_r', u)); ring.release((t, l, 'w_r', u))
                for h in range(4):
                    A('dve', lambda e, h=h: e.memset(big[:, S_CAT + 4 + h, :], 0.0), [], bw(S_CAT + 4 + h))
            if 'noatt' in flags:
                for j in range(4):
                    A('dve', lambda e, j=j: e.memset(big[:, S_CAT + j, :], 0.0), [], bw(S_CAT + j))
            csched = {
                4: [(ch_scores, 0), (ch_scores, 1)],
                5: [(ch_scores, 2), (ch_scores, 3), (ch_state, 0), (ch_p1, 0)],
                6: [(ch_state, 1), (ch_p2, 0), (ch_p1, 1)],
                7: [(ch_state, 2), (ch_p3, 0), (ch_p2, 1), (ch_p1, 2)],
                8: [(ch_state, 3), (ch_tr, 0), (ch_p3, 1), (ch_p2, 2), (ch_p1, 3)],
                9: [(ch_tr, 1), (ch_p3, 2), (ch_p2, 3)],
                10: [(ch_tr, 2), (ch_p3, 3)],
                11: [(ch_tr, 3)],
            }
            for s_i, (g, qb) in enumerate(items):
                if not noret and s_i == 0:
                    roff[0] = ring.get((t, l, 'w_r', 0))
                    roff[1] = ring.get((t, l, 'w_r', 1))
                if not noret and s_i == 4:
                    roff[2] = ring.get((t, l, 'w_r', 2))
                    roff[3] = ring.get((t, l, 'w_r', 3))
                outs = None
                if 'noatt' not in flags:
                    outs = att_front(g, qb, ei)
                new_tr = []
                if not noret:
                    if s_i < 4:
                        blk = s_i
                        bq = tm_block(0, roff[0], blk)
                        rot_front(S_QTOK, blk, bq, 0)
                        bk_ = tm_block(1, roff[1], blk)
                        rot_front(S_KTOK, blk, bk_, 1)
                        new_tr = [(S_QTOK, S_QT, S_QX, blk), (S_KTOK, S_KT, None, blk)]
                    else:
                        blk = s_i - 4
                        bv = tm_block(2, roff[2], blk)
                        v_post(blk, bv)
                        bg_ = tm_block(3, roff[3], blk)
                        g_post(blk, bg_)
                if 'noatt' not in flags:
                    if pend is not None:
                        att_back(*pend)
                    pend = (g, qb, outs)
                for tr in pend_tr:
                    rot_back(*tr)
                pend_tr = new_tr
                if not noret:
                    for fn, blk_ in csched.get(s_i, []):
                        fn(blk_)
                if not noret and s_i == 3:
                    ring.release((t, l, 'w_r', 0))
                    ring.release((t, l, 'w_r', 1))
                if not noret and s_i == 7:
                    ring.release((t, l, 'w_r', 2))
                    ring.release((t, l, 'w_r', 3))
            if pend is not None:
                att_back(*pend)
            for tr in pend_tr:
                rot_back(*tr)
            state['stage'] = 'att'
            if t + 1 < nt:
                A('act', lambda e: e.activation(out=akP[l][:], in_=akC[:, :, (NB - 1) * 128:NB * 128], func=AF.Copy),
                  [('akC', 0), ('akC', 1)], [('akP', l)])
                A('act', lambda e: e.activation(out=avP[l][:], in_=avC[:, NB - 1, :], func=AF.Copy),
                  ['avC'], [('avP', l)])

            offs = {}
            for u in range(2):
                offs[u] = ring.get((t, l, 'w_o', u))
            cat_reads = {}
            for cc in range(8):
                if cc < 4:
                    cat_reads[cc] = [('big', S_CAT + cc, 0), ('big', S_CAT + cc, 1)]
                else:
                    cat_reads[cc] = [('big', S_CAT + cc, blk) for blk in range(NB)]

            def wo_half(dc, half):
                state['stage'] = 'wo'
                b = nb()
                for cc in range(half * 4, half * 4 + 4):
                    u, ccl = cc // 4, cc % 4
                    key = (t, l, 'w_o', u)
                    a = ccl * 1024 + dc * 128
                    A('pe', lambda e, b=b, a=a, cc=cc, u=u: e.matmul(psf[b][:], ring_t[:, offs[u] + a:offs[u] + a + 128], big[:, S_CAT + cc, :],
                                                                       start=(cc % 4 == 0), stop=(cc % 4 == 3)),
                      ring.pages(key, a, a + 128) + cat_reads[cc], [('ps', b)])
                return b

            def wo_add(dc, b):
                state['stage'] = 'wo'
                A('dve', lambda e, b=b, dc=dc: e.tensor_tensor(out=hT[:, dc, :], in0=psf[b][:], in1=hT[:, dc, :], op=ALU.add),
                  [('ps', b), ('hT', dc)], [('hT', dc)])

            for s_i in range(8, 12):
                dcs = [2 * (s_i - 8), 2 * (s_i - 8) + 1]
                bs_ = [wo_half(dc, 0) for dc in dcs]
                if not noret:
                    for fn, blk_ in csched.get(s_i, []):
                        fn(blk_)
                for dc, b_ in zip(dcs, bs_):
                    wo_add(dc, b_)
            if not noret and t + 1 < nt:
                state['stage'] = 'chunks'
                A('act', lambda e: e.activation(out=RbC[l][:], in_=Rst[l][:], func=AF.Copy), [('R', l)], [('RbC', l)])
            np_start()
            for dc in range(KC):
                b_ = wo_half(dc, 1)
                np_mm()
                wo_add(dc, b_)
                np_feed(dc)
            ring.release((t, l, 'w_o', 0))
            ring.release((t, l, 'w_o', 1))
            state['stage'] = 'post'

        def load_x(t):
            ts = slice(t * TS, (t + 1) * TS)
            rec.add('sp', lambda e, ts=ts: e.dma_start(out=xbuf, in_=xT[:, :, ts].rearrange("k p s -> p k s")),
                    writes=sum([xb_w(k) for k in range(KC)], []) + ['xsem'], dma_sem='x')

        load_x(0)
        for t in range(nt):
            ts = slice(t * TS, (t + 1) * TS)
            np_start()
            for k_ in range(KC):
                np_feed(k_, from_x=True)
            if 'noffn' in flags:
                for k_ in range(KC):
                    A('dve', lambda e, k_=k_: e.tensor_copy(out=hT[:, k_, :], in_=xbuf[:, k_, :]), xb_r(k_), [('hT', k_)])
            for l in range(nl):
                if 'noffn' not in flags:
                    ffn(t, l, 0)
                else:
                    for u in range(11):
                        ring.get((t, l, 'w_gu1', u)); ring.release((t, l, 'w_gu1', u))
                    for u in range(8):
                        ring.get((t, l, 'w_d1', u)); ring.release((t, l, 'w_d1', u))
                if 'nomix' not in flags:
                    mixer(t, l)
                    if dbg is not None and t == 0 and l == 0:
                        allr = []
                        for s_ in range(40):
                            allr += br(s_)
                        rec.add('sp', lambda e: e.dma_start(out=dbg.rearrange("p (a b) -> p a b", a=40), in_=big[:, 0:40, :]), reads=allr, writes=['dbg'], dma_sem='dbg')
                else:
                    for name, nu in (('w_aq', 1), ('w_ak', 1), ('w_av', 1), ('w_r', 4), ('w_o', 2)):
                        for u in range(nu):
                            ring.get((t, l, name, u)); ring.release((t, l, name, u))
                if l == nl - 1 and t + 1 < nt:
                    load_x(t + 1)
                if 'noffn' not in flags:
                    ffn(t, l, 1)
                else:
                    for u in range(11):
                        ring.get((t, l, 'w_gu2', u)); ring.release((t, l, 'w_gu2', u))
                    for u in range(8):
                        ring.get((t, l, 'w_d2', u)); ring.release((t, l, 'w_d2', u))
            np_finish(None, 0, dst_is_h=True)
            rec.add('sp', lambda e, ts=ts: e.dma_start(out=yT[:, :, ts].rearrange("k p s -> p k s"), in_=oT),
                    reads=sum([ot_r(k) for k in range(KC)], []), writes=[('yT', t), 'ysem'], dma_sem='y')
        rec.add('sp', None, reads=[('yT', t) for t in range(nt)] + (['dbg'] if dbg is not None else []))
        if use_scratch:
            rec.add('sp', None, reads=[('ssem', j) for j in range(4)])

        semkeys = rec.finalize()
        sems = {k: es.enter_context(nc.semaphore(f"s{n}")) for n, k in enumerate(semkeys)}
        with nc.Block() as block:
            @block.sync
            def _(e):
                rec.emit_engine('sp', e, sems)

            @block.gpsimd
            def _(e):
                rec.emit_engine('pool', e, sems)

            @block.tensor
            def _(e):
                rec.emit_engine('pe', e, sems)

            @block.vector
            def _(e):
                rec.emit_engine('dve', e, sems)

            @block.scalar
            def _(e):
                rec.emit_engine('act', e, sems)
    return nc


def make_in_maps(inputs, nl, nt, ncore):
    S = nt * TS
    w = prep_weights(inputs, nl)
    tabs = const_tables(nt)
    shared = dict(w)
    for k, v in tabs.items():
        if k != 'dec':
            shared[k] = v
    x = np.asarray(inputs['x'], dtype=np.float32)
    maps = []
    for c in range(ncore):
        m = dict(shared)
        m['xT'] = _c(x[c, :S, :].T).reshape(KC, 128, S)
        maps.append(m)
    return maps


def run(inputs, nl=NLAYER, nt=SEQ // TS, ncore=NCORE, flags=(), trace=False):
    inputs = {k: np.asarray(v) for k, v in inputs.items()}
    nc = build(nl, nt, flags=flags)
    maps = make_in_maps(inputs, nl, nt, ncore)
    res = run_bass_kernel_spmd(nc, maps, core_ids=list(range(ncore)), trace=trace)
    S = nt * TS
    out = np.stack([r['yT'].reshape(D, S).T for r in res.results], axis=0)
    return np.ascontiguousarray(out, dtype=np.float32), res


def kernel(**inputs):
    out, _ = run(inputs)
    return out
```

```python
import numpy as np
from contextlib import ExitStack
import concourse.bass as bass
import concourse.mybir as mybir
from concourse.bass_utils import run_bass_kernel_spmd

F32 = mybir.dt.float32
BF16 = mybir.dt.bfloat16
AF = mybir.ActivationFunctionType
ALU = mybir.AluOpType
AX = mybir.AxisListType

D = 1024
KC = 8
FF = 2816
FC = 22
TS = 512
NB = 4
NLAYER = 4
SEQ = 2048
NCORE = 8
NORM_EPS = 1e-6
GN_EPS = 1e-5
PAGE = 512
RING_CAP = 20480
NWSEM = 12
MASK_NEG = -2400.0


class Rec:
    def __init__(self):
        self.ops = []
        self.last_w = {}
        self.readers = {}

    def add(self, eng, emit, reads=(), writes=(), dma_sem=None):
        idx = len(self.ops)
        deps = set()
        for r in reads:
            w = self.last_w.get(r)
            if w is not None:
                deps.add(w)
        for r in writes:
            w = self.last_w.get(r)
            if w is not None:
                deps.add(w)
            rs = self.readers.get(r)
            if rs:
                deps.update(rs)
        for r in reads:
            self.readers.setdefault(r, []).append(idx)
        for r in writes:
            self.last_w[r] = idx
            self.readers[r] = []
        self.ops.append(dict(eng=eng, emit=emit, deps=deps, dma_sem=dma_sem, sig=False, val=None, semkey=None))
        return idx

    def finalize(self):
        ops = self.ops
        for op in ops:
            keep = set()
            for d in op['deps']:
                dop = ops[d]
                if (dop['dma_sem'] is None and op['dma_sem'] is None
                        and dop['eng'] == 'pe' and op['eng'] == 'pe'):
                    continue
                keep.add(d)
                dop['sig'] = True
            op['deps'] = keep
        cnt = {}
        for op in ops:
            if op['dma_sem'] is not None:
                key = ('dma', op['dma_sem'])
                cnt[key] = cnt.get(key, 0) + 16
                op['val'] = cnt[key]
                op['semkey'] = key
                op['sig'] = True
            elif op['sig']:
                key = ('eng', op['eng'])
                cnt[key] = cnt.get(key, 0) + 1
                op['val'] = cnt[key]
                op['semkey'] = key
        return sorted({op['semkey'] for op in ops if op['sig']}, key=str)

    def emit_engine(self, engname, eng, sems):
        waited = {}
        for op in self.ops:
            if op['eng'] != engname:
                continue
            need = {}
            for d in op['deps']:
                dop = self.ops[d]
                k = dop['semkey']
                if dop['val'] > need.get(k, 0):
                    need[k] = dop['val']
            for k, v in need.items():
                if v > waited.get(k, 0):
                    eng.wait_ge(sems[k], v)
                    waited[k] = v
            if op['emit'] is not None:
                ins = op['emit'](eng)
                if op['sig']:
                    ins.then_inc(sems[op['semkey']], 16 if op['dma_sem'] is not None else 1)


class Ring:
    def __init__(self, rec, ring_t, cap, sched, use_scratch):
        self.rec = rec
        self.t = ring_t
        self.cap = cap
        self.sched = sched
        self.next_load = 0
        self.next_use = 0
        self.live = []
        self.head = 0
        self.off = {}
        self.ndma = 0
        self.nscr = 0
        self.use_scratch = use_scratch

    def _try_alloc(self, size):
        if not self.live:
            self.head = size
            return 0
        tail = self.live[0][1]
        if self.head > tail or (self.head == tail and False):
            if self.head + size <= self.cap:
                off = self.head
            elif size < tail:
                off = 0
            else:
                return None
        else:
            if self.head + size < tail:
                off = self.head
            else:
                return None
        self.head = off + size
        return off

    def pump(self):
        while self.next_load < len(self.sched):
            u = self.sched[self.next_load]
            size = u['size']
            asz = ((size + PAGE - 1) // PAGE) * PAGE
            off = self._try_alloc(asz)
            if off is None:
                return
            self.live.append((u['key'], off, asz))
            self.off[u['key']] = off
            pages = [('rg', p) for p in range(off // PAGE, (off + asz) // PAGE)]
            si = self.ndma % NWSEM
            self.ndma += 1
            dst = self.t[:, off:off + size]
            if u['first'] or not self.use_scratch:
                src = u['src32']
                self.rec.add('pool', lambda e, dst=dst, src=src: e.dma_start(out=dst, in_=src),
                             writes=pages + [('wsem', si)], dma_sem=f'w{si}')
                if self.use_scratch:
                    sj = self.nscr % 4
                    self.nscr += 1
                    scr = u['scr']
                    self.rec.add('sp', lambda e, dst=dst, scr=scr: e.dma_start(out=scr, in_=dst),
                                 reads=pages, writes=[('scr', u['skey']), ('ssem', sj)], dma_sem=f'sc{sj}')
            else:
                scr = u['scr']
                self.rec.add('pool', lambda e, dst=dst, scr=scr: e.dma_start(out=dst, in_=scr),
                             reads=[('scr', u['skey'])], writes=pages + [('wsem', si)], dma_sem=f'w{si}')
            self.next_load += 1

    def get(self, key):
        u = self.sched[self.next_use]
        assert u['key'] == key, (u['key'], key)
        assert self.next_load > self.next_use, "ring too small: unit not loaded " + str(key)
        self.next_use += 1
        return self.off[key]

    def pages(self, key, a, b):
        off = self.off[key]
        return [('rg', p) for p in range((off + a) // PAGE, (off + b - 1) // PAGE + 1)]

    def release(self, key):
        assert self.live[0][0] == key, (self.live[0][0], key)
        self.live.pop(0)
        self.pump()


def _c(a):
    return np.ascontiguousarray(a, dtype=np.float32)


def prep_weights(inp, nl):
    L = nl
    out = {}

    def gu(wg, wu):
        g = wg[:L].reshape(L, 8, 128, 11, 2, 128)
        u_ = wu[:L].reshape(L, 8, 128, 11, 2, 128)
        st = np.stack([g, u_], axis=0)
        st = st.transpose(1, 4, 3, 5, 0, 2, 6)
        return _c(st).reshape(L, 11, 128, 4096)

    def dn(wd):
        d = wd[:L].reshape(L, 22, 128, 8, 128)
        d = d.transpose(0, 3, 2, 1, 4)
        return _c(d).reshape(L, 8, 128, 2816)

    out['w_gu1'] = gu(inp['ffn1_w_gate'], inp['ffn1_w_up'])
    out['w_d1'] = dn(inp['ffn1_w_down'])
    out['w_gu2'] = gu(inp['ffn2_w_gate'], inp['ffn2_w_up'])
    out['w_d2'] = dn(inp['ffn2_w_down'])
    win = inp['w_in'][:L].reshape(L, 8, 128, 2816)
    aq = win[..., 0:512].reshape(L, 8, 128, 8, 64)
    aqt = np.concatenate([aq[:, :, :, 0:4, :], aq[:, :, :, 4:8, :]], axis=-1)
    out['w_aq'] = _c(aqt.transpose(0, 2, 1, 3, 4)).reshape(L, 1, 128, 4096)
    ak = win[..., 512:640]
    akp = np.zeros((L, 8, 128, 2, 128), np.float32)
    akp[:, :, :, 0, 0:64] = ak[..., 0:64]
    akp[:, :, :, 1, 64:128] = ak[..., 64:128]
    out['w_ak'] = _c(akp.transpose(0, 2, 1, 3, 4)).reshape(L, 1, 128, 2048)
    out['w_av'] = _c(win[..., 640:768].transpose(0, 2, 1, 3)).reshape(L, 1, 128, 1024)
    r = win[..., 768:2816].reshape(L, 8, 128, 4, 512)
    out['w_r'] = _c(r.transpose(0, 3, 2, 1, 4)).reshape(L, 4, 128, 4096)
    wo = inp['w_out'][:L]
    wa = wo[:, 0:512].reshape(L, 8, 64, 1024)
    wat = np.concatenate([wa[:, 0:4], wa[:, 4:8]], axis=2)
    wr = wo[:, 512:1024].reshape(L, 4, 128, 1024)
    woc = np.concatenate([wat, wr], axis=1)
    woc = woc.reshape(L, 2, 4, 128, 1024).transpose(0, 1, 3, 2, 4)
    out['w_o'] = _c(woc).reshape(L, 2, 128, 4096)
    nrm = np.stack([inp['ffn1_norm'][:L], inp['mix_norm'][:L], inp['ffn2_norm'][:L]], axis=1)
    nrm = nrm.reshape(L, 3, 8, 128).transpose(3, 0, 1, 2)
    fin = inp['final_norm'].reshape(8, 128).T
    out['c_norm'] = _c(np.concatenate([nrm.reshape(128, L * 24), fin], axis=1))
    out['c_gnw'] = _c(inp['ret_gn_w'][:L].reshape(L * 4, 128).T)
    out['c_sink'] = _c(np.broadcast_to(inp['attn_sinks'][:L].reshape(1, L * 8), (128, L * 8)))
    return out


def const_tables(nt):
    S = nt * TS
    pos = np.arange(S, dtype=np.float32)
    inv_freq = (10000.0 ** (-np.arange(0, 128, 2, dtype=np.float32) / 128.0)).astype(np.float32)
    ang = pos[:, None] * inv_freq[None, :]
    cos = np.cos(ang).astype(np.float32).reshape(S // 128, 128, 64).transpose(1, 0, 2)
    sin = np.sin(ang).astype(np.float32).reshape(S // 128, 128, 64).transpose(1, 0, 2)
    t = {}
    t['c_cos'] = _c(cos).reshape(128, (S // 128) * 64)
    t['c_sin'] = _c(sin).reshape(128, (S // 128) * 64)
    H = 4
    lg = np.log(1.0 - 2.0 ** (-5.0 - np.arange(H, dtype=np.float32))).astype(np.float32)
    idx = np.arange(128, dtype=np.float32)
    dif = idx[:, None] - idx[None, :]
    dm = np.where(dif[None] >= 0, np.exp(np.maximum(dif, 0.0)[None] * lg[:, None, None]), 0.0)
    dmt = dm.transpose(2, 0, 1) * np.float32(128.0 ** -0.5)
    t['c_dmat'] = _c(dmt).reshape(128, 512)
    zeta = np.exp((127.0 - idx)[None, :] * lg[:, None]) * np.float32(128.0 ** -0.5)
    xi = np.exp((idx + 1.0)[None, :] * lg[:, None])
    t['c_xi'] = _c(np.broadcast_to(xi.reshape(1, 512), (128, 512)))
    dec = np.exp(128.0 * lg)
    t['c_zs'] = _c(zeta.T)
    t['dec'] = [float(v) for v in dec.astype(np.float32)]
    kk = np.arange(128)[:, None]
    qq = np.arange(128)[None, :]
    m0 = np.where(kk <= qq, 0.0, MASK_NEG)
    m1 = np.where(qq < kk, 0.0, MASK_NEG)
    t['c_mask'] = _c(np.stack([m0, m1], axis=1)).reshape(128, 256)
    t['c_ident'] = _c(np.eye(128))
    return t


WSPEC = [('w_gu1', 11, 4096), ('w_d1', 8, 2816), ('w_aq', 1, 4096), ('w_ak', 1, 2048), ('w_av', 1, 1024),
         ('w_r', 4, 4096), ('w_o', 2, 4096), ('w_gu2', 11, 4096), ('w_d2', 8, 2816)]


def build(nl, nt, first_layer_only_ffn=False, flags=()):
    nc = bass.Bass("TRN2", target_bir_lowering=False)
    S = nt * TS
    NBLK = S // 128
    tabs = const_tables(nt)
    dec = tabs['dec']
    dram = {}
    for name, nu, sz in WSPEC:
        dram[name] = nc.dram_tensor(name, [nl, nu, 128, sz], F32, kind="ExternalInput").ap()
    use_scratch = nt > 1
    scr = {}
    if use_scratch:
        for name, nu, sz in WSPEC:
            scr[name] = nc.dram_tensor("s_" + name, [nl, nu, 128, sz], BF16, kind="Internal").ap()
    cshape = {'c_norm': nl * 24 + 8, 'c_gnw': nl * 4, 'c_sink': nl * 8, 'c_cos': NBLK * 64, 'c_sin': NBLK * 64,
              'c_dmat': 512, 'c_xi': 512, 'c_zs': 4, 'c_mask': 256, 'c_ident': 128}
    for name, w in cshape.items():
        dram[name] = nc.dram_tensor(name, [128, w], F32, kind="ExternalInput").ap()
    xT = nc.dram_tensor("xT", [KC, 128, S], F32, kind="ExternalInput").ap()
    yT = nc.dram_tensor("yT", [KC, 128, S], F32, kind="ExternalOutput").ap()
    dbg = nc.dram_tensor("dbg", [128, 40 * TS], BF16, kind="ExternalOutput").ap() if 'dbg' in flags else None

    sched = []
    for t in range(nt):
        for l in range(nl):
            order = [('w_gu1', 11), ('w_d1', 8), ('w_aq', 1), ('w_ak', 1), ('w_av', 1), ('w_r', 4), ('w_o', 2),
                     ('w_gu2', 11), ('w_d2', 8)]
            for name, nu in order:
                sz = dict((n, s) for n, _, s in WSPEC)[name]
                for u in range(nu):
                    sched.append(dict(key=(t, l, name, u), skey=(l, name, u), size=sz, first=(t == 0),
                                      src32=dram[name][l, u], scr=(scr[name][l, u] if use_scratch else None)))

    with ExitStack() as es:
        def sb(name, shape, dt):
            return es.enter_context(nc.sbuf_tensor(name, shape, dt))

        rec = Rec()
        ring_t = sb("ring", [128, RING_CAP], BF16)
        ring = Ring(rec, ring_t, RING_CAP, sched, use_scratch)
        hT = sb("hT", [128, KC, TS], F32)
        nT = sb("nT", [128, KC, TS], BF16)
        big = sb("big", [128, 48, TS], BF16)
        sgt = [sb(f"sg{i}", [128, TS], F32) for i in range(2)]
        rstd = sb("rstd", [128, TS], F32)
        cn = sb("cn", [128, cshape['c_norm']], F32)
        gnw = sb("gnw", [128, nl * 4], F32)
        sink = sb("sink", [128, nl * 8], F32)
        sinke = sb("sinke", [128, nl * 8], F32)
        cos_t = sb("cos", [128, NBLK * 64], F32)
        sin_t = sb("sin", [128, NBLK * 64], F32)
        dmat = sb("dmat", [128, 512], F32)
        xi_t = sb("xi", [128, 512], F32)
        zs_t = sb("zs", [128, 4], F32)
        maskf = sb("maskf", [128, 256], F32)
        maskb = sb("maskb", [128, 256], BF16)
        identf = sb("identf", [128, 128], F32)
        ident = sb("ident", [128, 128], BF16)
        onesd = sb("onesd", [128, 128], BF16)
        ones = sb("ones", [128, 128], BF16)
        akP = [sb(f"akP{l}", [128, 2, 128], BF16) for l in range(nl)]
        avP = [sb(f"avP{l}", [128, 128], BF16) for l in range(nl)]
        Rst = [sb(f"R{l}", [128, 512], F32) for l in range(nl)]
        RbC = [sb(f"RbC{l}", [128, 512], BF16) for l in range(nl)]
        Rbt = [sb(f"Rbt{i}", [128, 512], BF16) for i in range(3)]
        akC = sb("akC", [128, 2, TS], BF16)
        skb = sb("skb", [128, 2, TS], BF16)
        avC = sb("avC", [128, NB, 128], BF16)
        rt = [sb(f"rt{i}", [128, 256], F32) for i in range(8)]
        Et = [sb(f"E{i}", [128, TS], BF16) for i in range(4)]
        Sd = [sb(f"Sd{i}", [128, TS], BF16) for i in range(4)]
        gs = sb("gs", [128, NB, TS], F32)
        tmpf = [sb(f"tmpf{i}", [128, TS], F32) for i in range(3)]
        ycb = [sb(f"ycb{i}", [128, TS], F32) for i in range(2)]
        stat = [sb(f"stat{i}", [128, 16], F32) for i in range(2)]
        rtok = [sb(f"rtok{i}", [128, TS], BF16) for i in range(2)]
        NPS = 5
        psf = [es.enter_context(nc.psum_tensor(f"ps{i}", [128, TS], F32)) for i in range(NPS + 1)]
        psbs = [es.enter_context(nc.psum_tensor(f"psb{i}", [128, 2 * TS], BF16)) for i in range(2)]

        state = dict(bank=0, cnt=0)
        def slot(s0, n):
            return big[:, s0:s0 + n, :]
        aT = lambda fc: big[:, fc, :]
        S_AQ, S_QTOK, S_KTOK, S_QT, S_QX, S_KT, S_VT, S_VZ, S_CAT, S_SQ = 0, 4, 8, 12, 16, 20, 24, 28, 32, 40
        SUBS = {}
        for s_ in range(4, 12):
            SUBS[s_] = ['lo', 'hi']
        for s_ in range(12, 24):
            SUBS[s_] = [0, 1, 2, 3]
        for s_ in range(32, 36):
            SUBS[s_] = [0, 1]
        for s_ in range(36, 40):
            SUBS[s_] = [0, 1, 2, 3]

        def bw(slot_, sub=None):
            if sub is None:
                return [('big', slot_)] + [('big', slot_, x) for x in SUBS.get(slot_, [])]
            return [('big', slot_), ('big', slot_, sub)]

        def br(slot_, sub=None):
            if sub is None:
                if slot_ in SUBS:
                    return [('big', slot_, x) for x in SUBS[slot_]]
                return [('big', slot_)]
            return [('big', slot_, sub)]


        def nb():
            b = state['bank']
            state['bank'] = (b + 1) % NPS
            return b

        xbuf = big[:, 24:40, :].bitcast(F32).rearrange("p (k a) b -> p k (a b)", a=2)
        oT = big[:, 0:16, :].bitcast(F32).rearrange("p (k a) b -> p k (a b)", a=2)

        def xb_w(k):
            return bw(24 + 2 * k) + bw(25 + 2 * k)

        def xb_r(k):
            return br(24 + 2 * k) + br(25 + 2 * k)

        def ot_w(k):
            return bw(2 * k) + bw(2 * k + 1)

        def ot_r(k):
            return br(2 * k) + br(2 * k + 1)

        disabled = {f[4:] for f in flags if f.startswith('off_')}
        state['stage'] = 'init'

        def A(eng, fn, reads=(), writes=()):
            if state['stage'] in disabled:
                return
            reads = list(reads)
            writes = list(writes)
            for r in reads:
                if isinstance(r, tuple) and r[0] in ('ps', 'psb') and r not in writes:
                    writes.append(r)
            rec.add(eng, fn, reads=reads, writes=writes)

        cl = [('c_norm', cn), ('c_gnw', gnw), ('c_sink', sink), ('c_cos', cos_t), ('c_sin', sin_t), ('c_dmat', dmat),
              ('c_xi', xi_t), ('c_zs', zs_t), ('c_mask', maskf), ('c_ident', identf)]
        for i, (name, tt) in enumerate(cl):
            rec.add('sp', lambda e, tt=tt, name=name: e.dma_start(out=tt[:], in_=dram[name]),
                    writes=[('c', name)], dma_sem=f'c{i}')
        A('dve', lambda e: e.tensor_copy(out=maskb[:], in_=maskf[:]), [('c', 'c_mask')], ['maskb'])
        A('dve', lambda e: e.tensor_copy(out=ident[:], in_=identf[:]), [('c', 'c_ident')], ['ident'])
        A('dve', lambda e: e.memset(onesd[:], 1.0 / D), [], ['onesd'])
        A('dve', lambda e: e.memset(ones[:], 1.0), [], ['ones'])
        A('act', lambda e: e.activation(out=sinke[:], in_=sink[:], func=AF.Exp), [('c', 'c_sink')], ['sinke'])
        for l in range(nl):
            A('dve', lambda e, l=l: e.memset(Rst[l][:], 0.0), [], [('R', l)])
            A('dve', lambda e, l=l: e.memset(RbC[l][:], 0.0), [], [('RbC', l)])
        ring.pump()

        npst = dict(pending=[], nmm=0)
        NB_STAT = NPS

        def np_start():
            npst['pending'] = []
            npst['nmm'] = 0

        def np_feed(k, from_x=False):
            if from_x:
                A('act', lambda e, k=k: e.activation(out=big[:, S_SQ + k, :], in_=xbuf[:, k, :], func=AF.Square),
                  xb_r(k), bw(S_SQ + k))
            else:
                A('act', lambda e, k=k: e.activation(out=big[:, S_SQ + k, :], in_=hT[:, k, :], func=AF.Square),
                  [('hT', k)], bw(S_SQ + k))
            npst['pending'].append(k)

        def np_mm():
            for k in npst['pending']:
                i = npst['nmm']
                A('pe', lambda e, k=k, i=i: e.matmul(psf[NB_STAT][:], onesd[:], big[:, S_SQ + k, :], start=(i == 0), stop=(i == KC - 1)),
                  [('big', S_SQ + k), 'onesd'], [('ps', NB_STAT)])
                npst['nmm'] += 1
            npst['pending'] = []

        def np_finish(l, which, dst_is_h=False, from_x=False):
            cbase = (l * 24 + which * 8) if l is not None else nl * 24
            np_mm()
            assert npst['nmm'] == KC
            b = NB_STAT
            A('act', lambda e, b=b: e.activation(out=rstd[:], in_=psf[b][:], func=AF.Ln, bias=NORM_EPS, scale=1.0),
              [('ps', b)], ['rstd'])
            A('act', lambda e: e.activation(out=rstd[:], in_=rstd[:], func=AF.Exp, scale=-0.5), ['rstd'], ['rstd'])
            for k in range(KC):
                if dst_is_h:
                    A('dve', lambda e, k=k: e.scalar_tensor_tensor(out=oT[:, k, :], in0=hT[:, k, :], scalar=cn[:, cbase + k:cbase + k + 1],
                                                                    in1=rstd[:], op0=ALU.mult, op1=ALU.mult),
                      [('hT', k), 'rstd', ('c', 'c_norm')], ot_w(k))
                elif from_x:
                    A('dve', lambda e, k=k: e.scalar_tensor_tensor(out=nT[:, k, :], in0=xbuf[:, k, :], scalar=cn[:, cbase + k:cbase + k + 1],
                                                                    in1=rstd[:], op0=ALU.mult, op1=ALU.mult),
                      xb_r(k) + ['rstd', ('c', 'c_norm')], [('nT', k)])
                else:
                    A('dve', lambda e, k=k: e.scalar_tensor_tensor(out=nT[:, k, :], in0=hT[:, k, :], scalar=cn[:, cbase + k:cbase + k + 1],
                                                                    in1=rstd[:], op0=ALU.mult, op1=ALU.mult),
                      [('hT', k), 'rstd', ('c', 'c_norm')], [('nT', k)])

        def ffn(t, l, which):
            gname = 'w_gu1' if which == 0 else 'w_gu2'
            dname = 'w_d1' if which == 0 else 'w_d2'
            from_x = (l == 0 and which == 0)
            np_finish(l, 0 if which == 0 else 2, from_x=from_x)
            for u in range(11):
                key = (t, l, gname, u)
                off = ring.get(key)
                if u == 0:
                    banks0 = [[nb(), nb()], [nb(), nb()]]
                    for k in range(KC):
                        for fcl in range(2):
                            for gi in range(2):
                                b = banks0[fcl][gi]
                                a = ((fcl * 2 + gi) * 8 + k) * 128
                                A('pe', lambda e, b=b, a=a, k=k, off=off: e.matmul(psf[b][:], ring_t[:, off + a:off + a + 128], nT[:, k, :],
                                                                                     start=(k == 0), stop=(k == KC - 1)),
                                  ring.pages(key, a, a + 128) + [('nT', k)], [('ps', b)])
                for fcl in range(2):
                    fc = 2 * u + fcl
                    if u == 0:
                        bg, bu = banks0[fcl]
                    else:
                        bg, bu = nb(), nb()
                        for gi, b in ((0, bg), (1, bu)):
                            for k in range(KC):
                                a = ((fcl * 2 + gi) * 8 + k) * 128
                                A('pe', lambda e, b=b, a=a, k=k, off=off: e.matmul(psf[b][:], ring_t[:, off + a:off + a + 128], nT[:, k, :],
                                                                                     start=(k == 0), stop=(k == KC - 1)),
                                  ring.pages(key, a, a + 128) + [('nT', k)], [('ps', b)])
                    sg = sgt[fc % 2]
                    A('act', lambda e, sg=sg, bg=bg: e.activation(out=sg[:], in_=psf[bg][:], func=AF.Silu),
                      [('ps', bg)], [('sg', fc % 2)])
                    A('dve', lambda e, sg=sg, bu=bu, fc=fc: e.tensor_tensor(out=big[:, fc, :], in0=psf[bu][:], in1=sg[:], op=ALU.mult),
                      [('ps', bu), ('sg', fc % 2)], bw(fc))
                ring.release(key)
            np_start()
            for dc in range(KC):
                key = (t, l, dname, dc)
                off = ring.get(key)
                b = nb()
                for fc in range(FC):
                    a = fc * 128
                    A('pe', lambda e, b=b, a=a, fc=fc, off=off: e.matmul(psf[b][:], ring_t[:, off + a:off + a + 128], big[:, fc, :],
                                                                           start=(fc == 0), stop=(fc == FC - 1)),
                      ring.pages(key, a, a + 128) + [('big', fc)], [('ps', b)])
                np_mm()
                if from_x:
                    A('dve', lambda e, b=b, dc=dc: e.scalar_tensor_tensor(out=hT[:, dc, :], in0=psf[b][:], scalar=0.5, in1=xbuf[:, dc, :],
                                                                           op0=ALU.mult, op1=ALU.add),
                      [('ps', b)] + xb_r(dc), [('hT', dc)])
                else:
                    A('dve', lambda e, b=b, dc=dc: e.scalar_tensor_tensor(out=hT[:, dc, :], in0=psf[b][:], scalar=0.5, in1=hT[:, dc, :],
                                                                           op0=ALU.mult, op1=ALU.add),
                      [('ps', b), ('hT', dc)], [('hT', dc)])
                np_feed(dc)
                ring.release(key)

        def mixer(t, l):
            state['stage'] = 'mnorm'
            np_finish(l, 1)
            G0 = t * NB
            noret = 'noret' in flags
            state['stage'] = 'aq'
            key = (t, l, 'w_aq', 0)
            off = ring.get(key)
            banksq = [nb() for j in range(4)]
            for k in range(KC):
                for j in range(4):
                    b = banksq[j]
                    a = (k * 4 + j) * 128
                    A('pe', lambda e, b=b, a=a, k=k, off=off: e.matmul(psf[b][:], ring_t[:, off + a:off + a + 128], nT[:, k, :],
                                                                         start=(k == 0), stop=(k == KC - 1)),
                      ring.pages(key, a, a + 128) + [('nT', k)], [('ps', b)])
            for j in range(4):
                b = banksq[j]
                A('act', lambda e, b=b, j=j: e.activation(out=big[:, S_AQ + j, :], in_=psf[b][:], func=AF.Copy),
                  [('ps', b)], bw(S_AQ + j))
            ring.release(key)
            state['stage'] = 'ak'
            key = (t, l, 'w_ak', 0)
            off = ring.get(key)
            for g in range(2):
                b = nb()
                for k in range(KC):
                    a = (k * 2 + g) * 128
                    A('pe', lambda e, b=b, a=a, k=k, off=off: e.matmul(psf[b][:], ring_t[:, off + a:off + a + 128], nT[:, k, :],
                                                                         start=(k == 0), stop=(k == KC - 1)),
                      ring.pages(key, a, a + 128) + [('nT', k)], [('ps', b)])
                A('act', lambda e, b=b, g=g: e.activation(out=akC[:, g, :], in_=psf[b][:], func=AF.Copy),
                  [('ps', b)], [('akC', g)])
            ring.release(key)
            state['stage'] = 'av'
            key = (t, l, 'w_av', 0)
            off = ring.get(key)
            b = nb()
            for blk in range(NB):
                for k in range(KC):
                    a = k * 128
                    A('pe', lambda e, b=b, a=a, k=k, blk=blk, off=off: e.matmul(psf[b][:, blk * 128:(blk + 1) * 128], nT[:, k, blk * 128:(blk + 1) * 128],
                                                                                  ring_t[:, off + a:off + a + 128], start=(k == 0), stop=(k == KC - 1)),
                      ring.pages(key, a, a + 128) + [('nT', k)], [('ps', b)])
            A('act', lambda e, b=b: e.activation(out=avC[:].rearrange("p a b -> p (a b)"), in_=psf[b][:], func=AF.Copy),
              [('ps', b)], ['avC'])
            ring.release(key)

            state['stage'] = 'att'
            items = [(g, qb) for qb in range(NB) for g in range(2)]
            for g_ in range(2):
                A('dve', lambda e, g_=g_: e.tensor_scalar(out=skb[:, g_, :].rearrange("p (a b) -> p a b", a=4),
                                                          in0=sinke[:, l * 8 + 4 * g_:l * 8 + 4 * g_ + 4].unsqueeze(2).to_broadcast([128, 4, 128]),
                                                          scalar1=1.0 / 128, scalar2=None, op0=ALU.mult),
                  ['sinke'], [('skb', g_)])

            def att_front(g, qb, ei):
                state['stage'] = 'att'
                G = G0 + qb
                kbs = []
                if G > 0:
                    kbs.append(1)
                kbs.append(0)
                outs = []
                for kind in kbs:
                    b = nb()
                    if kind == 0:
                        lk = akC[:, g, qb * 128:(qb + 1) * 128]
                        rk = [('akC', g)]
                    elif qb == 0:
                        lk = akP[l][:, g, :]
                        rk = [('akP', l)]
                    else:
                        lk = akC[:, g, (qb - 1) * 128:qb * 128]
                        rk = [('akC', g)]
                    rq = big[:, S_AQ:S_AQ + 4, qb * 128:(qb + 1) * 128]
                    A('pe', lambda e, b=b, lk=lk, rq=rq: e.matmul(psf[b][:], lk, rq, start=True, stop=False),
                      rk + [('big', S_AQ + j) for j in range(4)], [('ps', b)])
                    mb = maskb[:, kind * 128:(kind + 1) * 128].unsqueeze(1).to_broadcast([128, 4, 128])
                    A('pe', lambda e, b=b, mb=mb: e.matmul(psf[b][:], ident[:], mb, start=False, stop=True),
                      ['maskb', 'ident'], [('ps', b)])
                    ee = ei[0] % 4
                    ei[0] += 1
                    A('act', lambda e, b=b, ee=ee: e.activation(out=Et[ee][:], in_=psf[b][:], func=AF.Exp, scale=0.125),
                      [('ps', b)], [('E', ee)])
                    outs.append((kind, ee))
                return outs

            def att_back(g, qb, outs):
                state['stage'] = 'att'
                bpv, bden = nb(), nb()
                n = len(outs)
                for i, (kind, ee) in enumerate(outs):
                    if kind == 0:
                        lv = avC[:, qb, :]
                        rv = ['avC']
                    elif qb == 0:
                        lv = avP[l][:]
                        rv = [('avP', l)]
                    else:
                        lv = avC[:, qb - 1, :]
                        rv = ['avC']
                    A('pe', lambda e, lv=lv, ee=ee, i=i, n=n, bpv=bpv: e.matmul(psf[bpv][:], lv, Et[ee][:], start=(i == 0), stop=(i == n - 1)),
                      rv + [('E', ee)], [('ps', bpv)])
                for i, (kind, ee) in enumerate(outs):
                    A('pe', lambda e, ee=ee, i=i, n=n, bden=bden: e.matmul(psf[bden][:], ones[:], Et[ee][:], start=(i == 0), stop=False),
                      ['ones', ('E', ee)], [('ps', bden)])
                A('pe', lambda e, bden=bden: e.matmul(psf[bden][:], ones[:], skb[:, g, :], start=False, stop=True),
                  ['ones', ('skb', g)], [('ps', bden)])
                r0, r1 = g * 64, (g + 1) * 64
                tf = tmpf[(qb * 2 + g) % 2]
                tfk = ('tmpf', (qb * 2 + g) % 2)
                A('act', lambda e, tf=tf, bden=bden: e.activation(out=tf[r0:r1, :], in_=psf[bden][r0:r1, :], func=AF.Ln), [('ps', bden)], [tfk])
                A('act', lambda e, tf=tf: e.activation(out=tf[r0:r1, :], in_=tf[r0:r1, :], func=AF.Exp, scale=-1.0), [tfk], [tfk])
                A('dve', lambda e, tf=tf, bpv=bpv: e.tensor_tensor(out=big[r0:r1, S_CAT:S_CAT + 4, qb * 128:(qb + 1) * 128],
                                                                    in0=psf[bpv][r0:r1, :].rearrange("p (a b) -> p a b", a=4),
                                                                    in1=tf[r0:r1, :].rearrange("p (a b) -> p a b", a=4), op=ALU.mult),
                  [('ps', bpv), tfk], sum([bw(S_CAT + j, g) for j in range(4)], []))

            def tm_block(u, off, blk):
                state['stage'] = f'rp{u}'
                key = (t, l, 'w_r', u)
                b = nb()
                for k in range(KC):
                    a = k * 512
                    A('pe', lambda e, b=b, a=a, k=k, blk=blk, off=off: e.matmul(psf[b][:], nT[:, k, blk * 128:(blk + 1) * 128],
                                                                                  ring_t[:, off + a:off + a + 512], start=(k == 0), stop=(k == KC - 1)),
                      ring.pages(key, a, a + 512) + [('nT', k)], [('ps', b)])
                return b

            def rot_front(dst_slot, blk, b, ri):
                state['stage'] = 'rot'
                G = G0 + blk
                X = psf[b][:].rearrange("p (h two e) -> p h two e", h=4, two=2)
                x1, x2 = X[:, :, 0, :], X[:, :, 1, :]
                cb = cos_t[:, G * 64:(G + 1) * 64].unsqueeze(1).to_broadcast([128, 4, 64])
                sn = sin_t[:, G * 64:(G + 1) * 64].unsqueeze(1).to_broadcast([128, 4, 64])
                r3 = [rt[ri * 4 + i][:].rearrange("p (h e) -> p h e", h=4) for i in range(4)]
                rk_ = [('rt', ri * 4 + i) for i in range(4)]
                O = big[:, dst_slot + blk, :].rearrange("p (h two e) -> p h two e", h=4, two=2)
                cst = [('c', 'c_cos'), ('c', 'c_sin')]
                A('dve', lambda e: e.tensor_tensor(out=r3[0], in0=x1, in1=cb, op=ALU.mult), [('ps', b)] + cst, [rk_[0]])
                A('dve', lambda e: e.tensor_tensor(out=r3[1], in0=x2, in1=sn, op=ALU.mult), [('ps', b)] + cst, [rk_[1]])
                A('dve', lambda e: e.tensor_tensor(out=r3[2], in0=x1, in1=sn, op=ALU.mult), [('ps', b)] + cst, [rk_[2]])
                A('dve', lambda e: e.tensor_tensor(out=r3[3], in0=x2, in1=cb, op=ALU.mult), [('ps', b)] + cst, [rk_[3]])
                A('dve', lambda e: e.tensor_tensor(out=O[:, :, 0, :], in0=r3[0], in1=r3[1], op=ALU.subtract),
                  [rk_[0], rk_[1]], bw(dst_slot + blk, 'lo'))
                A('dve', lambda e: e.tensor_tensor(out=O[:, :, 1, :], in0=r3[2], in1=r3[3], op=ALU.add),
                  [rk_[2], rk_[3]], bw(dst_slot + blk, 'hi'))

            def rot_back(dst_slot, t_slot, x_slot, blk):
                state['stage'] = 'tr'
                hb = state['cnt'] % 2
                state['cnt'] += 1
                for h in range(4):
                    A('pe', lambda e, h=h, hb=hb: e.transpose(psbs[hb][:, h * 128:(h + 1) * 128],
                                                              big[:, dst_slot + blk, h * 128:(h + 1) * 128], ident[:]),
                      [('big', dst_slot + blk, 'lo'), ('big', dst_slot + blk, 'hi'), 'ident'], [('psb', hb)])
                pv = psbs[hb][:, 0:512].rearrange("p (h i) -> p h i", h=4)
                A('act', lambda e, pv=pv: e.activation(out=big[:, t_slot:t_slot + 4, blk * 128:(blk + 1) * 128], in_=pv, func=AF.Copy),
                  [('psb', hb)], sum([bw(t_slot + h, blk) for h in range(4)], []))
                if x_slot is not None:
                    A('dve', lambda e, pv=pv: e.tensor_tensor(out=big[:, x_slot:x_slot + 4, blk * 128:(blk + 1) * 128], in0=pv,
                                                               in1=xi_t[:].rearrange("p (h i) -> p h i", h=4), op=ALU.mult),
                      [('psb', hb), ('c', 'c_xi')], sum([bw(x_slot + h, blk) for h in range(4)], []))

            def v_post(blk, b):
                state['stage'] = 'vpost'
                A('act', lambda e: e.activation(out=big[:, S_VT + blk, :], in_=psf[b][:], func=AF.Copy),
                  [('ps', b)], bw(S_VT + blk))
                A('dve', lambda e: e.tensor_tensor(out=big[:, S_VZ + blk, :].rearrange("p (h v) -> p h v", h=4),
                                                    in0=psf[b][:].rearrange("p (h v) -> p h v", h=4),
                                                    in1=zs_t[:].unsqueeze(2).to_broadcast([128, 4, 128]), op=ALU.mult),
                  [('ps', b), ('c', 'c_zs')], bw(S_VZ + blk))

            def g_post(blk, b):
                state['stage'] = 'gpost'
                A('act', lambda e: e.activation(out=gs[:, blk, :], in_=psf[b][:], func=AF.Silu), [('ps', b)], [('gs', blk)])

            state['stage'] = 'chunks'
            def rb_before(blk):
                return (RbC[l], ('RbC', l)) if blk == 0 else (Rbt[blk - 1], ('Rbt', blk - 1))

            def ch_scores(blk):
                state['stage'] = 'chunks'
                cs = slice(blk * 128, (blk + 1) * 128)
                bs = nb()
                for h in range(4):
                    A('pe', lambda e, h=h, bs=bs: e.matmul(psf[bs][:, h * 128:(h + 1) * 128], big[:, S_KT + h, cs], big[:, S_QT + h, cs],
                                                           start=True, stop=True),
                      [('big', S_KT + h, blk), ('big', S_QT + h, blk)], [('ps', bs)])
                sd = Sd[blk]
                A('dve', lambda e, bs=bs, sd=sd: e.tensor_tensor(out=sd[:], in0=psf[bs][:], in1=dmat[:], op=ALU.mult),
                  [('ps', bs), ('c', 'c_dmat')], [('Sd', blk)])

            def ch_state(blk):
                state['stage'] = 'chunks'
                bk = nb()
                for h in range(4):
                    hs = slice(h * 128, (h + 1) * 128)
                    A('pe', lambda e, h=h, hs=hs, bk=bk: e.matmul(psf[bk][:, hs], big[:, S_KTOK + blk, hs], big[:, S_VZ + blk, hs], start=True, stop=True),
                      [('big', S_KTOK + blk, 'lo'), ('big', S_KTOK + blk, 'hi'), ('big', S_VZ + blk)], [('ps', bk)])
                for h in range(4):
                    hs = slice(h * 128, (h + 1) * 128)
                    A('dve', lambda e, h=h, hs=hs, bk=bk: e.scalar_tensor_tensor(out=Rst[l][:, hs], in0=Rst[l][:, hs], scalar=dec[h], in1=psf[bk][:, hs],
                                                                                   op0=ALU.mult, op1=ALU.add),
                      [('R', l), ('ps', bk)], [('R', l)])
                if blk < NB - 1:
                    A('act', lambda e: e.activation(out=Rbt[blk][:], in_=Rst[l][:], func=AF.Copy), [('R', l)], [('Rbt', blk)])

            def ch_p1(blk):
                state['stage'] = 'chunks'
                n = G0 + blk
                cs = slice(blk * 128, (blk + 1) * 128)
                sd = Sd[blk]
                by = nb()
                rbp, rbk = rb_before(blk)
                for h in range(4):
                    hs = slice(h * 128, (h + 1) * 128)
                    A('pe', lambda e, h=h, hs=hs, by=by, sd=sd: e.matmul(psf[by][:, hs], sd[:, hs], big[:, S_VT + blk, hs], start=True, stop=False),
                      [('Sd', blk), ('big', S_VT + blk)], [('ps', by)])
                    A('pe', lambda e, h=h, hs=hs, by=by, rbp=rbp: e.matmul(psf[by][:, hs], big[:, S_QX + h, cs], rbp[:, hs], start=False, stop=True),
                      [('big', S_QX + h, blk), rbk], [('ps', by)])
                st = stat[n % 2]
                sk = ('stat', n % 2)
                y3 = psf[by][:].rearrange("p (h v) -> p h v", h=4)
                yc = ycb[n % 2]
                yk = ('ycb', n % 2)
                yc3 = yc[:].rearrange("p (h v) -> p h v", h=4)
                ysq = tmpf[2]
                A('dve', lambda e, st=st, y3=y3: e.tensor_reduce(out=st[:, 0:4], in_=y3, axis=AX.X, op=ALU.add), [('ps', by)], [sk])
                A('dve', lambda e, st=st: e.tensor_scalar(out=st[:, 0:4], in0=st[:, 0:4], scalar1=1.0 / 128, scalar2=None, op0=ALU.mult), [sk], [sk])
                A('dve', lambda e, st=st, y3=y3, yc3=yc3: e.tensor_tensor(out=yc3, in0=y3, in1=st[:, 0:4].unsqueeze(2).to_broadcast([128, 4, 128]), op=ALU.subtract),
                  [('ps', by), sk], [yk])
                A('act', lambda e, yc=yc: e.activation(out=ysq[:], in_=yc[:], func=AF.Square), [yk], [('tmpf', 2)])
                A('dve', lambda e, st=st: e.tensor_reduce(out=st[:, 4:8], in_=ysq[:].rearrange("p (h v) -> p h v", h=4), axis=AX.X, op=ALU.add),
                  [('tmpf', 2)], [sk])

            def ch_p2(blk):
                state['stage'] = 'chunks'
                n = G0 + blk
                st = stat[n % 2]
                sk = ('stat', n % 2)
                A('act', lambda e, st=st: e.activation(out=st[:, 4:8], in_=st[:, 4:8], func=AF.Ln, bias=GN_EPS, scale=1.0 / 128), [sk], [sk])
                A('act', lambda e, st=st: e.activation(out=st[:, 4:8], in_=st[:, 4:8], func=AF.Exp, scale=-0.5), [sk], [sk])

            def ch_p3(blk):
                state['stage'] = 'chunks'
                n = G0 + blk
                st = stat[n % 2]
                sk = ('stat', n % 2)
                yc = ycb[n % 2]
                yk = ('ycb', n % 2)
                yc3 = yc[:].rearrange("p (h v) -> p h v", h=4)
                A('dve', lambda e, st=st, yc3=yc3: e.tensor_tensor(out=yc3, in0=yc3, in1=st[:, 4:8].unsqueeze(2).to_broadcast([128, 4, 128]), op=ALU.mult),
                  [yk, sk], [yk])
                ro = rtok[n % 2]
                A('dve', lambda e, ro=ro, yc=yc: e.tensor_tensor(out=ro[:], in0=yc[:], in1=gs[:, blk, :], op=ALU.mult),
                  [yk, ('gs', blk)], [('rtok', n % 2)])

            def ch_tr(blk):
                state['stage'] = 'chunks'
                n = G0 + blk
                cs = slice(blk * 128, (blk + 1) * 128)
                ro = rtok[n % 2]
                hb = state['cnt'] % 2
                state['cnt'] += 1
                for h in range(4):
                    A('pe', lambda e, h=h, hb=hb, ro=ro: e.transpose(psbs[hb][:, h * 128:(h + 1) * 128], ro[:, h * 128:(h + 1) * 128], ident[:]),
                      [('rtok', n % 2), 'ident'], [('psb', hb)])
                for h in range(4):
                    A('act', lambda e, h=h, hb=hb: e.activation(out=big[:, S_CAT + 4 + h, cs], in_=psbs[hb][:, h * 128:(h + 1) * 128], func=AF.Copy,
                                                                scale=gnw[:, l * 4 + h:l * 4 + h + 1]),
                      [('psb', hb), ('c', 'c_gnw')], bw(S_CAT + 4 + h, blk))

            ei = [0]
            pend = None
            pend_tr = []
            roff = {}
            if noret:
                for u in range(4):
                    ring.get((t, l, 'w_r', u)); ring.release((t, l, 'w_r', u))
                for h in range(4):
                    A('dve', lambda e, h=h: e.memset(big[:, S_CAT + 4 + h, :], 0.0), [], bw(S_CAT + 4 + h))
            if 'noatt' in flags:
                for j in range(4):
                    A('dve', lambda e, j=j: e.memset(big[:, S_CAT + j, :], 0.0), [], bw(S_CAT + j))
            csched = {
                4: [(ch_scores, 0), (ch_scores, 1)],
                5: [(ch_scores, 2), (ch_scores, 3), (ch_state, 0), (ch_p1, 0)],
                6: [(ch_state, 1), (ch_p2, 0), (ch_p1, 1)],
                7: [(ch_state, 2), (ch_p3, 0), (ch_p2, 1), (ch_p1, 2)],
                8: [(ch_state, 3), (ch_tr, 0), (ch_p3, 1), (ch_p2, 2), (ch_p1, 3)],
                9: [(ch_tr, 1), (ch_p3, 2), (ch_p2, 3), (ch_p3, 3)],
                10: [(ch_tr, 2), (ch_tr, 3)],
            }
            for s_i, (g, qb) in enumerate(items):
                if not noret and s_i == 0:
                    roff[0] = ring.get((t, l, 'w_r', 0))
                    roff[1] = ring.get((t, l, 'w_r', 1))
                if not noret and s_i == 4:
                    roff[2] = ring.get((t, l, 'w_r', 2))
                    roff[3] = ring.get((t, l, 'w_r', 3))
                outs = None
                if 'noatt' not in flags:
                    outs = att_front(g, qb, ei)
                new_tr = []
                if not noret:
                    if s_i < 4:
                        blk = s_i
                        bq = tm_block(0, roff[0], blk)
                        rot_front(S_QTOK, blk, bq, 0)
                        bk_ = tm_block(1, roff[1], blk)
                        rot_front(S_KTOK, blk, bk_, 1)
                        new_tr = [(S_QTOK, S_QT, S_QX, blk), (S_KTOK, S_KT, None, blk)]
                    else:
                        blk = s_i - 4
                        bv = tm_block(2, roff[2], blk)
                        v_post(blk, bv)
                        bg_ = tm_block(3, roff[3], blk)
                        g_post(blk, bg_)
                if 'noatt' not in flags:
                    if pend is not None:
                        att_back(*pend)
                    pend = (g, qb, outs)
                for tr in pend_tr:
                    rot_back(*tr)
                pend_tr = new_tr
                if not noret:
                    for fn, blk_ in csched.get(s_i, []):
                        fn(blk_)
                if not noret and s_i == 3:
                    ring.release((t, l, 'w_r', 0))
                    ring.release((t, l, 'w_r', 1))
                if not noret and s_i == 7:
                    ring.release((t, l, 'w_r', 2))
                    ring.release((t, l, 'w_r', 3))
            if pend is not None:
                att_back(*pend)
            for tr in pend_tr:
                rot_back(*tr)
            state['stage'] = 'att'
            if t + 1 < nt:
                A('act', lambda e: e.activation(out=akP[l][:], in_=akC[:, :, (NB - 1) * 128:NB * 128], func=AF.Copy),
                  [('akC', 0), ('akC', 1)], [('akP', l)])
                A('act', lambda e: e.activation(out=avP[l][:], in_=avC[:, NB - 1, :], func=AF.Copy),
                  ['avC'], [('avP', l)])

            offs = {}
            for u in range(2):
                offs[u] = ring.get((t, l, 'w_o', u))
            cat_reads = {}
            for cc in range(8):
                if cc < 4:
                    cat_reads[cc] = [('big', S_CAT + cc, 0), ('big', S_CAT + cc, 1)]
                else:
                    cat_reads[cc] = [('big', S_CAT + cc, blk) for blk in range(NB)]

            def wo_half(dc, half):
                state['stage'] = 'wo'
                b = nb()
                for cc in range(half * 4, half * 4 + 4):
                    u, ccl = cc // 4, cc % 4
                    key = (t, l, 'w_o', u)
                    a = ccl * 1024 + dc * 128
                    A('pe', lambda e, b=b, a=a, cc=cc, u=u: e.matmul(psf[b][:], ring_t[:, offs[u] + a:offs[u] + a + 128], big[:, S_CAT + cc, :],
                                                                       start=(cc % 4 == 0), stop=(cc % 4 == 3)),
                      ring.pages(key, a, a + 128) + cat_reads[cc], [('ps', b)])
                return b

            def wo_add(dc, b):
                state['stage'] = 'wo'
                A('dve', lambda e, b=b, dc=dc: e.tensor_tensor(out=hT[:, dc, :], in0=psf[b][:], in1=hT[:, dc, :], op=ALU.add),
                  [('ps', b), ('hT', dc)], [('hT', dc)])

            for s_i in range(8, 11):
                dcs = {8: [0, 1], 9: [2, 3, 4], 10: [5, 6, 7]}[s_i]
                bs_ = [wo_half(dc, 0) for dc in dcs]
                if not noret:
                    for fn, blk_ in csched.get(s_i, []):
                        fn(blk_)
                for dc, b_ in zip(dcs, bs_):
                    wo_add(dc, b_)
            if not noret and t + 1 < nt:
                state['stage'] = 'chunks'
                A('act', lambda e: e.activation(out=RbC[l][:], in_=Rst[l][:], func=AF.Copy), [('R', l)], [('RbC', l)])
            np_start()
            for dc in range(KC):
                b_ = wo_half(dc, 1)
                np_mm()
                wo_add(dc, b_)
                np_feed(dc)
            ring.release((t, l, 'w_o', 0))
            ring.release((t, l, 'w_o', 1))
            state['stage'] = 'post'

        def load_x(t):
            ts = slice(t * TS, (t + 1) * TS)
            rec.add('sp', lambda e, ts=ts: e.dma_start(out=xbuf, in_=xT[:, :, ts].rearrange("k p s -> p k s")),
                    writes=sum([xb_w(k) for k in range(KC)], []) + ['xsem'], dma_sem='x')

        load_x(0)
        for t in range(nt):
            ts = slice(t * TS, (t + 1) * TS)
            np_start()
            for k_ in range(KC):
                np_feed(k_, from_x=True)
            if 'noffn' in flags:
                for k_ in range(KC):
                    A('dve', lambda e, k_=k_: e.tensor_copy(out=hT[:, k_, :], in_=xbuf[:, k_, :]), xb_r(k_), [('hT', k_)])
            for l in range(nl):
                if 'noffn' not in flags:
                    ffn(t, l, 0)
                else:
                    for u in range(11):
                        ring.get((t, l, 'w_gu1', u)); ring.release((t, l, 'w_gu1', u))
                    for u in range(8):
                        ring.get((t, l, 'w_d1', u)); ring.release((t, l, 'w_d1', u))
                if 'nomix' not in flags:
                    mixer(t, l)
                    if dbg is not None and t == 0 and l == 0:
                        allr = []
                        for s_ in range(40):
                            allr += br(s_)
                        rec.add('sp', lambda e: e.dma_start(out=dbg.rearrange("p (a b) -> p a b", a=40), in_=big[:, 0:40, :]), reads=allr, writes=['dbg'], dma_sem='dbg')
                else:
                    for name, nu in (('w_aq', 1), ('w_ak', 1), ('w_av', 1), ('w_r', 4), ('w_o', 2)):
                        for u in range(nu):
                            ring.get((t, l, name, u)); ring.release((t, l, name, u))
                if l == nl - 1 and t + 1 < nt:
                    load_x(t + 1)
                if 'noffn' not in flags:
                    ffn(t, l, 1)
                else:
                    for u in range(11):
                        ring.get((t, l, 'w_gu2', u)); ring.release((t, l, 'w_gu2', u))
                    for u in range(8):
                        ring.get((t, l, 'w_d2', u)); ring.release((t, l, 'w_d2', u))
            np_finish(None, 0, dst_is_h=True)
            rec.add('sp', lambda e, ts=ts: e.dma_start(out=yT[:, :, ts].rearrange("k p s -> p k s"), in_=oT),
                    reads=sum([ot_r(k) for k in range(KC)], []), writes=[('yT', t), 'ysem'], dma_sem='y')
        rec.add('sp', None, reads=[('yT', t) for t in range(nt)] + (['dbg'] if dbg is not None else []))
        if use_scratch:
            rec.add('sp', None, reads=[('ssem', j) for j in range(4)])

        semkeys = rec.finalize()
        sems = {k: es.enter_context(nc.semaphore(f"s{n}")) for n, k in enumerate(semkeys)}
        with nc.Block() as block:
            @block.sync
            def _(e):
                rec.emit_engine('sp', e, sems)

            @block.gpsimd
            def _(e):
                rec.emit_engine('pool', e, sems)

            @block.tensor
            def _(e):
                rec.emit_engine('pe', e, sems)

            @block.vector
            def _(e):
                rec.emit_engine('dve', e, sems)

            @block.scalar
            def _(e):
                rec.emit_engine('act', e, sems)
    return nc


def make_in_maps(inputs, nl, nt, ncore):
    S = nt * TS
    w = prep_weights(inputs, nl)
    tabs = const_tables(nt)
    shared = dict(w)
    for k, v in tabs.items():
        if k != 'dec':
            shared[k] = v
    x = np.asarray(inputs['x'], dtype=np.float32)
    maps = []
    for c in range(ncore):
        m = dict(shared)
        m['xT'] = _c(x[c, :S, :].T).reshape(KC, 128, S)
        maps.append(m)
    return maps


def run(inputs, nl=NLAYER, nt=SEQ // TS, ncore=NCORE, flags=(), trace=False):
    inputs = {k: np.asarray(v) for k, v in inputs.items()}
    nc = build(nl, nt, flags=flags)
    maps = make_in_maps(inputs, nl, nt, ncore)
    res = run_bass_kernel_spmd(nc, maps, core_ids=list(range(ncore)), trace=trace)
    S = nt * TS
    out = np.stack([r['yT'].reshape(D, S).T for r in res.results], axis=0)
    return np.ascontiguousarray(out, dtype=np.float32), res


def kernel(**inputs):
    out, _ = run(inputs)
    return out
```

```python
import numpy as np
from contextlib import ExitStack
import concourse.bass as bass
import concourse.mybir as mybir
from concourse.bass_utils import run_bass_kernel_spmd

F32 = mybir.dt.float32
BF16 = mybir.dt.bfloat16
AF = mybir.ActivationFunctionType
ALU = mybir.AluOpType
AX = mybir.AxisListType

D = 1024
KC = 8
FF = 2816
FC = 22
TS = 512
NB = 4
NLAYER = 4
SEQ = 2048
NCORE = 8
NORM_EPS = 1e-6
GN_EPS = 1e-5
PAGE = 512
RING_CAP = 20480
NWSEM = 12
MASK_NEG = -2400.0


class Rec:
    def __init__(self):
        self.ops = []
        self.last_w = {}
        self.readers = {}

    def add(self, eng, emit, reads=(), writes=(), dma_sem=None):
        idx = len(self.ops)
        deps = set()
        for r in reads:
            w = self.last_w.get(r)
            if w is not None:
                deps.add(w)
        for r in writes:
            w = self.last_w.get(r)
            if w is not None:
                deps.add(w)
            rs = self.readers.get(r)
            if rs:
                deps.update(rs)
        for r in reads:
            self.readers.setdefault(r, []).append(idx)
        for r in writes:
            self.last_w[r] = idx
            self.readers[r] = []
        self.ops.append(dict(eng=eng, emit=emit, deps=deps, dma_sem=dma_sem, sig=False, val=None, semkey=None))
        return idx

    def finalize(self):
        ops = self.ops
        for op in ops:
            keep = set()
            for d in op['deps']:
                dop = ops[d]
                if (dop['dma_sem'] is None and op['dma_sem'] is None
                        and dop['eng'] == 'pe' and op['eng'] == 'pe'):
                    continue
                keep.add(d)
                dop['sig'] = True
            op['deps'] = keep
        cnt = {}
        for op in ops:
            if op['dma_sem'] is not None:
                key = ('dma', op['dma_sem'])
                cnt[key] = cnt.get(key, 0) + 16
                op['val'] = cnt[key]
                op['semkey'] = key
                op['sig'] = True
            elif op['sig']:
                key = ('eng', op['eng'])
                cnt[key] = cnt.get(key, 0) + 1
                op['val'] = cnt[key]
                op['semkey'] = key
        return sorted({op['semkey'] for op in ops if op['sig']}, key=str)

    def emit_engine(self, engname, eng, sems):
        waited = {}
        for op in self.ops:
            if op['eng'] != engname:
                continue
            need = {}
            for d in op['deps']:
                dop = self.ops[d]
                k = dop['semkey']
                if dop['val'] > need.get(k, 0):
                    need[k] = dop['val']
            for k, v in need.items():
                if v > waited.get(k, 0):
                    eng.wait_ge(sems[k], v)
                    waited[k] = v
            if op['emit'] is not None:
                ins = op['emit'](eng)
                if op['sig']:
                    ins.then_inc(sems[op['semkey']], 16 if op['dma_sem'] is not None else 1)


class Ring:
    def __init__(self, rec, ring_t, cap, sched, use_scratch):
        self.rec = rec
        self.t = ring_t
        self.cap = cap
        self.sched = sched
        self.next_load = 0
        self.next_use = 0
        self.live = []
        self.head = 0
        self.off = {}
        self.ndma = 0
        self.nscr = 0
        self.use_scratch = use_scratch

    def _try_alloc(self, size):
        if not self.live:
            self.head = size
            return 0
        tail = self.live[0][1]
        if self.head > tail or (self.head == tail and False):
            if self.head + size <= self.cap:
                off = self.head
            elif size < tail:
                off = 0
            else:
                return None
        else:
            if self.head + size < tail:
                off = self.head
            else:
                return None
        self.head = off + size
        return off

    def pump(self):
        while self.next_load < len(self.sched):
            u = self.sched[self.next_load]
            size = u['size']
            asz = ((size + PAGE - 1) // PAGE) * PAGE
            off = self._try_alloc(asz)
            if off is None:
                return
            self.live.append((u['key'], off, asz))
            self.off[u['key']] = off
            pages = [('rg', p) for p in range(off // PAGE, (off + asz) // PAGE)]
            si = self.ndma % NWSEM
            self.ndma += 1
            dst = self.t[:, off:off + size]
            if u['first'] or not self.use_scratch:
                src = u['src32']
                self.rec.add('pool', lambda e, dst=dst, src=src: e.dma_start(out=dst, in_=src),
                             writes=pages + [('wsem', si)], dma_sem=f'w{si}')
                if self.use_scratch:
                    sj = self.nscr % 4
                    self.nscr += 1
                    scr = u['scr']
                    self.rec.add('sp', lambda e, dst=dst, scr=scr: e.dma_start(out=scr, in_=dst),
                                 reads=pages, writes=[('scr', u['skey']), ('ssem', sj)], dma_sem=f'sc{sj}')
            else:
                scr = u['scr']
                self.rec.add('pool', lambda e, dst=dst, scr=scr: e.dma_start(out=dst, in_=scr),
                             reads=[('scr', u['skey'])], writes=pages + [('wsem', si)], dma_sem=f'w{si}')
            self.next_load += 1

    def get(self, key):
        u = self.sched[self.next_use]
        assert u['key'] == key, (u['key'], key)
        assert self.next_load > self.next_use, "ring too small: unit not loaded " + str(key)
        self.next_use += 1
        return self.off[key]

    def pages(self, key, a, b):
        off = self.off[key]
        return [('rg', p) for p in range((off + a) // PAGE, (off + b - 1) // PAGE + 1)]

    def release(self, key):
        assert self.live[0][0] == key, (self.live[0][0], key)
        self.live.pop(0)
        self.pump()


def _c(a):
    return np.ascontiguousarray(a, dtype=np.float32)


def prep_weights(inp, nl):
    L = nl
    out = {}

    def gu(wg, wu):
        g = wg[:L].reshape(L, 8, 128, 11, 2, 128)
        u_ = wu[:L].reshape(L, 8, 128, 11, 2, 128)
        st = np.stack([g, u_], axis=0)
        st = st.transpose(1, 4, 3, 5, 0, 2, 6)
        return _c(st).reshape(L, 11, 128, 4096)

    def dn(wd):
        d = wd[:L].reshape(L, 22, 128, 8, 128)
        d = d.transpose(0, 3, 2, 1, 4)
        return _c(d).reshape(L, 8, 128, 2816)

    out['w_gu1'] = gu(inp['ffn1_w_gate'], inp['ffn1_w_up'])
    out['w_d1'] = dn(inp['ffn1_w_down'])
    out['w_gu2'] = gu(inp['ffn2_w_gate'], inp['ffn2_w_up'])
    out['w_d2'] = dn(inp['ffn2_w_down'])
    win = inp['w_in'][:L].reshape(L, 8, 128, 2816)
    aq = win[..., 0:512].reshape(L, 8, 128, 8, 64)
    aqt = np.concatenate([aq[:, :, :, 0:4, :], aq[:, :, :, 4:8, :]], axis=-1)
    out['w_aq'] = _c(aqt.transpose(0, 2, 1, 3, 4)).reshape(L, 1, 128, 4096)
    ak = win[..., 512:640]
    akp = np.zeros((L, 8, 128, 2, 128), np.float32)
    akp[:, :, :, 0, 0:64] = ak[..., 0:64]
    akp[:, :, :, 1, 64:128] = ak[..., 64:128]
    out['w_ak'] = _c(akp.transpose(0, 2, 1, 3, 4)).reshape(L, 1, 128, 2048)
    out['w_av'] = _c(win[..., 640:768].transpose(0, 2, 1, 3)).reshape(L, 1, 128, 1024)
    r = win[..., 768:2816].reshape(L, 8, 128, 4, 512)
    out['w_r'] = _c(r.transpose(0, 3, 2, 1, 4)).reshape(L, 4, 128, 4096)
    wo = inp['w_out'][:L]
    wa = wo[:, 0:512].reshape(L, 8, 64, 1024)
    wat = np.concatenate([wa[:, 0:4], wa[:, 4:8]], axis=2)
    wr = wo[:, 512:1024].reshape(L, 4, 128, 1024)
    woc = np.concatenate([wat, wr], axis=1)
    woc = woc.reshape(L, 2, 4, 128, 1024).transpose(0, 1, 3, 2, 4)
    out['w_o'] = _c(woc).reshape(L, 2, 128, 4096)
    nrm = np.stack([inp['ffn1_norm'][:L], inp['mix_norm'][:L], inp['ffn2_norm'][:L]], axis=1)
    nrm = nrm.reshape(L, 3, 8, 128).transpose(3, 0, 1, 2)
    fin = inp['final_norm'].reshape(8, 128).T
    out['c_norm'] = _c(np.concatenate([nrm.reshape(128, L * 24), fin], axis=1))
    out['c_gnw'] = _c(inp['ret_gn_w'][:L].reshape(L * 4, 128).T)
    out['c_sink'] = _c(np.broadcast_to(inp['attn_sinks'][:L].reshape(1, L * 8), (128, L * 8)))
    return out


def const_tables(nt):
    S = nt * TS
    pos = np.arange(S, dtype=np.float32)
    inv_freq = (10000.0 ** (-np.arange(0, 128, 2, dtype=np.float32) / 128.0)).astype(np.float32)
    ang = pos[:, None] * inv_freq[None, :]
    cos = np.cos(ang).astype(np.float32).reshape(S // 128, 128, 64).transpose(1, 0, 2)
    sin = np.sin(ang).astype(np.float32).reshape(S // 128, 128, 64).transpose(1, 0, 2)
    t = {}
    t['c_cos'] = _c(cos).reshape(128, (S // 128) * 64)
    t['c_sin'] = _c(sin).reshape(128, (S // 128) * 64)
    H = 4
    lg = np.log(1.0 - 2.0 ** (-5.0 - np.arange(H, dtype=np.float32))).astype(np.float32)
    idx = np.arange(128, dtype=np.float32)
    dif = idx[:, None] - idx[None, :]
    dm = np.where(dif[None] >= 0, np.exp(np.maximum(dif, 0.0)[None] * lg[:, None, None]), 0.0)
    dmt = dm.transpose(2, 0, 1) * np.float32(128.0 ** -0.5)
    t['c_dmat'] = _c(dmt).reshape(128, 512)
    zeta = np.exp((127.0 - idx)[None, :] * lg[:, None]) * np.float32(128.0 ** -0.5)
    xi = np.exp((idx + 1.0)[None, :] * lg[:, None])
    t['c_xi'] = _c(np.broadcast_to(xi.reshape(1, 512), (128, 512)))
    dec = np.exp(128.0 * lg)
    t['c_zs'] = _c(zeta.T)
    t['dec'] = [float(v) for v in dec.astype(np.float32)]
    kk = np.arange(128)[:, None]
    qq = np.arange(128)[None, :]
    m0 = np.where(kk <= qq, 0.0, MASK_NEG)
    m1 = np.where(qq < kk, 0.0, MASK_NEG)
    t['c_mask'] = _c(np.stack([m0, m1], axis=1)).reshape(128, 256)
    t['c_ident'] = _c(np.eye(128))
    return t


WSPEC = [('w_gu1', 11, 4096), ('w_d1', 8, 2816), ('w_aq', 1, 4096), ('w_ak', 1, 2048), ('w_av', 1, 1024),
         ('w_r', 4, 4096), ('w_o', 2, 4096), ('w_gu2', 11, 4096), ('w_d2', 8, 2816)]


def build(nl, nt, first_layer_only_ffn=False, flags=()):
    nc = bass.Bass("TRN2", target_bir_lowering=False)
    S = nt * TS
    NBLK = S // 128
    tabs = const_tables(nt)
    dec = tabs['dec']
    dram = {}
    for name, nu, sz in WSPEC:
        dram[name] = nc.dram_tensor(name, [nl, nu, 128, sz], F32, kind="ExternalInput").ap()
    use_scratch = nt > 1
    scr = {}
    if use_scratch:
        for name, nu, sz in WSPEC:
            scr[name] = nc.dram_tensor("s_" + name, [nl, nu, 128, sz], BF16, kind="Internal").ap()
    cshape = {'c_norm': nl * 24 + 8, 'c_gnw': nl * 4, 'c_sink': nl * 8, 'c_cos': NBLK * 64, 'c_sin': NBLK * 64,
              'c_dmat': 512, 'c_xi': 512, 'c_zs': 4, 'c_mask': 256, 'c_ident': 128}
    for name, w in cshape.items():
        dram[name] = nc.dram_tensor(name, [128, w], F32, kind="ExternalInput").ap()
    xT = nc.dram_tensor("xT", [KC, 128, S], F32, kind="ExternalInput").ap()
    yT = nc.dram_tensor("yT", [KC, 128, S], F32, kind="ExternalOutput").ap()
    dbg = nc.dram_tensor("dbg", [128, 40 * TS], BF16, kind="ExternalOutput").ap() if 'dbg' in flags else None

    sched = []
    for t in range(nt):
        for l in range(nl):
            order = [('w_gu1', 11), ('w_d1', 8), ('w_aq', 1), ('w_ak', 1), ('w_av', 1), ('w_r', 4), ('w_o', 2),
                     ('w_gu2', 11), ('w_d2', 8)]
            for name, nu in order:
                sz = dict((n, s) for n, _, s in WSPEC)[name]
                for u in range(nu):
                    sched.append(dict(key=(t, l, name, u), skey=(l, name, u), size=sz, first=(t == 0),
                                      src32=dram[name][l, u], scr=(scr[name][l, u] if use_scratch else None)))

    with ExitStack() as es:
        def sb(name, shape, dt):
            return es.enter_context(nc.sbuf_tensor(name, shape, dt))

        rec = Rec()
        ring_t = sb("ring", [128, RING_CAP], BF16)
        ring = Ring(rec, ring_t, RING_CAP, sched, use_scratch)
        hT = sb("hT", [128, KC, TS], F32)
        nT = sb("nT", [128, KC, TS], BF16)
        big = sb("big", [128, 48, TS], BF16)
        sgt = [sb(f"sg{i}", [128, TS], F32) for i in range(2)]
        rstd = sb("rstd", [128, TS], F32)
        cn = sb("cn", [128, cshape['c_norm']], F32)
        gnw = sb("gnw", [128, nl * 4], F32)
        sink = sb("sink", [128, nl * 8], F32)
        sinke = sb("sinke", [128, nl * 8], F32)
        cos_t = sb("cos", [128, NBLK * 64], F32)
        sin_t = sb("sin", [128, NBLK * 64], F32)
        dmat = sb("dmat", [128, 512], F32)
        xi_t = sb("xi", [128, 512], F32)
        zs_t = sb("zs", [128, 4], F32)
        maskf = sb("maskf", [128, 256], F32)
        maskb = sb("maskb", [128, 256], BF16)
        identf = sb("identf", [128, 128], F32)
        ident = sb("ident", [128, 128], BF16)
        onesd = sb("onesd", [128, 128], BF16)
        ones = sb("ones", [128, 128], BF16)
        akP = [sb(f"akP{l}", [128, 2, 128], BF16) for l in range(nl)]
        avP = [sb(f"avP{l}", [128, 128], BF16) for l in range(nl)]
        Rst = [sb(f"R{l}", [128, 512], F32) for l in range(nl)]
        RbC = [sb(f"RbC{l}", [128, 512], BF16) for l in range(nl)]
        Rbt = [sb(f"Rbt{i}", [128, 512], BF16) for i in range(3)]
        akC = sb("akC", [128, 2, TS], BF16)
        skb = sb("skb", [128, 2, TS], BF16)
        avC = sb("avC", [128, NB, 128], BF16)
        rt = [sb(f"rt{i}", [128, 256], F32) for i in range(8)]
        Et = [sb(f"E{i}", [128, TS], BF16) for i in range(4)]
        Sd = [sb(f"Sd{i}", [128, TS], BF16) for i in range(4)]
        gs = sb("gs", [128, NB, TS], F32)
        tmpf = [sb(f"tmpf{i}", [128, TS], F32) for i in range(3)]
        ycb = [sb(f"ycb{i}", [128, TS], F32) for i in range(2)]
        stat = [sb(f"stat{i}", [128, 16], F32) for i in range(2)]
        rtok = [sb(f"rtok{i}", [128, TS], BF16) for i in range(2)]
        NPS = 5
        psf = [es.enter_context(nc.psum_tensor(f"ps{i}", [128, TS], F32)) for i in range(NPS + 1)]
        psbs = [es.enter_context(nc.psum_tensor(f"psb{i}", [128, 2 * TS], BF16)) for i in range(2)]

        state = dict(bank=0, cnt=0)
        def slot(s0, n):
            return big[:, s0:s0 + n, :]
        aT = lambda fc: big[:, fc, :]
        S_AQ, S_QTOK, S_KTOK, S_QT, S_QX, S_KT, S_VT, S_VZ, S_CAT, S_SQ = 0, 4, 8, 12, 16, 20, 24, 28, 32, 40
        SUBS = {}
        for s_ in range(4, 12):
            SUBS[s_] = ['lo', 'hi']
        for s_ in range(12, 24):
            SUBS[s_] = [0, 1, 2, 3]
        for s_ in range(32, 36):
            SUBS[s_] = [0, 1]
        for s_ in range(36, 40):
            SUBS[s_] = [0, 1, 2, 3]

        def bw(slot_, sub=None):
            if sub is None:
                return [('big', slot_)] + [('big', slot_, x) for x in SUBS.get(slot_, [])]
            return [('big', slot_), ('big', slot_, sub)]

        def br(slot_, sub=None):
            if sub is None:
                if slot_ in SUBS:
                    return [('big', slot_, x) for x in SUBS[slot_]]
                return [('big', slot_)]
            return [('big', slot_, sub)]


        def nb():
            b = state['bank']
            state['bank'] = (b + 1) % NPS
            return b

        xbuf = big[:, 24:40, :].bitcast(F32).rearrange("p (k a) b -> p k (a b)", a=2)
        oT = big[:, 0:16, :].bitcast(F32).rearrange("p (k a) b -> p k (a b)", a=2)

        def xb_w(k):
            return bw(24 + 2 * k) + bw(25 + 2 * k)

        def xb_r(k):
            return br(24 + 2 * k) + br(25 + 2 * k)

        def ot_w(k):
            return bw(2 * k) + bw(2 * k + 1)

        def ot_r(k):
            return br(2 * k) + br(2 * k + 1)

        disabled = {f[4:] for f in flags if f.startswith('off_')}
        state['stage'] = 'init'

        def A(eng, fn, reads=(), writes=()):
            if state['stage'] in disabled:
                return
            reads = list(reads)
            writes = list(writes)
            for r in reads:
                if isinstance(r, tuple) and r[0] in ('ps', 'psb') and r not in writes:
                    writes.append(r)
            rec.add(eng, fn, reads=reads, writes=writes)

        cl = [('c_norm', cn), ('c_gnw', gnw), ('c_sink', sink), ('c_cos', cos_t), ('c_sin', sin_t), ('c_dmat', dmat),
              ('c_xi', xi_t), ('c_zs', zs_t), ('c_mask', maskf), ('c_ident', identf)]
        for i, (name, tt) in enumerate(cl):
            rec.add('sp', lambda e, tt=tt, name=name: e.dma_start(out=tt[:], in_=dram[name]),
                    writes=[('c', name)], dma_sem=f'c{i}')
        A('dve', lambda e: e.tensor_copy(out=maskb[:], in_=maskf[:]), [('c', 'c_mask')], ['maskb'])
        A('dve', lambda e: e.tensor_copy(out=ident[:], in_=identf[:]), [('c', 'c_ident')], ['ident'])
        A('dve', lambda e: e.memset(onesd[:], 1.0 / D), [], ['onesd'])
        A('dve', lambda e: e.memset(ones[:], 1.0), [], ['ones'])
        A('act', lambda e: e.activation(out=sinke[:], in_=sink[:], func=AF.Exp), [('c', 'c_sink')], ['sinke'])
        for l in range(nl):
            A('dve', lambda e, l=l: e.memset(Rst[l][:], 0.0), [], [('R', l)])
            A('dve', lambda e, l=l: e.memset(RbC[l][:], 0.0), [], [('RbC', l)])
        ring.pump()

        npst = dict(pending=[], nmm=0)
        NB_STAT = NPS

        def np_start():
            npst['pending'] = []
            npst['nmm'] = 0

        def np_feed(k, from_x=False):
            if from_x:
                A('act', lambda e, k=k: e.activation(out=big[:, S_SQ + k, :], in_=xbuf[:, k, :], func=AF.Square),
                  xb_r(k), bw(S_SQ + k))
            else:
                A('act', lambda e, k=k: e.activation(out=big[:, S_SQ + k, :], in_=hT[:, k, :], func=AF.Square),
                  [('hT', k)], bw(S_SQ + k))
            npst['pending'].append(k)

        def np_mm():
            for k in npst['pending']:
                i = npst['nmm']
                A('pe', lambda e, k=k, i=i: e.matmul(psf[NB_STAT][:], onesd[:], big[:, S_SQ + k, :], start=(i == 0), stop=(i == KC - 1)),
                  [('big', S_SQ + k), 'onesd'], [('ps', NB_STAT)])
                npst['nmm'] += 1
            npst['pending'] = []

        def np_finish(l, which, dst_is_h=False, from_x=False):
            cbase = (l * 24 + which * 8) if l is not None else nl * 24
            np_mm()
            assert npst['nmm'] == KC
            b = NB_STAT
            A('act', lambda e, b=b: e.activation(out=rstd[:], in_=psf[b][:], func=AF.Ln, bias=NORM_EPS, scale=1.0),
              [('ps', b)], ['rstd'])
            A('act', lambda e: e.activation(out=rstd[:], in_=rstd[:], func=AF.Exp, scale=-0.5), ['rstd'], ['rstd'])
            for k in range(KC):
                if dst_is_h:
                    A('dve', lambda e, k=k: e.scalar_tensor_tensor(out=oT[:, k, :], in0=hT[:, k, :], scalar=cn[:, cbase + k:cbase + k + 1],
                                                                    in1=rstd[:], op0=ALU.mult, op1=ALU.mult),
                      [('hT', k), 'rstd', ('c', 'c_norm')], ot_w(k))
                elif from_x:
                    A('dve', lambda e, k=k: e.scalar_tensor_tensor(out=nT[:, k, :], in0=xbuf[:, k, :], scalar=cn[:, cbase + k:cbase + k + 1],
                                                                    in1=rstd[:], op0=ALU.mult, op1=ALU.mult),
                      xb_r(k) + ['rstd', ('c', 'c_norm')], [('nT', k)])
                else:
                    A('dve', lambda e, k=k: e.scalar_tensor_tensor(out=nT[:, k, :], in0=hT[:, k, :], scalar=cn[:, cbase + k:cbase + k + 1],
                                                                    in1=rstd[:], op0=ALU.mult, op1=ALU.mult),
                      [('hT', k), 'rstd', ('c', 'c_norm')], [('nT', k)])

        def ffn(t, l, which):
            gname = 'w_gu1' if which == 0 else 'w_gu2'
            dname = 'w_d1' if which == 0 else 'w_d2'
            from_x = (l == 0 and which == 0)
            np_finish(l, 0 if which == 0 else 2, from_x=from_x)
            for u in range(11):
                key = (t, l, gname, u)
                off = ring.get(key)
                if u == 0:
                    banks0 = [[nb(), nb()], [nb(), nb()]]
                    for k in range(KC):
                        for fcl in range(2):
                            for gi in range(2):
                                b = banks0[fcl][gi]
                                a = ((fcl * 2 + gi) * 8 + k) * 128
                                A('pe', lambda e, b=b, a=a, k=k, off=off: e.matmul(psf[b][:], ring_t[:, off + a:off + a + 128], nT[:, k, :],
                                                                                     start=(k == 0), stop=(k == KC - 1)),
                                  ring.pages(key, a, a + 128) + [('nT', k)], [('ps', b)])
                for fcl in range(2):
                    fc = 2 * u + fcl
                    if u == 0:
                        bg, bu = banks0[fcl]
                    else:
                        bg, bu = nb(), nb()
                        for gi, b in ((0, bg), (1, bu)):
                            for k in range(KC):
                                a = ((fcl * 2 + gi) * 8 + k) * 128
                                A('pe', lambda e, b=b, a=a, k=k, off=off: e.matmul(psf[b][:], ring_t[:, off + a:off + a + 128], nT[:, k, :],
                                                                                     start=(k == 0), stop=(k == KC - 1)),
                                  ring.pages(key, a, a + 128) + [('nT', k)], [('ps', b)])
                    sg = sgt[fc % 2]
                    A('act', lambda e, sg=sg, bg=bg: e.activation(out=sg[:], in_=psf[bg][:], func=AF.Silu),
                      [('ps', bg)], [('sg', fc % 2)])
                    A('dve', lambda e, sg=sg, bu=bu, fc=fc: e.tensor_tensor(out=big[:, fc, :], in0=psf[bu][:], in1=sg[:], op=ALU.mult),
                      [('ps', bu), ('sg', fc % 2)], bw(fc))
                ring.release(key)
            np_start()
            for dc in range(KC):
                key = (t, l, dname, dc)
                off = ring.get(key)
                b = nb()
                for fc in range(FC):
                    a = fc * 128
                    A('pe', lambda e, b=b, a=a, fc=fc, off=off: e.matmul(psf[b][:], ring_t[:, off + a:off + a + 128], big[:, fc, :],
                                                                           start=(fc == 0), stop=(fc == FC - 1)),
                      ring.pages(key, a, a + 128) + [('big', fc)], [('ps', b)])
                np_mm()
                if from_x:
                    A('dve', lambda e, b=b, dc=dc: e.scalar_tensor_tensor(out=hT[:, dc, :], in0=psf[b][:], scalar=0.5, in1=xbuf[:, dc, :],
                                                                           op0=ALU.mult, op1=ALU.add),
                      [('ps', b)] + xb_r(dc), [('hT', dc)])
                else:
                    A('dve', lambda e, b=b, dc=dc: e.scalar_tensor_tensor(out=hT[:, dc, :], in0=psf[b][:], scalar=0.5, in1=hT[:, dc, :],
                                                                           op0=ALU.mult, op1=ALU.add),
                      [('ps', b), ('hT', dc)], [('hT', dc)])
                np_feed(dc)
                ring.release(key)

        def mixer(t, l):
            state['stage'] = 'mnorm'
            np_finish(l, 1)
            G0 = t * NB
            noret = 'noret' in flags
            state['stage'] = 'aq'
            key = (t, l, 'w_aq', 0)
            off = ring.get(key)
            banksq = [nb() for j in range(4)]
            for k in range(KC):
                for j in range(4):
                    b = banksq[j]
                    a = (k * 4 + j) * 128
                    A('pe', lambda e, b=b, a=a, k=k, off=off: e.matmul(psf[b][:], ring_t[:, off + a:off + a + 128], nT[:, k, :],
                                                                         start=(k == 0), stop=(k == KC - 1)),
                      ring.pages(key, a, a + 128) + [('nT', k)], [('ps', b)])
            for j in range(4):
                b = banksq[j]
                A('act', lambda e, b=b, j=j: e.activation(out=big[:, S_AQ + j, :], in_=psf[b][:], func=AF.Copy),
                  [('ps', b)], bw(S_AQ + j))
            ring.release(key)
            state['stage'] = 'ak'
            key = (t, l, 'w_ak', 0)
            off = ring.get(key)
            for g in range(2):
                b = nb()
                for k in range(KC):
                    a = (k * 2 + g) * 128
                    A('pe', lambda e, b=b, a=a, k=k, off=off: e.matmul(psf[b][:], ring_t[:, off + a:off + a + 128], nT[:, k, :],
                                                                         start=(k == 0), stop=(k == KC - 1)),
                      ring.pages(key, a, a + 128) + [('nT', k)], [('ps', b)])
                A('act', lambda e, b=b, g=g: e.activation(out=akC[:, g, :], in_=psf[b][:], func=AF.Copy),
                  [('ps', b)], [('akC', g)])
            ring.release(key)
            state['stage'] = 'av'
            key = (t, l, 'w_av', 0)
            off = ring.get(key)
            b = nb()
            for blk in range(NB):
                for k in range(KC):
                    a = k * 128
                    A('pe', lambda e, b=b, a=a, k=k, blk=blk, off=off: e.matmul(psf[b][:, blk * 128:(blk + 1) * 128], nT[:, k, blk * 128:(blk + 1) * 128],
                                                                                  ring_t[:, off + a:off + a + 128], start=(k == 0), stop=(k == KC - 1)),
                      ring.pages(key, a, a + 128) + [('nT', k)], [('ps', b)])
            A('act', lambda e, b=b: e.activation(out=avC[:].rearrange("p a b -> p (a b)"), in_=psf[b][:], func=AF.Copy),
              [('ps', b)], ['avC'])
            ring.release(key)

            state['stage'] = 'att'
            items = [(g, qb) for qb in range(NB) for g in range(2)]
            for g_ in range(2):
                A('dve', lambda e, g_=g_: e.tensor_scalar(out=skb[:, g_, :].rearrange("p (a b) -> p a b", a=4),
                                                          in0=sinke[:, l * 8 + 4 * g_:l * 8 + 4 * g_ + 4].unsqueeze(2).to_broadcast([128, 4, 128]),
                                                          scalar1=1.0 / 128, scalar2=None, op0=ALU.mult),
                  ['sinke'], [('skb', g_)])

            def att_front(g, qb, ei):
                state['stage'] = 'att'
                G = G0 + qb
                kbs = []
                if G > 0:
                    kbs.append(1)
                kbs.append(0)
                outs = []
                for kind in kbs:
                    b = nb()
                    if kind == 0:
                        lk = akC[:, g, qb * 128:(qb + 1) * 128]
                        rk = [('akC', g)]
                    elif qb == 0:
                        lk = akP[l][:, g, :]
                        rk = [('akP', l)]
                    else:
                        lk = akC[:, g, (qb - 1) * 128:qb * 128]
                        rk = [('akC', g)]
                    rq = big[:, S_AQ:S_AQ + 4, qb * 128:(qb + 1) * 128]
                    A('pe', lambda e, b=b, lk=lk, rq=rq: e.matmul(psf[b][:], lk, rq, start=True, stop=False),
                      rk + [('big', S_AQ + j) for j in range(4)], [('ps', b)])
                    mb = maskb[:, kind * 128:(kind + 1) * 128].unsqueeze(1).to_broadcast([128, 4, 128])
                    A('pe', lambda e, b=b, mb=mb: e.matmul(psf[b][:], ident[:], mb, start=False, stop=True),
                      ['maskb', 'ident'], [('ps', b)])
                    ee = ei[0] % 4
                    ei[0] += 1
                    A('act', lambda e, b=b, ee=ee: e.activation(out=Et[ee][:], in_=psf[b][:], func=AF.Exp, scale=0.125),
                      [('ps', b)], [('E', ee)])
                    outs.append((kind, ee))
                return outs

            def att_back(g, qb, outs):
                state['stage'] = 'att'
                bpv, bden = nb(), nb()
                n = len(outs)
                for i, (kind, ee) in enumerate(outs):
                    if kind == 0:
                        lv = avC[:, qb, :]
                        rv = ['avC']
                    elif qb == 0:
                        lv = avP[l][:]
                        rv = [('avP', l)]
                    else:
                        lv = avC[:, qb - 1, :]
                        rv = ['avC']
                    A('pe', lambda e, lv=lv, ee=ee, i=i, n=n, bpv=bpv: e.matmul(psf[bpv][:], lv, Et[ee][:], start=(i == 0), stop=(i == n - 1)),
                      rv + [('E', ee)], [('ps', bpv)])
                for i, (kind, ee) in enumerate(outs):
                    A('pe', lambda e, ee=ee, i=i, n=n, bden=bden: e.matmul(psf[bden][:], ones[:], Et[ee][:], start=(i == 0), stop=False),
                      ['ones', ('E', ee)], [('ps', bden)])
                A('pe', lambda e, bden=bden: e.matmul(psf[bden][:], ones[:], skb[:, g, :], start=False, stop=True),
                  ['ones', ('skb', g)], [('ps', bden)])
                r0, r1 = g * 64, (g + 1) * 64
                tf = tmpf[(qb * 2 + g) % 2]
                tfk = ('tmpf', (qb * 2 + g) % 2)
                A('act', lambda e, tf=tf, bden=bden: e.activation(out=tf[r0:r1, :], in_=psf[bden][r0:r1, :], func=AF.Ln), [('ps', bden)], [tfk])
                A('act', lambda e, tf=tf: e.activation(out=tf[r0:r1, :], in_=tf[r0:r1, :], func=AF.Exp, scale=-1.0), [tfk], [tfk])
                A('dve', lambda e, tf=tf, bpv=bpv: e.tensor_tensor(out=big[r0:r1, S_CAT:S_CAT + 4, qb * 128:(qb + 1) * 128],
                                                                    in0=psf[bpv][r0:r1, :].rearrange("p (a b) -> p a b", a=4),
                                                                    in1=tf[r0:r1, :].rearrange("p (a b) -> p a b", a=4), op=ALU.mult),
                  [('ps', bpv), tfk], sum([bw(S_CAT + j, g) for j in range(4)], []))

            def tm_block(u, off, blk):
                state['stage'] = f'rp{u}'
                key = (t, l, 'w_r', u)
                b = nb()
                for k in range(KC):
                    a = k * 512
                    A('pe', lambda e, b=b, a=a, k=k, blk=blk, off=off: e.matmul(psf[b][:], nT[:, k, blk * 128:(blk + 1) * 128],
                                                                                  ring_t[:, off + a:off + a + 512], start=(k == 0), stop=(k == KC - 1)),
                      ring.pages(key, a, a + 512) + [('nT', k)], [('ps', b)])
                return b

            def rot_front(dst_slot, blk, b, ri):
                state['stage'] = 'rot'
                G = G0 + blk
                X = psf[b][:].rearrange("p (h two e) -> p h two e", h=4, two=2)
                x1, x2 = X[:, :, 0, :], X[:, :, 1, :]
                cb = cos_t[:, G * 64:(G + 1) * 64].unsqueeze(1).to_broadcast([128, 4, 64])
                sn = sin_t[:, G * 64:(G + 1) * 64].unsqueeze(1).to_broadcast([128, 4, 64])
                r3 = [rt[ri * 4 + i][:].rearrange("p (h e) -> p h e", h=4) for i in range(4)]
                rk_ = [('rt', ri * 4 + i) for i in range(4)]
                O = big[:, dst_slot + blk, :].rearrange("p (h two e) -> p h two e", h=4, two=2)
                cst = [('c', 'c_cos'), ('c', 'c_sin')]
                A('dve', lambda e: e.tensor_tensor(out=r3[0], in0=x1, in1=cb, op=ALU.mult), [('ps', b)] + cst, [rk_[0]])
                A('dve', lambda e: e.tensor_tensor(out=r3[1], in0=x2, in1=sn, op=ALU.mult), [('ps', b)] + cst, [rk_[1]])
                A('dve', lambda e: e.tensor_tensor(out=r3[2], in0=x1, in1=sn, op=ALU.mult), [('ps', b)] + cst, [rk_[2]])
                A('dve', lambda e: e.tensor_tensor(out=r3[3], in0=x2, in1=cb, op=ALU.mult), [('ps', b)] + cst, [rk_[3]])
                A('dve', lambda e: e.tensor_tensor(out=O[:, :, 0, :], in0=r3[0], in1=r3[1], op=ALU.subtract),
                  [rk_[0], rk_[1]], bw(dst_slot + blk, 'lo'))
                A('dve', lambda e: e.tensor_tensor(out=O[:, :, 1, :], in0=r3[2], in1=r3[3], op=ALU.add),
                  [rk_[2], rk_[3]], bw(dst_slot + blk, 'hi'))

            def rot_back(dst_slot, t_slot, x_slot, blk):
                state['stage'] = 'tr'
                hb = state['cnt'] % 2
                state['cnt'] += 1
                for h in range(4):
                    A('pe', lambda e, h=h, hb=hb: e.transpose(psbs[hb][:, h * 128:(h + 1) * 128],
                                                              big[:, dst_slot + blk, h * 128:(h + 1) * 128], ident[:]),
                      [('big', dst_slot + blk, 'lo'), ('big', dst_slot + blk, 'hi'), 'ident'], [('psb', hb)])
                pv = psbs[hb][:, 0:512].rearrange("p (h i) -> p h i", h=4)
                A('act', lambda e, pv=pv: e.activation(out=big[:, t_slot:t_slot + 4, blk * 128:(blk + 1) * 128], in_=pv, func=AF.Copy),
                  [('psb', hb)], sum([bw(t_slot + h, blk) for h in range(4)], []))
                if x_slot is not None:
                    A('dve', lambda e, pv=pv: e.tensor_tensor(out=big[:, x_slot:x_slot + 4, blk * 128:(blk + 1) * 128], in0=pv,
                                                               in1=xi_t[:].rearrange("p (h i) -> p h i", h=4), op=ALU.mult),
                      [('psb', hb), ('c', 'c_xi')], sum([bw(x_slot + h, blk) for h in range(4)], []))

            def v_post(blk, b):
                state['stage'] = 'vpost'
                A('act', lambda e: e.activation(out=big[:, S_VT + blk, :], in_=psf[b][:], func=AF.Copy),
                  [('ps', b)], bw(S_VT + blk))
                A('dve', lambda e: e.tensor_tensor(out=big[:, S_VZ + blk, :].rearrange("p (h v) -> p h v", h=4),
                                                    in0=psf[b][:].rearrange("p (h v) -> p h v", h=4),
                                                    in1=zs_t[:].unsqueeze(2).to_broadcast([128, 4, 128]), op=ALU.mult),
                  [('ps', b), ('c', 'c_zs')], bw(S_VZ + blk))

            def g_post(blk, b):
                state['stage'] = 'gpost'
                A('act', lambda e: e.activation(out=gs[:, blk, :], in_=psf[b][:], func=AF.Silu), [('ps', b)], [('gs', blk)])

            state['stage'] = 'chunks'
            def rb_before(blk):
                return (RbC[l], ('RbC', l)) if blk == 0 else (Rbt[blk - 1], ('Rbt', blk - 1))

            def ch_scores(blk):
                state['stage'] = 'chunks'
                cs = slice(blk * 128, (blk + 1) * 128)
                bs = nb()
                for h in range(4):
                    A('pe', lambda e, h=h, bs=bs: e.matmul(psf[bs][:, h * 128:(h + 1) * 128], big[:, S_KT + h, cs], big[:, S_QT + h, cs],
                                                           start=True, stop=True),
                      [('big', S_KT + h, blk), ('big', S_QT + h, blk)], [('ps', bs)])
                sd = Sd[blk]
                A('dve', lambda e, bs=bs, sd=sd: e.tensor_tensor(out=sd[:], in0=psf[bs][:], in1=dmat[:], op=ALU.mult),
                  [('ps', bs), ('c', 'c_dmat')], [('Sd', blk)])

            def ch_state(blk):
                state['stage'] = 'chunks'
                bk = nb()
                for h in range(4):
                    hs = slice(h * 128, (h + 1) * 128)
                    A('pe', lambda e, h=h, hs=hs, bk=bk: e.matmul(psf[bk][:, hs], big[:, S_KTOK + blk, hs], big[:, S_VZ + blk, hs], start=True, stop=True),
                      [('big', S_KTOK + blk, 'lo'), ('big', S_KTOK + blk, 'hi'), ('big', S_VZ + blk)], [('ps', bk)])
                for h in range(4):
                    hs = slice(h * 128, (h + 1) * 128)
                    A('dve', lambda e, h=h, hs=hs, bk=bk: e.scalar_tensor_tensor(out=Rst[l][:, hs], in0=Rst[l][:, hs], scalar=dec[h], in1=psf[bk][:, hs],
                                                                                   op0=ALU.mult, op1=ALU.add),
                      [('R', l), ('ps', bk)], [('R', l)])
                if blk < NB - 1:
                    A('act', lambda e: e.activation(out=Rbt[blk][:], in_=Rst[l][:], func=AF.Copy), [('R', l)], [('Rbt', blk)])

            def ch_p1(blk):
                state['stage'] = 'chunks'
                n = G0 + blk
                cs = slice(blk * 128, (blk + 1) * 128)
                sd = Sd[blk]
                by = nb()
                rbp, rbk = rb_before(blk)
                for h in range(4):
                    hs = slice(h * 128, (h + 1) * 128)
                    A('pe', lambda e, h=h, hs=hs, by=by, sd=sd: e.matmul(psf[by][:, hs], sd[:, hs], big[:, S_VT + blk, hs], start=True, stop=False),
                      [('Sd', blk), ('big', S_VT + blk)], [('ps', by)])
                    A('pe', lambda e, h=h, hs=hs, by=by, rbp=rbp: e.matmul(psf[by][:, hs], big[:, S_QX + h, cs], rbp[:, hs], start=False, stop=True),
                      [('big', S_QX + h, blk), rbk], [('ps', by)])
                st = stat[n % 2]
                sk = ('stat', n % 2)
                y3 = psf[by][:].rearrange("p (h v) -> p h v", h=4)
                yc = ycb[n % 2]
                yk = ('ycb', n % 2)
                yc3 = yc[:].rearrange("p (h v) -> p h v", h=4)
                ysq = tmpf[2]
                A('dve', lambda e, st=st, y3=y3: e.tensor_reduce(out=st[:, 0:4], in_=y3, axis=AX.X, op=ALU.add), [('ps', by)], [sk])
                A('dve', lambda e, st=st: e.tensor_scalar(out=st[:, 0:4], in0=st[:, 0:4], scalar1=1.0 / 128, scalar2=None, op0=ALU.mult), [sk], [sk])
                A('dve', lambda e, st=st, y3=y3, yc3=yc3: e.tensor_tensor(out=yc3, in0=y3, in1=st[:, 0:4].unsqueeze(2).to_broadcast([128, 4, 128]), op=ALU.subtract),
                  [('ps', by), sk], [yk])
                A('act', lambda e, yc=yc: e.activation(out=ysq[:], in_=yc[:], func=AF.Square), [yk], [('tmpf', 2)])
                A('dve', lambda e, st=st: e.tensor_reduce(out=st[:, 4:8], in_=ysq[:].rearrange("p (h v) -> p h v", h=4), axis=AX.X, op=ALU.add),
                  [('tmpf', 2)], [sk])

            def ch_p2(blk):
                state['stage'] = 'chunks'
                n = G0 + blk
                st = stat[n % 2]
                sk = ('stat', n % 2)
                A('act', lambda e, st=st: e.activation(out=st[:, 4:8], in_=st[:, 4:8], func=AF.Ln, bias=GN_EPS, scale=1.0 / 128), [sk], [sk])
                A('act', lambda e, st=st: e.activation(out=st[:, 4:8], in_=st[:, 4:8], func=AF.Exp, scale=-0.5), [sk], [sk])

            def ch_p3(blk):
                state['stage'] = 'chunks'
                n = G0 + blk
                st = stat[n % 2]
                sk = ('stat', n % 2)
                yc = ycb[n % 2]
                yk = ('ycb', n % 2)
                yc3 = yc[:].rearrange("p (h v) -> p h v", h=4)
                A('dve', lambda e, st=st, yc3=yc3: e.tensor_tensor(out=yc3, in0=yc3, in1=st[:, 4:8].unsqueeze(2).to_broadcast([128, 4, 128]), op=ALU.mult),
                  [yk, sk], [yk])
                ro = rtok[n % 2]
                A('dve', lambda e, ro=ro, yc=yc: e.tensor_tensor(out=ro[:], in0=yc[:], in1=gs[:, blk, :], op=ALU.mult),
                  [yk, ('gs', blk)], [('rtok', n % 2)])

            def ch_tr(blk):
                state['stage'] = 'chunks'
                n = G0 + blk
                cs = slice(blk * 128, (blk + 1) * 128)
                ro = rtok[n % 2]
                hb = state['cnt'] % 2
                state['cnt'] += 1
                for h in range(4):
                    A('pe', lambda e, h=h, hb=hb, ro=ro: e.transpose(psbs[hb][:, h * 128:(h + 1) * 128], ro[:, h * 128:(h + 1) * 128], ident[:]),
                      [('rtok', n % 2), 'ident'], [('psb', hb)])
                for h in range(4):
                    A('act', lambda e, h=h, hb=hb: e.activation(out=big[:, S_CAT + 4 + h, cs], in_=psbs[hb][:, h * 128:(h + 1) * 128], func=AF.Copy,
                                                                scale=gnw[:, l * 4 + h:l * 4 + h + 1]),
                      [('psb', hb), ('c', 'c_gnw')], bw(S_CAT + 4 + h, blk))

            ei = [0]
            pend = None
            pend_tr = []
            roff = {}
            if noret:
                for u in range(4):
                    ring.get((t, l, 'w_r', u)); ring.release((t, l, 'w_r', u))
                for h in range(4):
                    A('dve', lambda e, h=h: e.memset(big[:, S_CAT + 4 + h, :], 0.0), [], bw(S_CAT + 4 + h))
            if 'noatt' in flags:
                for j in range(4):
                    A('dve', lambda e, j=j: e.memset(big[:, S_CAT + j, :], 0.0), [], bw(S_CAT + j))
            csched = {
                4: [(ch_scores, 0), (ch_scores, 1)],
                5: [(ch_scores, 2), (ch_scores, 3), (ch_state, 0), (ch_p1, 0)],
                6: [(ch_state, 1), (ch_p2, 0), (ch_p1, 1)],
                7: [(ch_state, 2), (ch_p3, 0), (ch_p2, 1), (ch_p1, 2)],
                8: [(ch_state, 3), (ch_tr, 0), (ch_p3, 1), (ch_p2, 2), (ch_p1, 3)],
                9: [(ch_tr, 1), (ch_p3, 2), (ch_p2, 3)],
                10: [(ch_tr, 2), (ch_p3, 3)],
                11: [(ch_tr, 3)],
            }
            for s_i, (g, qb) in enumerate(items):
                if not noret and s_i == 0:
                    roff[0] = ring.get((t, l, 'w_r', 0))
                    roff[1] = ring.get((t, l, 'w_r', 1))
                if not noret and s_i == 4:
                    roff[2] = ring.get((t, l, 'w_r', 2))
                    roff[3] = ring.get((t, l, 'w_r', 3))
                outs = None
                if 'noatt' not in flags:
                    outs = att_front(g, qb, ei)
                new_tr = []
                if not noret:
                    if s_i < 4:
                        blk = s_i
                        bq = tm_block(0, roff[0], blk)
                        rot_front(S_QTOK, blk, bq, 0)
                        bk_ = tm_block(1, roff[1], blk)
                        rot_front(S_KTOK, blk, bk_, 1)
                        new_tr = [(S_QTOK, S_QT, S_QX, blk), (S_KTOK, S_KT, None, blk)]
                    elif s_i in (4, 5):
                        for blk in (2 * (s_i - 4), 2 * (s_i - 4) + 1):
                            bv = tm_block(2, roff[2], blk)
                            v_post(blk, bv)
                    elif s_i == 6:
                        for blk in range(NB):
                            bg_ = tm_block(3, roff[3], blk)
                            g_post(blk, bg_)
                if 'noatt' not in flags:
                    if pend is not None:
                        att_back(*pend)
                    pend = (g, qb, outs)
                for tr in pend_tr:
                    rot_back(*tr)
                pend_tr = new_tr
                if not noret:
                    for fn, blk_ in csched.get(s_i, []):
                        fn(blk_)
                if not noret and s_i == 3:
                    ring.release((t, l, 'w_r', 0))
                    ring.release((t, l, 'w_r', 1))
                if not noret and s_i == 6:
                    ring.release((t, l, 'w_r', 2))
                    ring.release((t, l, 'w_r', 3))
            if pend is not None:
                att_back(*pend)
            for tr in pend_tr:
                rot_back(*tr)
            state['stage'] = 'att'
            if t + 1 < nt:
                A('act', lambda e: e.activation(out=akP[l][:], in_=akC[:, :, (NB - 1) * 128:NB * 128], func=AF.Copy),
                  [('akC', 0), ('akC', 1)], [('akP', l)])
                A('act', lambda e: e.activation(out=avP[l][:], in_=avC[:, NB - 1, :], func=AF.Copy),
                  ['avC'], [('avP', l)])

            offs = {}
            for u in range(2):
                offs[u] = ring.get((t, l, 'w_o', u))
            cat_reads = {}
            for cc in range(8):
                if cc < 4:
                    cat_reads[cc] = [('big', S_CAT + cc, 0), ('big', S_CAT + cc, 1)]
                else:
                    cat_reads[cc] = [('big', S_CAT + cc, blk) for blk in range(NB)]

            def wo_half(dc, half):
                state['stage'] = 'wo'
                b = nb()
                for cc in range(half * 4, half * 4 + 4):
                    u, ccl = cc // 4, cc % 4
                    key = (t, l, 'w_o', u)
                    a = ccl * 1024 + dc * 128
                    A('pe', lambda e, b=b, a=a, cc=cc, u=u: e.matmul(psf[b][:], ring_t[:, offs[u] + a:offs[u] + a + 128], big[:, S_CAT + cc, :],
                                                                       start=(cc % 4 == 0), stop=(cc % 4 == 3)),
                      ring.pages(key, a, a + 128) + cat_reads[cc], [('ps', b)])
                return b

            def wo_add(dc, b):
                state['stage'] = 'wo'
                A('dve', lambda e, b=b, dc=dc: e.tensor_tensor(out=hT[:, dc, :], in0=psf[b][:], in1=hT[:, dc, :], op=ALU.add),
                  [('ps', b), ('hT', dc)], [('hT', dc)])

            for s_i in range(8, 12):
                dcs = [2 * (s_i - 8), 2 * (s_i - 8) + 1]
                bs_ = [wo_half(dc, 0) for dc in dcs]
                if not noret:
                    for fn, blk_ in csched.get(s_i, []):
                        fn(blk_)
                for dc, b_ in zip(dcs, bs_):
                    wo_add(dc, b_)
            if not noret and t + 1 < nt:
                state['stage'] = 'chunks'
                A('act', lambda e: e.activation(out=RbC[l][:], in_=Rst[l][:], func=AF.Copy), [('R', l)], [('RbC', l)])
            np_start()
            for dc in range(KC):
                b_ = wo_half(dc, 1)
                np_mm()
                wo_add(dc, b_)
                np_feed(dc)
            ring.release((t, l, 'w_o', 0))
            ring.release((t, l, 'w_o', 1))
            state['stage'] = 'post'

        def load_x(t):
            ts = slice(t * TS, (t + 1) * TS)
            rec.add('sp', lambda e, ts=ts: e.dma_start(out=xbuf, in_=xT[:, :, ts].rearrange("k p s -> p k s")),
                    writes=sum([xb_w(k) for k in range(KC)], []) + ['xsem'], dma_sem='x')

        load_x(0)
        for t in range(nt):
            ts = slice(t * TS, (t + 1) * TS)
            np_start()
            for k_ in range(KC):
                np_feed(k_, from_x=True)
            if 'noffn' in flags:
                for k_ in range(KC):
                    A('dve', lambda e, k_=k_: e.tensor_copy(out=hT[:, k_, :], in_=xbuf[:, k_, :]), xb_r(k_), [('hT', k_)])
            for l in range(nl):
                if 'noffn' not in flags:
                    ffn(t, l, 0)
                else:
                    for u in range(11):
                        ring.get((t, l, 'w_gu1', u)); ring.release((t, l, 'w_gu1', u))
                    for u in range(8):
                        ring.get((t, l, 'w_d1', u)); ring.release((t, l, 'w_d1', u))
                if 'nomix' not in flags:
                    mixer(t, l)
                    if dbg is not None and t == 0 and l == 0:
                        allr = []
                        for s_ in range(40):
                            allr += br(s_)
                        rec.add('sp', lambda e: e.dma_start(out=dbg.rearrange("p (a b) -> p a b", a=40), in_=big[:, 0:40, :]), reads=allr, writes=['dbg'], dma_sem='dbg')
                else:
                    for name, nu in (('w_aq', 1), ('w_ak', 1), ('w_av', 1), ('w_r', 4), ('w_o', 2)):
                        for u in range(nu):
                            ring.get((t, l, name, u)); ring.release((t, l, name, u))
                if l == nl - 1 and t + 1 < nt:
                    load_x(t + 1)
                if 'noffn' not in flags:
                    ffn(t, l, 1)
                else:
                    for u in range(11):
                        ring.get((t, l, 'w_gu2', u)); ring.release((t, l, 'w_gu2', u))
                    for u in range(8):
                        ring.get((t, l, 'w_d2', u)); ring.release((t, l, 'w_d2', u))
            np_finish(None, 0, dst_is_h=True)
            rec.add('sp', lambda e, ts=ts: e.dma_start(out=yT[:, :, ts].rearrange("k p s -> p k s"), in_=oT),
                    reads=sum([ot_r(k) for k in range(KC)], []), writes=[('yT', t), 'ysem'], dma_sem='y')
        rec.add('sp', None, reads=[('yT', t) for t in range(nt)] + (['dbg'] if dbg is not None else []))
        if use_scratch:
            rec.add('sp', None, reads=[('ssem', j) for j in range(4)])

        semkeys = rec.finalize()
        sems = {k: es.enter_context(nc.semaphore(f"s{n}")) for n, k in enumerate(semkeys)}
        with nc.Block() as block:
            @block.sync
            def _(e):
                rec.emit_engine('sp', e, sems)

            @block.gpsimd
            def _(e):
                rec.emit_engine('pool', e, sems)

            @block.tensor
            def _(e):
                rec.emit_engine('pe', e, sems)

            @block.vector
            def _(e):
                rec.emit_engine('dve', e, sems)

            @block.scalar
            def _(e):
                rec.emit_engine('act', e, sems)
    return nc


def make_in_maps(inputs, nl, nt, ncore):
    S = nt * TS
    w = prep_weights(inputs, nl)
    tabs = const_tables(nt)
    shared = dict(w)
    for k, v in tabs.items():
        if k != 'dec':
            shared[k] = v
    x = np.asarray(inputs['x'], dtype=np.float32)
    maps = []
    for c in range(ncore):
        m = dict(shared)
        m['xT'] = _c(x[c, :S, :].T).reshape(KC, 128, S)
        maps.append(m)
    return maps


def run(inputs, nl=NLAYER, nt=SEQ // TS, ncore=NCORE, flags=(), trace=False):
    inputs = {k: np.asarray(v) for k, v in inputs.items()}
    nc = build(nl, nt, flags=flags)
    maps = make_in_maps(inputs, nl, nt, ncore)
    res = run_bass_kernel_spmd(nc, maps, core_ids=list(range(ncore)), trace=trace)
    S = nt * TS
    out = np.stack([r['yT'].reshape(D, S).T for r in res.results], axis=0)
    return np.ascontiguousarray(out, dtype=np.float32), res


def kernel(**inputs):
    out, _ = run(inputs)
    return out
```

```python
import numpy as np
from contextlib import ExitStack
import concourse.bass as bass
import concourse.mybir as mybir
from concourse.bass_utils import run_bass_kernel_spmd

F32 = mybir.dt.float32
BF16 = mybir.dt.bfloat16
AF = mybir.ActivationFunctionType
ALU = mybir.AluOpType
AX = mybir.AxisListType

D = 1024
KC = 8
FF = 2816
FC = 22
TS = 512
NB = 4
NLAYER = 4
SEQ = 2048
NCORE = 8
NORM_EPS = 1e-6
GN_EPS = 1e-5
PAGE = 512
RING_CAP = 20480
NWSEM = 12
MASK_NEG = -2400.0


class Rec:
    def __init__(self):
        self.ops = []
        self.last_w = {}
        self.readers = {}

    def add(self, eng, emit, reads=(), writes=(), dma_sem=None):
        idx = len(self.ops)
        deps = set()
        for r in reads:
            w = self.last_w.get(r)
            if w is not None:
                deps.add(w)
        for r in writes:
            w = self.last_w.get(r)
            if w is not None:
                deps.add(w)
            rs = self.readers.get(r)
            if rs:
                deps.update(rs)
        for r in reads:
            self.readers.setdefault(r, []).append(idx)
        for r in writes:
            self.last_w[r] = idx
            self.readers[r] = []
        self.ops.append(dict(eng=eng, emit=emit, deps=deps, dma_sem=dma_sem, sig=False, val=None, semkey=None))
        return idx

    def finalize(self):
        ops = self.ops
        for op in ops:
            keep = set()
            for d in op['deps']:
                dop = ops[d]
                if (dop['dma_sem'] is None and op['dma_sem'] is None
                        and dop['eng'] == 'pe' and op['eng'] == 'pe'):
                    continue
                keep.add(d)
                dop['sig'] = True
            op['deps'] = keep
        cnt = {}
        for op in ops:
            if op['dma_sem'] is not None:
                key = ('dma', op['dma_sem'])
                cnt[key] = cnt.get(key, 0) + 16
                op['val'] = cnt[key]
                op['semkey'] = key
                op['sig'] = True
            elif op['sig']:
                key = ('eng', op['eng'])
                cnt[key] = cnt.get(key, 0) + 1
                op['val'] = cnt[key]
                op['semkey'] = key
        return sorted({op['semkey'] for op in ops if op['sig']}, key=str)

    def emit_engine(self, engname, eng, sems):
        waited = {}
        for op in self.ops:
            if op['eng'] != engname:
                continue
            need = {}
            for d in op['deps']:
                dop = self.ops[d]
                k = dop['semkey']
                if dop['val'] > need.get(k, 0):
                    need[k] = dop['val']
            for k, v in need.items():
                if v > waited.get(k, 0):
                    eng.wait_ge(sems[k], v)
                    waited[k] = v
            if op['emit'] is not None:
                ins = op['emit'](eng)
                if op['sig']:
                    ins.then_inc(sems[op['semkey']], 16 if op['dma_sem'] is not None else 1)


class Ring:
    def __init__(self, rec, ring_t, cap, sched, use_scratch):
        self.rec = rec
        self.t = ring_t
        self.cap = cap
        self.sched = sched
        self.next_load = 0
        self.next_use = 0
        self.live = []
        self.head = 0
        self.off = {}
        self.ndma = 0
        self.nscr = 0
        self.use_scratch = use_scratch

    def _try_alloc(self, size):
        if not self.live:
            self.head = size
            return 0
        tail = self.live[0][1]
        if self.head > tail or (self.head == tail and False):
            if self.head + size <= self.cap:
                off = self.head
            elif size < tail:
                off = 0
            else:
                return None
        else:
            if self.head + size < tail:
                off = self.head
            else:
                return None
        self.head = off + size
        return off

    def pump(self):
        while self.next_load < len(self.sched):
            u = self.sched[self.next_load]
            size = u['size']
            asz = ((size + PAGE - 1) // PAGE) * PAGE
            off = self._try_alloc(asz)
            if off is None:
                return
            self.live.append((u['key'], off, asz))
            self.off[u['key']] = off
            pages = [('rg', p) for p in range(off // PAGE, (off + asz) // PAGE)]
            si = self.ndma % NWSEM
            self.ndma += 1
            dst = self.t[:, off:off + size]
            if u['first'] or not self.use_scratch:
                src = u['src32']
                self.rec.add('pool', lambda e, dst=dst, src=src: e.dma_start(out=dst, in_=src),
                             writes=pages + [('wsem', si)], dma_sem=f'w{si}')
                if self.use_scratch:
                    sj = self.nscr % 4
                    self.nscr += 1
                    scr = u['scr']
                    self.rec.add('sp', lambda e, dst=dst, scr=scr: e.dma_start(out=scr, in_=dst),
                                 reads=pages, writes=[('scr', u['skey']), ('ssem', sj)], dma_sem=f'sc{sj}')
            else:
                scr = u['scr']
                self.rec.add('pool', lambda e, dst=dst, scr=scr: e.dma_start(out=dst, in_=scr),
                             reads=[('scr', u['skey'])], writes=pages + [('wsem', si)], dma_sem=f'w{si}')
            self.next_load += 1

    def get(self, key):
        u = self.sched[self.next_use]
        assert u['key'] == key, (u['key'], key)
        assert self.next_load > self.next_use, "ring too small: unit not loaded " + str(key)
        self.next_use += 1
        return self.off[key]

    def pages(self, key, a, b):
        off = self.off[key]
        return [('rg', p) for p in range((off + a) // PAGE, (off + b - 1) // PAGE + 1)]

    def release(self, key):
        assert self.live[0][0] == key, (self.live[0][0], key)
        self.live.pop(0)
        self.pump()


def _c(a):
    return np.ascontiguousarray(a, dtype=np.float32)


def prep_weights(inp, nl):
    L = nl
    out = {}

    def gu(wg, wu):
        g = wg[:L].reshape(L, 8, 128, 11, 2, 128)
        u_ = wu[:L].reshape(L, 8, 128, 11, 2, 128)
        st = np.stack([g, u_], axis=0)
        st = st.transpose(1, 4, 3, 5, 0, 2, 6)
        return _c(st).reshape(L, 11, 128, 4096)

    def dn(wd):
        d = wd[:L].reshape(L, 22, 128, 8, 128)
        d = d.transpose(0, 3, 2, 1, 4)
        return _c(d).reshape(L, 8, 128, 2816)

    out['w_gu1'] = gu(inp['ffn1_w_gate'], inp['ffn1_w_up'])
    out['w_d1'] = dn(inp['ffn1_w_down'])
    out['w_gu2'] = gu(inp['ffn2_w_gate'], inp['ffn2_w_up'])
    out['w_d2'] = dn(inp['ffn2_w_down'])
    win = inp['w_in'][:L].reshape(L, 8, 128, 2816)
    aq = win[..., 0:512].reshape(L, 8, 128, 8, 64)
    aqt = np.concatenate([aq[:, :, :, 0:4, :], aq[:, :, :, 4:8, :]], axis=-1)
    out['w_aq'] = _c(aqt.transpose(0, 2, 1, 3, 4)).reshape(L, 1, 128, 4096)
    ak = win[..., 512:640]
    akp = np.zeros((L, 8, 128, 2, 128), np.float32)
    akp[:, :, :, 0, 0:64] = ak[..., 0:64]
    akp[:, :, :, 1, 64:128] = ak[..., 64:128]
    out['w_ak'] = _c(akp.transpose(0, 2, 1, 3, 4)).reshape(L, 1, 128, 2048)
    out['w_av'] = _c(win[..., 640:768].transpose(0, 2, 1, 3)).reshape(L, 1, 128, 1024)
    r = win[..., 768:2816].reshape(L, 8, 128, 4, 512)
    out['w_r'] = _c(r.transpose(0, 3, 2, 1, 4)).reshape(L, 4, 128, 4096)
    wo = inp['w_out'][:L]
    wa = wo[:, 0:512].reshape(L, 8, 64, 1024)
    wat = np.concatenate([wa[:, 0:4], wa[:, 4:8]], axis=2)
    wr = wo[:, 512:1024].reshape(L, 4, 128, 1024)
    woc = np.concatenate([wat, wr], axis=1)
    woc = woc.reshape(L, 2, 4, 128, 1024).transpose(0, 1, 3, 2, 4)
    out['w_o'] = _c(woc).reshape(L, 2, 128, 4096)
    nrm = np.stack([inp['ffn1_norm'][:L], inp['mix_norm'][:L], inp['ffn2_norm'][:L]], axis=1)
    nrm = nrm.reshape(L, 3, 8, 128).transpose(3, 0, 1, 2)
    fin = inp['final_norm'].reshape(8, 128).T
    out['c_norm'] = _c(np.concatenate([nrm.reshape(128, L * 24), fin], axis=1))
    out['c_gnw'] = _c(inp['ret_gn_w'][:L].reshape(L * 4, 128).T)
    out['c_sink'] = _c(np.broadcast_to(inp['attn_sinks'][:L].reshape(1, L * 8), (128, L * 8)))
    return out


def const_tables(nt):
    S = nt * TS
    pos = np.arange(S, dtype=np.float32)
    inv_freq = (10000.0 ** (-np.arange(0, 128, 2, dtype=np.float32) / 128.0)).astype(np.float32)
    ang = pos[:, None] * inv_freq[None, :]
    cos = np.cos(ang).astype(np.float32).reshape(S // 128, 128, 64).transpose(1, 0, 2)
    sin = np.sin(ang).astype(np.float32).reshape(S // 128, 128, 64).transpose(1, 0, 2)
    t = {}
    t['c_cos'] = _c(cos).reshape(128, (S // 128) * 64)
    t['c_sin'] = _c(sin).reshape(128, (S // 128) * 64)
    H = 4
    lg = np.log(1.0 - 2.0 ** (-5.0 - np.arange(H, dtype=np.float32))).astype(np.float32)
    idx = np.arange(128, dtype=np.float32)
    dif = idx[:, None] - idx[None, :]
    dm = np.where(dif[None] >= 0, np.exp(np.maximum(dif, 0.0)[None] * lg[:, None, None]), 0.0)
    dmt = dm.transpose(2, 0, 1) * np.float32(128.0 ** -0.5)
    t['c_dmat'] = _c(dmt).reshape(128, 512)
    zeta = np.exp((127.0 - idx)[None, :] * lg[:, None]) * np.float32(128.0 ** -0.5)
    xi = np.exp((idx + 1.0)[None, :] * lg[:, None])
    t['c_xi'] = _c(np.broadcast_to(xi.reshape(1, 512), (128, 512)))
    dec = np.exp(128.0 * lg)
    t['c_zs'] = _c(zeta.T)
    t['dec'] = [float(v) for v in dec.astype(np.float32)]
    kk = np.arange(128)[:, None]
    qq = np.arange(128)[None, :]
    m0 = np.where(kk <= qq, 0.0, MASK_NEG)
    m1 = np.where(qq < kk, 0.0, MASK_NEG)
    t['c_mask'] = _c(np.stack([m0, m1], axis=1)).reshape(128, 256)
    t['c_ident'] = _c(np.eye(128))
    return t


WSPEC = [('w_gu1', 11, 4096), ('w_d1', 8, 2816), ('w_aq', 1, 4096), ('w_ak', 1, 2048), ('w_av', 1, 1024),
         ('w_r', 4, 4096), ('w_o', 2, 4096), ('w_gu2', 11, 4096), ('w_d2', 8, 2816)]


def build(nl, nt, first_layer_only_ffn=False, flags=()):
    nc = bass.Bass("TRN2", target_bir_lowering=False)
    S = nt * TS
    NBLK = S // 128
    tabs = const_tables(nt)
    dec = tabs['dec']
    dram = {}
    for name, nu, sz in WSPEC:
        dram[name] = nc.dram_tensor(name, [nl, nu, 128, sz], F32, kind="ExternalInput").ap()
    use_scratch = nt > 1
    scr = {}
    if use_scratch:
        for name, nu, sz in WSPEC:
            scr[name] = nc.dram_tensor("s_" + name, [nl, nu, 128, sz], BF16, kind="Internal").ap()
    cshape = {'c_norm': nl * 24 + 8, 'c_gnw': nl * 4, 'c_sink': nl * 8, 'c_cos': NBLK * 64, 'c_sin': NBLK * 64,
              'c_dmat': 512, 'c_xi': 512, 'c_zs': 4, 'c_mask': 256, 'c_ident': 128}
    for name, w in cshape.items():
        dram[name] = nc.dram_tensor(name, [128, w], F32, kind="ExternalInput").ap()
    xT = nc.dram_tensor("xT", [KC, 128, S], F32, kind="ExternalInput").ap()
    yT = nc.dram_tensor("yT", [KC, 128, S], F32, kind="ExternalOutput").ap()
    dbg = nc.dram_tensor("dbg", [128, 40 * TS], BF16, kind="ExternalOutput").ap() if 'dbg' in flags else None

    sched = []
    for t in range(nt):
        for l in range(nl):
            order = [('w_gu1', 11), ('w_d1', 8), ('w_aq', 1), ('w_ak', 1), ('w_av', 1), ('w_r', 4), ('w_o', 2),
                     ('w_gu2', 11), ('w_d2', 8)]
            for name, nu in order:
                sz = dict((n, s) for n, _, s in WSPEC)[name]
                for u in range(nu):
                    sched.append(dict(key=(t, l, name, u), skey=(l, name, u), size=sz, first=(t == 0),
                                      src32=dram[name][l, u], scr=(scr[name][l, u] if use_scratch else None)))

    with ExitStack() as es:
        def sb(name, shape, dt):
            return es.enter_context(nc.sbuf_tensor(name, shape, dt))

        rec = Rec()
        ring_t = sb("ring", [128, RING_CAP], BF16)
        ring = Ring(rec, ring_t, RING_CAP, sched, use_scratch)
        hT = sb("hT", [128, KC, TS], F32)
        nT = sb("nT", [128, KC, TS], BF16)
        big = sb("big", [128, 48, TS], BF16)
        sgt = [sb(f"sg{i}", [128, TS], F32) for i in range(2)]
        rstd = sb("rstd", [128, TS], F32)
        dmy = sb("dmy", [128, 4], F32)
        cn = sb("cn", [128, cshape['c_norm']], F32)
        gnw = sb("gnw", [128, nl * 4], F32)
        sink = sb("sink", [128, nl * 8], F32)
        sinke = sb("sinke", [128, nl * 8], F32)
        cos_t = sb("cos", [128, NBLK * 64], F32)
        sin_t = sb("sin", [128, NBLK * 64], F32)
        dmat = sb("dmat", [128, 512], F32)
        xi_t = sb("xi", [128, 512], F32)
        zs_t = sb("zs", [128, 4], F32)
        maskf = sb("maskf", [128, 256], F32)
        maskb = sb("maskb", [128, 256], BF16)
        identf = sb("identf", [128, 128], F32)
        ident = sb("ident", [128, 128], BF16)
        onesd = sb("onesd", [128, 128], BF16)
        ones = sb("ones", [128, 128], BF16)
        akP = [sb(f"akP{l}", [128, 2, 128], BF16) for l in range(nl)]
        avP = [sb(f"avP{l}", [128, 128], BF16) for l in range(nl)]
        Rst = [sb(f"R{l}", [128, 512], F32) for l in range(nl)]
        RbC = [sb(f"RbC{l}", [128, 512], BF16) for l in range(nl)]
        Rbt = [sb(f"Rbt{i}", [128, 512], BF16) for i in range(3)]
        akC = sb("akC", [128, 2, TS], BF16)
        skb = sb("skb", [128, 2, TS], BF16)
        avC = sb("avC", [128, NB, 128], BF16)
        rt = [sb(f"rt{i}", [128, 256], F32) for i in range(8)]
        Et = [sb(f"E{i}", [128, TS], BF16) for i in range(4)]
        Sd = [sb(f"Sd{i}", [128, TS], BF16) for i in range(4)]
        gs = sb("gs", [128, NB, TS], F32)
        tmpf = [sb(f"tmpf{i}", [128, TS], F32) for i in range(3)]
        ycb = [sb(f"ycb{i}", [128, TS], F32) for i in range(2)]
        stat = [sb(f"stat{i}", [128, 16], F32) for i in range(2)]
        rtok = [sb(f"rtok{i}", [128, TS], BF16) for i in range(2)]
        NPS = 5
        psf = [es.enter_context(nc.psum_tensor(f"ps{i}", [128, TS], F32)) for i in range(NPS + 1)]
        psbs = [es.enter_context(nc.psum_tensor(f"psb{i}", [128, 2 * TS], BF16)) for i in range(2)]

        state = dict(bank=0, cnt=0)
        def slot(s0, n):
            return big[:, s0:s0 + n, :]
        aT = lambda fc: big[:, fc, :]
        S_AQ, S_QTOK, S_KTOK, S_QT, S_QX, S_KT, S_VT, S_VZ, S_CAT, S_SQ = 0, 4, 8, 12, 16, 20, 24, 28, 32, 40
        SUBS = {}
        for s_ in range(4, 12):
            SUBS[s_] = ['lo', 'hi']
        for s_ in range(12, 24):
            SUBS[s_] = [0, 1, 2, 3]
        for s_ in range(32, 36):
            SUBS[s_] = [0, 1]
        for s_ in range(36, 40):
            SUBS[s_] = [0, 1, 2, 3]

        def bw(slot_, sub=None):
            if sub is None:
                return [('big', slot_)] + [('big', slot_, x) for x in SUBS.get(slot_, [])]
            return [('big', slot_), ('big', slot_, sub)]

        def br(slot_, sub=None):
            if sub is None:
                if slot_ in SUBS:
                    return [('big', slot_, x) for x in SUBS[slot_]]
                return [('big', slot_)]
            return [('big', slot_, sub)]


        def nb():
            b = state['bank']
            state['bank'] = (b + 1) % NPS
            return b

        xbuf = big[:, 24:40, :].bitcast(F32).rearrange("p (k a) b -> p k (a b)", a=2)
        oT = big[:, 0:16, :].bitcast(F32).rearrange("p (k a) b -> p k (a b)", a=2)

        def xb_w(k):
            return bw(24 + 2 * k) + bw(25 + 2 * k)

        def xb_r(k):
            return br(24 + 2 * k) + br(25 + 2 * k)

        def ot_w(k):
            return bw(2 * k) + bw(2 * k + 1)

        def ot_r(k):
            return br(2 * k) + br(2 * k + 1)

        disabled = {f[4:] for f in flags if f.startswith('off_')}
        state['stage'] = 'init'

        def A(eng, fn, reads=(), writes=()):
            if state['stage'] in disabled:
                return
            reads = list(reads)
            writes = list(writes)
            for r in reads:
                if isinstance(r, tuple) and r[0] in ('ps', 'psb') and r not in writes:
                    writes.append(r)
            rec.add(eng, fn, reads=reads, writes=writes)

        cl = [('c_norm', cn), ('c_gnw', gnw), ('c_sink', sink), ('c_cos', cos_t), ('c_sin', sin_t), ('c_dmat', dmat),
              ('c_xi', xi_t), ('c_zs', zs_t), ('c_mask', maskf), ('c_ident', identf)]
        for i, (name, tt) in enumerate(cl):
            rec.add('sp', lambda e, tt=tt, name=name: e.dma_start(out=tt[:], in_=dram[name]),
                    writes=[('c', name)], dma_sem=f'c{i}')
        A('dve', lambda e: e.tensor_copy(out=maskb[:], in_=maskf[:]), [('c', 'c_mask')], ['maskb'])
        A('dve', lambda e: e.tensor_copy(out=ident[:], in_=identf[:]), [('c', 'c_ident')], ['ident'])
        A('dve', lambda e: e.memset(onesd[:], 1.0 / D), [], ['onesd'])
        A('dve', lambda e: e.memset(ones[:], 1.0), [], ['ones'])
        A('dve', lambda e: e.memset(dmy[:], 1.0), [], ['dmy0'])
        A('act', lambda e: e.activation(out=sinke[:], in_=sink[:], func=AF.Exp), [('c', 'c_sink')], ['sinke'])
        for l in range(nl):
            A('dve', lambda e, l=l: e.memset(Rst[l][:], 0.0), [], [('R', l)])
            A('dve', lambda e, l=l: e.memset(RbC[l][:], 0.0), [], [('RbC', l)])
        ring.pump()

        npst = dict(pending=[], nmm=0)
        NB_STAT = NPS

        def act_preswitch(func):
            A('act', lambda e, func=func: e.activation(out=dmy[:, 2:3], in_=dmy[:, 0:1], func=func), ['dmy0'], ['dmy1'])

        def np_start():
            npst['pending'] = []
            npst['nmm'] = 0

        def np_feed(k, from_x=False):
            if from_x:
                A('act', lambda e, k=k: e.activation(out=big[:, S_SQ + k, :], in_=xbuf[:, k, :], func=AF.Square),
                  xb_r(k), bw(S_SQ + k))
            else:
                A('act', lambda e, k=k: e.activation(out=big[:, S_SQ + k, :], in_=hT[:, k, :], func=AF.Square),
                  [('hT', k)], bw(S_SQ + k))
            npst['pending'].append(k)

        def np_mm():
            for k in npst['pending']:
                i = npst['nmm']
                A('pe', lambda e, k=k, i=i: e.matmul(psf[NB_STAT][:], onesd[:], big[:, S_SQ + k, :], start=(i == 0), stop=(i == KC - 1)),
                  [('big', S_SQ + k), 'onesd'], [('ps', NB_STAT)])
                npst['nmm'] += 1
            npst['pending'] = []

        def np_finish(l, which, dst_is_h=False, from_x=False):
            cbase = (l * 24 + which * 8) if l is not None else nl * 24
            np_mm()
            assert npst['nmm'] == KC
            b = NB_STAT
            A('act', lambda e, b=b: e.activation(out=rstd[:], in_=psf[b][:], func=AF.Ln, bias=NORM_EPS, scale=1.0),
              [('ps', b)], ['rstd'])
            A('act', lambda e: e.activation(out=rstd[:], in_=rstd[:], func=AF.Exp, scale=-0.5), ['rstd'], ['rstd'])
            for k in range(KC):
                if dst_is_h:
                    A('dve', lambda e, k=k: e.scalar_tensor_tensor(out=oT[:, k, :], in0=hT[:, k, :], scalar=cn[:, cbase + k:cbase + k + 1],
                                                                    in1=rstd[:], op0=ALU.mult, op1=ALU.mult),
                      [('hT', k), 'rstd', ('c', 'c_norm')], ot_w(k))
                elif from_x:
                    A('dve', lambda e, k=k: e.scalar_tensor_tensor(out=nT[:, k, :], in0=xbuf[:, k, :], scalar=cn[:, cbase + k:cbase + k + 1],
                                                                    in1=rstd[:], op0=ALU.mult, op1=ALU.mult),
                      xb_r(k) + ['rstd', ('c', 'c_norm')], [('nT', k)])
                else:
                    A('dve', lambda e, k=k: e.scalar_tensor_tensor(out=nT[:, k, :], in0=hT[:, k, :], scalar=cn[:, cbase + k:cbase + k + 1],
                                                                    in1=rstd[:], op0=ALU.mult, op1=ALU.mult),
                      [('hT', k), 'rstd', ('c', 'c_norm')], [('nT', k)])

        def ffn(t, l, which):
            gname = 'w_gu1' if which == 0 else 'w_gu2'
            dname = 'w_d1' if which == 0 else 'w_d2'
            from_x = (l == 0 and which == 0)
            np_finish(l, 0 if which == 0 else 2, from_x=from_x)
            act_preswitch(AF.Silu)
            for u in range(11):
                key = (t, l, gname, u)
                off = ring.get(key)
                if u == 0:
                    banks0 = [[nb(), nb()], [nb(), nb()]]
                    for k in range(KC):
                        for fcl in range(2):
                            for gi in range(2):
                                b = banks0[fcl][gi]
                                a = ((fcl * 2 + gi) * 8 + k) * 128
                                A('pe', lambda e, b=b, a=a, k=k, off=off: e.matmul(psf[b][:], ring_t[:, off + a:off + a + 128], nT[:, k, :],
                                                                                     start=(k == 0), stop=(k == KC - 1)),
                                  ring.pages(key, a, a + 128) + [('nT', k)], [('ps', b)])
                for fcl in range(2):
                    fc = 2 * u + fcl
                    if u == 0:
                        bg, bu = banks0[fcl]
                    else:
                        bg, bu = nb(), nb()
                        for gi, b in ((0, bg), (1, bu)):
                            for k in range(KC):
                                a = ((fcl * 2 + gi) * 8 + k) * 128
                                A('pe', lambda e, b=b, a=a, k=k, off=off: e.matmul(psf[b][:], ring_t[:, off + a:off + a + 128], nT[:, k, :],
                                                                                     start=(k == 0), stop=(k == KC - 1)),
                                  ring.pages(key, a, a + 128) + [('nT', k)], [('ps', b)])
                    sg = sgt[fc % 2]
                    A('act', lambda e, sg=sg, bg=bg: e.activation(out=sg[:], in_=psf[bg][:], func=AF.Silu),
                      [('ps', bg)], [('sg', fc % 2)])
                    A('dve', lambda e, sg=sg, bu=bu, fc=fc: e.tensor_tensor(out=big[:, fc, :], in0=psf[bu][:], in1=sg[:], op=ALU.mult),
                      [('ps', bu), ('sg', fc % 2)], bw(fc))
                ring.release(key)
            act_preswitch(AF.Ln)
            np_start()
            for dc in range(KC):
                key = (t, l, dname, dc)
                off = ring.get(key)
                b = nb()
                for fc in range(FC):
                    a = fc * 128
                    A('pe', lambda e, b=b, a=a, fc=fc, off=off: e.matmul(psf[b][:], ring_t[:, off + a:off + a + 128], big[:, fc, :],
                                                                           start=(fc == 0), stop=(fc == FC - 1)),
                      ring.pages(key, a, a + 128) + [('big', fc)], [('ps', b)])
                np_mm()
                if from_x:
                    A('dve', lambda e, b=b, dc=dc: e.scalar_tensor_tensor(out=hT[:, dc, :], in0=psf[b][:], scalar=0.5, in1=xbuf[:, dc, :],
                                                                           op0=ALU.mult, op1=ALU.add),
                      [('ps', b)] + xb_r(dc), [('hT', dc)])
                else:
                    A('dve', lambda e, b=b, dc=dc: e.scalar_tensor_tensor(out=hT[:, dc, :], in0=psf[b][:], scalar=0.5, in1=hT[:, dc, :],
                                                                           op0=ALU.mult, op1=ALU.add),
                      [('ps', b), ('hT', dc)], [('hT', dc)])
                np_feed(dc)
                ring.release(key)

        def mixer(t, l):
            state['stage'] = 'mnorm'
            np_finish(l, 1)
            G0 = t * NB
            noret = 'noret' in flags
            state['stage'] = 'aq'
            key = (t, l, 'w_aq', 0)
            off = ring.get(key)
            banksq = [nb() for j in range(4)]
            for k in range(KC):
                for j in range(4):
                    b = banksq[j]
                    a = (k * 4 + j) * 128
                    A('pe', lambda e, b=b, a=a, k=k, off=off: e.matmul(psf[b][:], ring_t[:, off + a:off + a + 128], nT[:, k, :],
                                                                         start=(k == 0), stop=(k == KC - 1)),
                      ring.pages(key, a, a + 128) + [('nT', k)], [('ps', b)])
            for j in range(4):
                b = banksq[j]
                A('act', lambda e, b=b, j=j: e.activation(out=big[:, S_AQ + j, :], in_=psf[b][:], func=AF.Copy),
                  [('ps', b)], bw(S_AQ + j))
            ring.release(key)
            state['stage'] = 'ak'
            key = (t, l, 'w_ak', 0)
            off = ring.get(key)
            for g in range(2):
                b = nb()
                for k in range(KC):
                    a = (k * 2 + g) * 128
                    A('pe', lambda e, b=b, a=a, k=k, off=off: e.matmul(psf[b][:], ring_t[:, off + a:off + a + 128], nT[:, k, :],
                                                                         start=(k == 0), stop=(k == KC - 1)),
                      ring.pages(key, a, a + 128) + [('nT', k)], [('ps', b)])
                A('act', lambda e, b=b, g=g: e.activation(out=akC[:, g, :], in_=psf[b][:], func=AF.Copy),
                  [('ps', b)], [('akC', g)])
            ring.release(key)
            state['stage'] = 'av'
            key = (t, l, 'w_av', 0)
            off = ring.get(key)
            b = nb()
            for blk in range(NB):
                for k in range(KC):
                    a = k * 128
                    A('pe', lambda e, b=b, a=a, k=k, blk=blk, off=off: e.matmul(psf[b][:, blk * 128:(blk + 1) * 128], nT[:, k, blk * 128:(blk + 1) * 128],
                                                                                  ring_t[:, off + a:off + a + 128], start=(k == 0), stop=(k == KC - 1)),
                      ring.pages(key, a, a + 128) + [('nT', k)], [('ps', b)])
            A('act', lambda e, b=b: e.activation(out=avC[:].rearrange("p a b -> p (a b)"), in_=psf[b][:], func=AF.Copy),
              [('ps', b)], ['avC'])
            ring.release(key)

            state['stage'] = 'att'
            items = [(g, qb) for qb in range(NB) for g in range(2)]
            for g_ in range(2):
                A('dve', lambda e, g_=g_: e.tensor_scalar(out=skb[:, g_, :].rearrange("p (a b) -> p a b", a=4),
                                                          in0=sinke[:, l * 8 + 4 * g_:l * 8 + 4 * g_ + 4].unsqueeze(2).to_broadcast([128, 4, 128]),
                                                          scalar1=1.0 / 128, scalar2=None, op0=ALU.mult),
                  ['sinke'], [('skb', g_)])

            def att_front(g, qb, ei):
                state['stage'] = 'att'
                G = G0 + qb
                kbs = []
                if G > 0:
                    kbs.append(1)
                kbs.append(0)
                outs = []
                for kind in kbs:
                    b = nb()
                    if kind == 0:
                        lk = akC[:, g, qb * 128:(qb + 1) * 128]
                        rk = [('akC', g)]
                    elif qb == 0:
                        lk = akP[l][:, g, :]
                        rk = [('akP', l)]
                    else:
                        lk = akC[:, g, (qb - 1) * 128:qb * 128]
                        rk = [('akC', g)]
                    rq = big[:, S_AQ:S_AQ + 4, qb * 128:(qb + 1) * 128]
                    A('pe', lambda e, b=b, lk=lk, rq=rq: e.matmul(psf[b][:], lk, rq, start=True, stop=False),
                      rk + [('big', S_AQ + j) for j in range(4)], [('ps', b)])
                    mb = maskb[:, kind * 128:(kind + 1) * 128].unsqueeze(1).to_broadcast([128, 4, 128])
                    A('pe', lambda e, b=b, mb=mb: e.matmul(psf[b][:], ident[:], mb, start=False, stop=True),
                      ['maskb', 'ident'], [('ps', b)])
                    ee = ei[0] % 4
                    ei[0] += 1
                    A('act', lambda e, b=b, ee=ee: e.activation(out=Et[ee][:], in_=psf[b][:], func=AF.Exp, scale=0.125),
                      [('ps', b)], [('E', ee)])
                    outs.append((kind, ee))
                return outs

            def att_back(g, qb, outs):
                state['stage'] = 'att'
                bpv, bden = nb(), nb()
                n = len(outs)
                for i, (kind, ee) in enumerate(outs):
                    if kind == 0:
                        lv = avC[:, qb, :]
                        rv = ['avC']
                    elif qb == 0:
                        lv = avP[l][:]
                        rv = [('avP', l)]
                    else:
                        lv = avC[:, qb - 1, :]
                        rv = ['avC']
                    A('pe', lambda e, lv=lv, ee=ee, i=i, n=n, bpv=bpv: e.matmul(psf[bpv][:], lv, Et[ee][:], start=(i == 0), stop=(i == n - 1)),
                      rv + [('E', ee)], [('ps', bpv)])
                for i, (kind, ee) in enumerate(outs):
                    A('pe', lambda e, ee=ee, i=i, n=n, bden=bden: e.matmul(psf[bden][:], ones[:], Et[ee][:], start=(i == 0), stop=False),
                      ['ones', ('E', ee)], [('ps', bden)])
                A('pe', lambda e, bden=bden: e.matmul(psf[bden][:], ones[:], skb[:, g, :], start=False, stop=True),
                  ['ones', ('skb', g)], [('ps', bden)])
                r0, r1 = g * 64, (g + 1) * 64
                tf = tmpf[(qb * 2 + g) % 2]
                tfk = ('tmpf', (qb * 2 + g) % 2)
                A('act', lambda e, tf=tf, bden=bden: e.activation(out=tf[r0:r1, :], in_=psf[bden][r0:r1, :], func=AF.Ln), [('ps', bden)], [tfk])
                A('act', lambda e, tf=tf: e.activation(out=tf[r0:r1, :], in_=tf[r0:r1, :], func=AF.Exp, scale=-1.0), [tfk], [tfk])
                A('dve', lambda e, tf=tf, bpv=bpv: e.tensor_tensor(out=big[r0:r1, S_CAT:S_CAT + 4, qb * 128:(qb + 1) * 128],
                                                                    in0=psf[bpv][r0:r1, :].rearrange("p (a b) -> p a b", a=4),
                                                                    in1=tf[r0:r1, :].rearrange("p (a b) -> p a b", a=4), op=ALU.mult),
                  [('ps', bpv), tfk], sum([bw(S_CAT + j, g) for j in range(4)], []))

            def tm_block(u, off, blk):
                state['stage'] = f'rp{u}'
                key = (t, l, 'w_r', u)
                b = nb()
                for k in range(KC):
                    a = k * 512
                    A('pe', lambda e, b=b, a=a, k=k, blk=blk, off=off: e.matmul(psf[b][:], nT[:, k, blk * 128:(blk + 1) * 128],
                                                                                  ring_t[:, off + a:off + a + 512], start=(k == 0), stop=(k == KC - 1)),
                      ring.pages(key, a, a + 512) + [('nT', k)], [('ps', b)])
                return b

            def rot_front(dst_slot, blk, b, ri):
                state['stage'] = 'rot'
                G = G0 + blk
                X = psf[b][:].rearrange("p (h two e) -> p h two e", h=4, two=2)
                x1, x2 = X[:, :, 0, :], X[:, :, 1, :]
                cb = cos_t[:, G * 64:(G + 1) * 64].unsqueeze(1).to_broadcast([128, 4, 64])
                sn = sin_t[:, G * 64:(G + 1) * 64].unsqueeze(1).to_broadcast([128, 4, 64])
                r3 = [rt[ri * 4 + i][:].rearrange("p (h e) -> p h e", h=4) for i in range(4)]
                rk_ = [('rt', ri * 4 + i) for i in range(4)]
                O = big[:, dst_slot + blk, :].rearrange("p (h two e) -> p h two e", h=4, two=2)
                cst = [('c', 'c_cos'), ('c', 'c_sin')]
                A('dve', lambda e: e.tensor_tensor(out=r3[0], in0=x1, in1=cb, op=ALU.mult), [('ps', b)] + cst, [rk_[0]])
                A('dve', lambda e: e.tensor_tensor(out=r3[1], in0=x2, in1=sn, op=ALU.mult), [('ps', b)] + cst, [rk_[1]])
                A('dve', lambda e: e.tensor_tensor(out=r3[2], in0=x1, in1=sn, op=ALU.mult), [('ps', b)] + cst, [rk_[2]])
                A('dve', lambda e: e.tensor_tensor(out=r3[3], in0=x2, in1=cb, op=ALU.mult), [('ps', b)] + cst, [rk_[3]])
                A('dve', lambda e: e.tensor_tensor(out=O[:, :, 0, :], in0=r3[0], in1=r3[1], op=ALU.subtract),
                  [rk_[0], rk_[1]], bw(dst_slot + blk, 'lo'))
                A('dve', lambda e: e.tensor_tensor(out=O[:, :, 1, :], in0=r3[2], in1=r3[3], op=ALU.add),
                  [rk_[2], rk_[3]], bw(dst_slot + blk, 'hi'))

            def rot_back(dst_slot, t_slot, x_slot, blk):
                state['stage'] = 'tr'
                hb = state['cnt'] % 2
                state['cnt'] += 1
                for h in range(4):
                    A('pe', lambda e, h=h, hb=hb: e.transpose(psbs[hb][:, h * 128:(h + 1) * 128],
                                                              big[:, dst_slot + blk, h * 128:(h + 1) * 128], ident[:]),
                      [('big', dst_slot + blk, 'lo'), ('big', dst_slot + blk, 'hi'), 'ident'], [('psb', hb)])
                pv = psbs[hb][:, 0:512].rearrange("p (h i) -> p h i", h=4)
                A('act', lambda e, pv=pv: e.activation(out=big[:, t_slot:t_slot + 4, blk * 128:(blk + 1) * 128], in_=pv, func=AF.Copy),
                  [('psb', hb)], sum([bw(t_slot + h, blk) for h in range(4)], []))
                if x_slot is not None:
                    A('dve', lambda e, pv=pv: e.tensor_tensor(out=big[:, x_slot:x_slot + 4, blk * 128:(blk + 1) * 128], in0=pv,
                                                               in1=xi_t[:].rearrange("p (h i) -> p h i", h=4), op=ALU.mult),
                      [('psb', hb), ('c', 'c_xi')], sum([bw(x_slot + h, blk) for h in range(4)], []))

            def v_post(blk, b):
                state['stage'] = 'vpost'
                A('act', lambda e: e.activation(out=big[:, S_VT + blk, :], in_=psf[b][:], func=AF.Copy),
                  [('ps', b)], bw(S_VT + blk))
                A('dve', lambda e: e.tensor_tensor(out=big[:, S_VZ + blk, :].rearrange("p (h v) -> p h v", h=4),
                                                    in0=psf[b][:].rearrange("p (h v) -> p h v", h=4),
                                                    in1=zs_t[:].unsqueeze(2).to_broadcast([128, 4, 128]), op=ALU.mult),
                  [('ps', b), ('c', 'c_zs')], bw(S_VZ + blk))

            def g_post(blk, b):
                state['stage'] = 'gpost'
                A('act', lambda e: e.activation(out=gs[:, blk, :], in_=psf[b][:], func=AF.Silu), [('ps', b)], [('gs', blk)])

            state['stage'] = 'chunks'
            def rb_before(blk):
                return (RbC[l], ('RbC', l)) if blk == 0 else (Rbt[blk - 1], ('Rbt', blk - 1))

            def ch_scores(blk):
                state['stage'] = 'chunks'
                cs = slice(blk * 128, (blk + 1) * 128)
                bs = nb()
                for h in range(4):
                    A('pe', lambda e, h=h, bs=bs: e.matmul(psf[bs][:, h * 128:(h + 1) * 128], big[:, S_KT + h, cs], big[:, S_QT + h, cs],
                                                           start=True, stop=True),
                      [('big', S_KT + h, blk), ('big', S_QT + h, blk)], [('ps', bs)])
                sd = Sd[blk]
                A('dve', lambda e, bs=bs, sd=sd: e.tensor_tensor(out=sd[:], in0=psf[bs][:], in1=dmat[:], op=ALU.mult),
                  [('ps', bs), ('c', 'c_dmat')], [('Sd', blk)])

            def ch_state(blk):
                state['stage'] = 'chunks'
                bk = nb()
                for h in range(4):
                    hs = slice(h * 128, (h + 1) * 128)
                    A('pe', lambda e, h=h, hs=hs, bk=bk: e.matmul(psf[bk][:, hs], big[:, S_KTOK + blk, hs], big[:, S_VZ + blk, hs], start=True, stop=True),
                      [('big', S_KTOK + blk, 'lo'), ('big', S_KTOK + blk, 'hi'), ('big', S_VZ + blk)], [('ps', bk)])
                for h in range(4):
                    hs = slice(h * 128, (h + 1) * 128)
                    A('dve', lambda e, h=h, hs=hs, bk=bk: e.scalar_tensor_tensor(out=Rst[l][:, hs], in0=Rst[l][:, hs], scalar=dec[h], in1=psf[bk][:, hs],
                                                                                   op0=ALU.mult, op1=ALU.add),
                      [('R', l), ('ps', bk)], [('R', l)])
                if blk < NB - 1:
                    A('act', lambda e: e.activation(out=Rbt[blk][:], in_=Rst[l][:], func=AF.Copy), [('R', l)], [('Rbt', blk)])

            def ch_p1(blk):
                state['stage'] = 'chunks'
                n = G0 + blk
                cs = slice(blk * 128, (blk + 1) * 128)
                sd = Sd[blk]
                by = nb()
                rbp, rbk = rb_before(blk)
                for h in range(4):
                    hs = slice(h * 128, (h + 1) * 128)
                    A('pe', lambda e, h=h, hs=hs, by=by, sd=sd: e.matmul(psf[by][:, hs], sd[:, hs], big[:, S_VT + blk, hs], start=True, stop=False),
                      [('Sd', blk), ('big', S_VT + blk)], [('ps', by)])
                    A('pe', lambda e, h=h, hs=hs, by=by, rbp=rbp: e.matmul(psf[by][:, hs], big[:, S_QX + h, cs], rbp[:, hs], start=False, stop=True),
                      [('big', S_QX + h, blk), rbk], [('ps', by)])
                st = stat[n % 2]
                sk = ('stat', n % 2)
                y3 = psf[by][:].rearrange("p (h v) -> p h v", h=4)
                yc = ycb[n % 2]
                yk = ('ycb', n % 2)
                yc3 = yc[:].rearrange("p (h v) -> p h v", h=4)
                ysq = tmpf[2]
                A('dve', lambda e, st=st, y3=y3: e.tensor_reduce(out=st[:, 0:4], in_=y3, axis=AX.X, op=ALU.add), [('ps', by)], [sk])
                A('dve', lambda e, st=st: e.tensor_scalar(out=st[:, 0:4], in0=st[:, 0:4], scalar1=1.0 / 128, scalar2=None, op0=ALU.mult), [sk], [sk])
                A('dve', lambda e, st=st, y3=y3, yc3=yc3: e.tensor_tensor(out=yc3, in0=y3, in1=st[:, 0:4].unsqueeze(2).to_broadcast([128, 4, 128]), op=ALU.subtract),
                  [('ps', by), sk], [yk])
                A('act', lambda e, yc=yc: e.activation(out=ysq[:], in_=yc[:], func=AF.Square), [yk], [('tmpf', 2)])
                A('dve', lambda e, st=st: e.tensor_reduce(out=st[:, 4:8], in_=ysq[:].rearrange("p (h v) -> p h v", h=4), axis=AX.X, op=ALU.add),
                  [('tmpf', 2)], [sk])

            def ch_p2(blk):
                state['stage'] = 'chunks'
                n = G0 + blk
                st = stat[n % 2]
                sk = ('stat', n % 2)
                A('act', lambda e, st=st: e.activation(out=st[:, 4:8], in_=st[:, 4:8], func=AF.Ln, bias=GN_EPS, scale=1.0 / 128), [sk], [sk])
                A('act', lambda e, st=st: e.activation(out=st[:, 4:8], in_=st[:, 4:8], func=AF.Exp, scale=-0.5), [sk], [sk])

            def ch_p3(blk):
                state['stage'] = 'chunks'
                n = G0 + blk
                st = stat[n % 2]
                sk = ('stat', n % 2)
                yc = ycb[n % 2]
                yk = ('ycb', n % 2)
                yc3 = yc[:].rearrange("p (h v) -> p h v", h=4)
                A('dve', lambda e, st=st, yc3=yc3: e.tensor_tensor(out=yc3, in0=yc3, in1=st[:, 4:8].unsqueeze(2).to_broadcast([128, 4, 128]), op=ALU.mult),
                  [yk, sk], [yk])
                ro = rtok[n % 2]
                A('dve', lambda e, ro=ro, yc=yc: e.tensor_tensor(out=ro[:], in0=yc[:], in1=gs[:, blk, :], op=ALU.mult),
                  [yk, ('gs', blk)], [('rtok', n % 2)])

            def ch_tr(blk):
                state['stage'] = 'chunks'
                n = G0 + blk
                cs = slice(blk * 128, (blk + 1) * 128)
                ro = rtok[n % 2]
                hb = state['cnt'] % 2
                state['cnt'] += 1
                for h in range(4):
                    A('pe', lambda e, h=h, hb=hb, ro=ro: e.transpose(psbs[hb][:, h * 128:(h + 1) * 128], ro[:, h * 128:(h + 1) * 128], ident[:]),
                      [('rtok', n % 2), 'ident'], [('psb', hb)])
                for h in range(4):
                    A('act', lambda e, h=h, hb=hb: e.activation(out=big[:, S_CAT + 4 + h, cs], in_=psbs[hb][:, h * 128:(h + 1) * 128], func=AF.Copy,
                                                                scale=gnw[:, l * 4 + h:l * 4 + h + 1]),
                      [('psb', hb), ('c', 'c_gnw')], bw(S_CAT + 4 + h, blk))

            ei = [0]
            pend = None
            pend_tr = []
            roff = {}
            if noret:
                for u in range(4):
                    ring.get((t, l, 'w_r', u)); ring.release((t, l, 'w_r', u))
                for h in range(4):
                    A('dve', lambda e, h=h: e.memset(big[:, S_CAT + 4 + h, :], 0.0), [], bw(S_CAT + 4 + h))
            if 'noatt' in flags:
                for j in range(4):
                    A('dve', lambda e, j=j: e.memset(big[:, S_CAT + j, :], 0.0), [], bw(S_CAT + j))
            csched = {
                4: [(ch_scores, 0), (ch_scores, 1)],
                5: [(ch_scores, 2), (ch_scores, 3), (ch_state, 0), (ch_p1, 0)],
                6: [(ch_state, 1), (ch_p2, 0), (ch_p1, 1)],
                7: [(ch_state, 2), (ch_p3, 0), (ch_p2, 1), (ch_p1, 2)],
                8: [(ch_state, 3), (ch_tr, 0), (ch_p3, 1), (ch_p2, 2), (ch_p1, 3)],
                9: [(ch_tr, 1), (ch_p3, 2), (ch_p2, 3)],
                10: [(ch_tr, 2), (ch_p3, 3)],
                11: [(ch_tr, 3)],
            }
            for s_i, (g, qb) in enumerate(items):
                if not noret and s_i == 0:
                    roff[0] = ring.get((t, l, 'w_r', 0))
                    roff[1] = ring.get((t, l, 'w_r', 1))
                if not noret and s_i == 4:
                    roff[2] = ring.get((t, l, 'w_r', 2))
                    roff[3] = ring.get((t, l, 'w_r', 3))
                outs = None
                if 'noatt' not in flags:
                    outs = att_front(g, qb, ei)
                new_tr = []
                if not noret:
                    if s_i < 4:
                        blk = s_i
                        bq = tm_block(0, roff[0], blk)
                        rot_front(S_QTOK, blk, bq, 0)
                        bk_ = tm_block(1, roff[1], blk)
                        rot_front(S_KTOK, blk, bk_, 1)
                        new_tr = [(S_QTOK, S_QT, S_QX, blk), (S_KTOK, S_KT, None, blk)]
                    elif s_i in (4, 5):
                        for blk in (2 * (s_i - 4), 2 * (s_i - 4) + 1):
                            bv = tm_block(2, roff[2], blk)
                            v_post(blk, bv)
                    elif s_i == 6:
                        for blk in range(NB):
                            bg_ = tm_block(3, roff[3], blk)
                            g_post(blk, bg_)
                if 'noatt' not in flags:
                    if pend is not None:
                        att_back(*pend)
                    pend = (g, qb, outs)
                for tr in pend_tr:
                    rot_back(*tr)
                pend_tr = new_tr
                if not noret:
                    for fn, blk_ in csched.get(s_i, []):
                        fn(blk_)
                if not noret and s_i == 3:
                    ring.release((t, l, 'w_r', 0))
                    ring.release((t, l, 'w_r', 1))
                if not noret and s_i == 6:
                    ring.release((t, l, 'w_r', 2))
                    ring.release((t, l, 'w_r', 3))
            if pend is not None:
                att_back(*pend)
            for tr in pend_tr:
                rot_back(*tr)
            state['stage'] = 'att'
            if t + 1 < nt:
                A('act', lambda e: e.activation(out=akP[l][:], in_=akC[:, :, (NB - 1) * 128:NB * 128], func=AF.Copy),
                  [('akC', 0), ('akC', 1)], [('akP', l)])
                A('act', lambda e: e.activation(out=avP[l][:], in_=avC[:, NB - 1, :], func=AF.Copy),
                  ['avC'], [('avP', l)])

            offs = {}
            for u in range(2):
                offs[u] = ring.get((t, l, 'w_o', u))
            cat_reads = {}
            for cc in range(8):
                if cc < 4:
                    cat_reads[cc] = [('big', S_CAT + cc, 0), ('big', S_CAT + cc, 1)]
                else:
                    cat_reads[cc] = [('big', S_CAT + cc, blk) for blk in range(NB)]

            def wo_half(dc, half):
                state['stage'] = 'wo'
                b = nb()
                for cc in range(half * 4, half * 4 + 4):
                    u, ccl = cc // 4, cc % 4
                    key = (t, l, 'w_o', u)
                    a = ccl * 1024 + dc * 128
                    A('pe', lambda e, b=b, a=a, cc=cc, u=u: e.matmul(psf[b][:], ring_t[:, offs[u] + a:offs[u] + a + 128], big[:, S_CAT + cc, :],
                                                                       start=(cc % 4 == 0), stop=(cc % 4 == 3)),
                      ring.pages(key, a, a + 128) + cat_reads[cc], [('ps', b)])
                return b

            def wo_add(dc, b):
                state['stage'] = 'wo'
                A('dve', lambda e, b=b, dc=dc: e.tensor_tensor(out=hT[:, dc, :], in0=psf[b][:], in1=hT[:, dc, :], op=ALU.add),
                  [('ps', b), ('hT', dc)], [('hT', dc)])

            for s_i in range(8, 12):
                dcs = [2 * (s_i - 8), 2 * (s_i - 8) + 1]
                bs_ = [wo_half(dc, 0) for dc in dcs]
                if not noret:
                    for fn, blk_ in csched.get(s_i, []):
                        fn(blk_)
                for dc, b_ in zip(dcs, bs_):
                    wo_add(dc, b_)
            if not noret and t + 1 < nt:
                state['stage'] = 'chunks'
                A('act', lambda e: e.activation(out=RbC[l][:], in_=Rst[l][:], func=AF.Copy), [('R', l)], [('RbC', l)])
            np_start()
            for dc in range(KC):
                b_ = wo_half(dc, 1)
                np_mm()
                wo_add(dc, b_)
                np_feed(dc)
            ring.release((t, l, 'w_o', 0))
            ring.release((t, l, 'w_o', 1))
            state['stage'] = 'post'

        def load_x(t):
            ts = slice(t * TS, (t + 1) * TS)
            rec.add('sp', lambda e, ts=ts: e.dma_start(out=xbuf, in_=xT[:, :, ts].rearrange("k p s -> p k s")),
                    writes=sum([xb_w(k) for k in range(KC)], []) + ['xsem'], dma_sem='x')

        load_x(0)
        for t in range(nt):
            ts = slice(t * TS, (t + 1) * TS)
            np_start()
            for k_ in range(KC):
                np_feed(k_, from_x=True)
            if 'noffn' in flags:
                for k_ in range(KC):
                    A('dve', lambda e, k_=k_: e.tensor_copy(out=hT[:, k_, :], in_=xbuf[:, k_, :]), xb_r(k_), [('hT', k_)])
            for l in range(nl):
                if 'noffn' not in flags:
                    ffn(t, l, 0)
                else:
                    for u in range(11):
                        ring.get((t, l, 'w_gu1', u)); ring.release((t, l, 'w_gu1', u))
                    for u in range(8):
                        ring.get((t, l, 'w_d1', u)); ring.release((t, l, 'w_d1', u))
                if 'nomix' not in flags:
                    mixer(t, l)
                    if dbg is not None and t == 0 and l == 0:
                        allr = []
                        for s_ in range(40):
                            allr += br(s_)
                        rec.add('sp', lambda e: e.dma_start(out=dbg.rearrange("p (a b) -> p a b", a=40), in_=big[:, 0:40, :]), reads=allr, writes=['dbg'], dma_sem='dbg')
                else:
                    for name, nu in (('w_aq', 1), ('w_ak', 1), ('w_av', 1), ('w_r', 4), ('w_o', 2)):
                        for u in range(nu):
                            ring.get((t, l, name, u)); ring.release((t, l, name, u))
                if l == nl - 1 and t + 1 < nt:
                    load_x(t + 1)
                if 'noffn' not in flags:
                    ffn(t, l, 1)
                else:
                    for u in range(11):
                        ring.get((t, l, 'w_gu2', u)); ring.release((t, l, 'w_gu2', u))
                    for u in range(8):
                        ring.get((t, l, 'w_d2', u)); ring.release((t, l, 'w_d2', u))
            np_finish(None, 0, dst_is_h=True)
            rec.add('sp', lambda e, ts=ts: e.dma_start(out=yT[:, :, ts].rearrange("k p s -> p k s"), in_=oT),
                    reads=sum([ot_r(k) for k in range(KC)], []), writes=[('yT', t), 'ysem'], dma_sem='y')
        rec.add('sp', None, reads=[('yT', t) for t in range(nt)] + (['dbg'] if dbg is not None else []))
        if use_scratch:
            rec.add('sp', None, reads=[('ssem', j) for j in range(4)])

        semkeys = rec.finalize()
        sems = {k: es.enter_context(nc.semaphore(f"s{n}")) for n, k in enumerate(semkeys)}
        with nc.Block() as block:
            @block.sync
            def _(e):
                rec.emit_engine('sp', e, sems)

            @block.gpsimd
            def _(e):
                rec.emit_engine('pool', e, sems)

            @block.tensor
            def _(e):
                rec.emit_engine('pe', e, sems)

            @block.vector
            def _(e):
                rec.emit_engine('dve', e, sems)

            @block.scalar
            def _(e):
                rec.emit_engine('act', e, sems)
    return nc


def make_in_maps(inputs, nl, nt, ncore):
    S = nt * TS
    w = prep_weights(inputs, nl)
    tabs = const_tables(nt)
    shared = dict(w)
    for k, v in tabs.items():
        if k != 'dec':
            shared[k] = v
    x = np.asarray(inputs['x'], dtype=np.float32)
    maps = []
    for c in range(ncore):
        m = dict(shared)
        m['xT'] = _c(x[c, :S, :].T).reshape(KC, 128, S)
        maps.append(m)
    return maps


def run(inputs, nl=NLAYER, nt=SEQ // TS, ncore=NCORE, flags=(), trace=False):
    inputs = {k: np.asarray(v) for k, v in inputs.items()}
    nc = build(nl, nt, flags=flags)
    maps = make_in_maps(inputs, nl, nt, ncore)
    res = run_bass_kernel_spmd(nc, maps, core_ids=list(range(ncore)), trace=trace)
    S = nt * TS
    out = np.stack([r['yT'].reshape(D, S).T for r in res.results], axis=0)
    return np.ascontiguousarray(out, dtype=np.float32), res


def kernel(**inputs):
    out, _ = run(inputs)
    return out
```
